# Optimizing a Trainium2 kernel written in Bass

```python
import math
import jax
import jax.numpy as jnp
from jax import lax
import numpy as np

D_MODEL = 1024
BATCH = 2
SEQ = 8192
DEPTH = 4

CHUNK = 64
N_EVEN = (DEPTH + 1) // 2
N_ODD = DEPTH // 2
MIX_W = D_MODEL
A_HEADS = 8
A_HD = 64
A_W = A_HEADS * A_HD
A_DECAY_R = 32
A_AAA_R = 32
A_MV_R = 32
A_GATE_R = 96
A_COLS = 3 * A_W + A_DECAY_R + A_AAA_R + A_GATE_R
A_SPLITS = (A_W, 2 * A_W, 3 * A_W, 3 * A_W + A_DECAY_R, 3 * A_W + A_DECAY_R + A_AAA_R)
RWKV_LN_EPS = 64e-5
B_W = MIX_W - A_W
B_GROUPS = 8
B_GD = B_W // B_GROUPS
SGU_BLOCK = 128
SGU_LN_EPS = 1e-5
EVEN_COLS = A_COLS + 2 * B_W
C_HEADS = 8
C_HD = 64
C_W = C_HEADS * C_HD
D_HEADS = 4
D_HD = 128
D_W = D_HEADS * D_HD
D_CONV = 4
ODD_COLS = 4 * C_W + 4 * D_W + 2 * D_HEADS
D_FF = 2816
FFN_CONV = 3
NORM_EPS = 1e-6

kernel_name = 'hybrid_rwkv7_gmlp_hgrn2_gdn_trunk'


def rms_norm(x, g, eps=NORM_EPS):
    xf = x.astype(jnp.float32)
    y = xf * lax.rsqrt(jnp.mean(xf * xf, axis=-1, keepdims=True) + eps)
    return (y * g.astype(jnp.float32)).astype(x.dtype)


def group_layer_norm(x, g, b, n_groups, eps):
    shp = x.shape
    xf = x.astype(jnp.float32).reshape(shp[:-1] + (n_groups, shp[-1] // n_groups))
    mu = jnp.mean(xf, axis=-1, keepdims=True)
    var = jnp.mean(jnp.square(xf - mu), axis=-1, keepdims=True)
    y = ((xf - mu) * lax.rsqrt(var + eps)).reshape(shp)
    return (y * g + b).astype(x.dtype)


def l2_normalize(x, eps=1e-6):
    xf = x.astype(jnp.float32)
    return xf * lax.rsqrt(jnp.sum(xf * xf, axis=-1, keepdims=True) + eps)


def token_shift(x):
    return jnp.pad(x, ((0, 0), (1, 0), (0, 0)))[:, :-1]


def causal_dwconv(x, w):
    k_w = w.shape[0]
    return lax.conv_general_dilated(
        x, w[:, None, :].astype(x.dtype), window_strides=(1,), padding=[(k_w - 1, 0)],
        dimension_numbers=('NWC', 'WIO', 'NWC'), feature_group_count=x.shape[-1])


def to_heads(x, n_heads):
    return x.reshape(x.shape[:-1] + (n_heads, x.shape[-1] // n_heads))


def split_heads(x, n_heads):
    return to_heads(x, n_heads).transpose(0, 2, 1, 3)


def merge_heads(x):
    b, h, t, d = x.shape
    return x.transpose(0, 2, 1, 3).reshape(b, t, h * d)


def rwkv7_mix(za, v_first, vres, mu, w0, w_up, a0, a_up, g_up, k_k, k_a, r_k, ln_g, ln_b):
    f32 = jnp.float32
    bsz = za.shape[0]
    za = za + mu * (token_shift(za) - za)
    r, k, v, xw, xa, xg = jnp.split(za, A_SPLITS, axis=-1)
    if vres is not None:
        v_lr, v_up, v0 = vres
        v = v + (v_first - v) * jax.nn.sigmoid(v0 + v_lr @ v_up)
    w_log = -jax.nn.softplus(-(w0 + jnp.tanh(xw) @ w_up).astype(f32)) - 0.5
    decay = jnp.exp(-jnp.exp(w_log))
    a = jax.nn.sigmoid((a0 + xa @ a_up).astype(f32))
    g = jax.nn.sigmoid(xg) @ g_up
    kk = l2_normalize(to_heads(k * k_k, A_HEADS))
    k_mod = k.astype(f32) * (1.0 + (a - 1.0) * k_a)
    rh, kh, vh, wh, ah = (to_heads(t.astype(f32), A_HEADS) for t in (r, k_mod, v, decay, a))

    def step(s, inp):
        r_t, w_t, k_t, v_t, kk_t, a_t = inp
        sa = jnp.einsum('bhvk,bhk->bhv', s, kk_t)
        s = (s * w_t[:, :, None, :] - sa[..., None] * (kk_t * a_t)[:, :, None, :]
             + v_t[..., None] * k_t[:, :, None, :])
        return s, jnp.einsum('bhvk,bhk->bhv', s, r_t)

    s0 = jnp.zeros((bsz, A_HEADS, A_HD, A_HD), f32)
    xs = tuple(jnp.moveaxis(t, 1, 0) for t in (rh, wh, kh, vh, kk, ah))
    _, y = lax.scan(step, s0, xs)
    y = jnp.moveaxis(y, 0, 1)
    mu_y = jnp.mean(y, axis=-1, keepdims=True)
    var_y = jnp.mean(jnp.square(y - mu_y), axis=-1, keepdims=True)
    y = ((y - mu_y) * lax.rsqrt(var_y + RWKV_LN_EPS) * ln_g.reshape(A_HEADS, A_HD)
         + ln_b.reshape(A_HEADS, A_HD))
    y = y + jnp.sum(rh * kh * r_k, axis=-1, keepdims=True) * vh
    y = y.reshape(za.shape[:2] + (A_W,)).astype(za.dtype) * g
    return y, v


def sgu_mix(zb, ln_g, ln_b, w_s, b_s):
    bsz, t_len, _ = zb.shape
    u, v = jnp.split(jax.nn.gelu(zb, approximate=False), 2, axis=-1)
    v = group_layer_norm(v, ln_g, ln_b, B_GROUPS, SGU_LN_EPS)
    v = v.reshape(bsz, t_len // SGU_BLOCK, SGU_BLOCK, B_GROUPS, B_GD)
    pos_chunk = jnp.arange(SGU_BLOCK) // CHUNK
    mask = pos_chunk[:, None] >= pos_chunk[None, :]
    w = jnp.where(mask, w_s, 0.0).astype(v.dtype)
    mixed = jnp.einsum('gij,bnjgc->bnigc', w, v) + b_s.T[None, None, :, :, None]
    return u * mixed.reshape(bsz, t_len, B_W)


def gla_chunked(q, k, v, log_f):
    bsz, h, t_len, dk = q.shape
    dv = v.shape[-1]
    n = t_len // CHUNK
    q, k, v, log_f = (t.astype(jnp.float32).reshape(bsz, h, n, CHUNK, t.shape[-1])
                      for t in (q, k, v, log_f))
    b = jnp.cumsum(log_f, axis=3)
    b_ref = b[:, :, :, CHUNK // 2:CHUNK // 2 + 1]
    b_last = b[:, :, :, -1:]
    causal = jnp.tril(jnp.ones((CHUNK, CHUNK), dtype=bool))
    scores = jnp.einsum('bhntc,bhnsc->bhnts', q * jnp.exp(b - b_ref), k * jnp.exp(b_ref - b))
    o_intra = jnp.einsum('bhnts,bhnsv->bhntv', jnp.where(causal, scores, 0.0), v)
    q_in = q * jnp.exp(b)
    k_st = k * jnp.exp(b_last - b)
    decay_last = jnp.exp(b_last[:, :, :, 0])

    def step(s, inp):
        q_c, k_c, v_c, d_c = inp
        o_c = jnp.einsum('bhtk,bhkv->bhtv', q_c, s)
        s = s * d_c[..., None] + jnp.einsum('bhtk,bhtv->bhkv', k_c, v_c)
        return s, o_c

    s0 = jnp.zeros((bsz, h, dk, dv), jnp.float32)
    xs = tuple(jnp.moveaxis(t, 2, 0) for t in (q_in, k_st, v, decay_last))
    _, o_inter = lax.scan(step, s0, xs)
    o = jnp.moveaxis(o_inter, 0, 2) + o_intra
    return o.reshape(bsz, h, t_len, dv)


def gated_delta_chunked(q, k, v, g, beta):
    bsz, h, t_len, dk = q.shape
    dv = v.shape[-1]
    n = t_len // CHUNK
    q, k, v = (t.astype(jnp.float32).reshape(bsz, h, n, CHUNK, t.shape[-1]) for t in (q, k, v))
    g, beta = (t.astype(jnp.float32).reshape(bsz, h, n, CHUNK) for t in (g, beta))
    gc = jnp.cumsum(g, axis=-1)
    causal = jnp.tril(jnp.ones((CHUNK, CHUNK), dtype=bool))
    strict = jnp.tril(jnp.ones((CHUNK, CHUNK), dtype=bool), -1)
    diff = gc[..., :, None] - gc[..., None, :]
    decay = jnp.where(causal, jnp.exp(jnp.where(causal, diff, 0.0)), 0.0)
    kb = k * beta[..., None]
    lower = jnp.where(strict, jnp.einsum('bhntc,bhnsc->bhnts', kb, k) * decay, 0.0)
    rhs = jnp.concatenate([v * beta[..., None], kb * jnp.exp(gc)[..., None]], axis=-1)
    sol = lax.linalg.triangular_solve(lower + jnp.eye(CHUNK, dtype=jnp.float32), rhs,
                                      left_side=True, lower=True)
    u, w = sol[..., :dv], sol[..., dv:]
    qk = jnp.where(causal, jnp.einsum('bhntc,bhnsc->bhnts', q, k) * decay, 0.0)
    q_in = q * jnp.exp(gc)[..., None]
    k_st = k * jnp.exp(gc[..., -1:] - gc)[..., None]
    decay_last = jnp.exp(gc[..., -1])

    def step(s, inp):
        u_c, w_c, q_c, k_c, qk_c, d_c = inp
        v_new = u_c - jnp.einsum('bhtk,bhkv->bhtv', w_c, s)
        o_c = jnp.einsum('bhtk,bhkv->bhtv', q_c, s) + jnp.einsum('bhts,bhsv->bhtv', qk_c, v_new)
        s = s * d_c[..., None, None] + jnp.einsum('bhtk,bhtv->bhkv', k_c, v_new)
        return s, o_c

    s0 = jnp.zeros((bsz, h, dk, dv), jnp.float32)
    xs = tuple(jnp.moveaxis(t, 2, 0) for t in (u, w, q_in, k_st, qk, decay_last))
    _, o = lax.scan(step, s0, xs)
    return jnp.moveaxis(o, 0, 2).reshape(bsz, h, t_len, dv)


def hgrn2_mix(zc, lb, norm_g):
    q, f_logit, i, gate = jnp.split(zc, 4, axis=-1)
    f = lb + (1.0 - lb) * jax.nn.sigmoid(f_logit.astype(jnp.float32))
    o = gla_chunked(split_heads(jax.nn.silu(q), C_HEADS), split_heads(1.0 - f, C_HEADS),
                    split_heads(i, C_HEADS), split_heads(jnp.log(f), C_HEADS))
    o = rms_norm(o, norm_g.reshape(C_HEADS, 1, C_HD))
    return merge_heads(o).astype(zc.dtype) * jax.nn.silu(gate)


def gdn_mix(zd, conv_w, a_log, dt_bias, norm_g):
    qkv, z, b, a = jnp.split(zd, [3 * D_W, 4 * D_W, 4 * D_W + D_HEADS], axis=-1)
    qkv = jax.nn.silu(causal_dwconv(qkv, conv_w))
    q, k, v = jnp.split(qkv, 3, axis=-1)
    q = l2_normalize(split_heads(q, D_HEADS)) * (D_HD ** -0.5)
    k = l2_normalize(split_heads(k, D_HEADS))
    v = split_heads(v, D_HEADS)
    beta = jax.nn.sigmoid(b.astype(jnp.float32)).transpose(0, 2, 1)
    g = (-jnp.exp(a_log.astype(jnp.float32))
         * jax.nn.softplus(a.astype(jnp.float32) + dt_bias)).transpose(0, 2, 1)
    o = gated_delta_chunked(q, k, v, g, beta)
    o = rms_norm(o, norm_g)
    return merge_heads(o).astype(zd.dtype) * jax.nn.silu(z)


def conv_glu_ffn(h, w_in, conv_w, conv_b, w_out):
    gate, up = jnp.split(h @ w_in, 2, axis=-1)
    gate = causal_dwconv(gate, conv_w) + conv_b
    return (jax.nn.gelu(gate, approximate=True) * up) @ w_out


def setup_inputs(seed: int = 0) -> dict:
    key = jax.random.key(seed)
    keys = jax.random.split(key, 48)
    ctr = iter(range(48))

    def nk():
        return keys[next(ctr)]

    def nrm(shape, scale):
        return jax.random.normal(nk(), shape, jnp.float32) * scale

    def gain(shape):
        return 1.0 + nrm(shape, 0.02)

    d = D_MODEL
    dt = jnp.exp(jax.random.uniform(nk(), (N_ODD, D_HEADS), jnp.float32,
                                    minval=math.log(1e-3), maxval=math.log(1e-1)))
    return {
        'x': nrm((BATCH, SEQ, d), 1.0),
        'norm_mix_pre': gain((DEPTH, d)),
        'norm_mix_post': gain((DEPTH, d)),
        'norm_ffn_pre': gain((DEPTH, d)),
        'norm_ffn_post': gain((DEPTH, d)),
        'ev_w_in': nrm((N_EVEN, d, EVEN_COLS), d ** -0.5),
        'ev_w_out': nrm((N_EVEN, MIX_W, d), MIX_W ** -0.5),
        'rwkv_mu': jax.random.uniform(nk(), (N_EVEN, A_COLS), jnp.float32, minval=0.2, maxval=0.8),
        'rwkv_w0': jnp.linspace(-6.5, -1.5, A_W, dtype=jnp.float32)[None, :] + nrm((N_EVEN, A_W), 0.1),
        'rwkv_w_up': nrm((N_EVEN, A_DECAY_R, A_W), 0.5 * A_DECAY_R ** -0.5),
        'rwkv_a0': nrm((N_EVEN, A_W), 0.1),
        'rwkv_a_up': nrm((N_EVEN, A_AAA_R, A_W), 0.5 * A_AAA_R ** -0.5),
        'rwkv_g_up': nrm((N_EVEN, A_GATE_R, A_W), A_GATE_R ** -0.5),
        'rwkv_k_k': 0.85 + nrm((N_EVEN, A_W), 0.02),
        'rwkv_k_a': gain((N_EVEN, A_W)),
        'rwkv_r_k': nrm((N_EVEN, A_HEADS, A_HD), 0.1),
        'rwkv_ln_g': gain((N_EVEN, A_W)),
        'rwkv_ln_b': nrm((N_EVEN, A_W), 0.02),
        'rwkv_vres_down': nrm((N_EVEN - 1, d, A_MV_R), d ** -0.5),
        'rwkv_vres_up': nrm((N_EVEN - 1, A_MV_R, A_W), 0.5 * A_MV_R ** -0.5),
        'rwkv_v0': gain((N_EVEN - 1, A_W)),
        'sgu_ln_g': gain((N_EVEN, B_W)),
        'sgu_ln_b': nrm((N_EVEN, B_W), 0.02),
        'sgu_w': nrm((N_EVEN, B_GROUPS, SGU_BLOCK, SGU_BLOCK), SGU_BLOCK ** -0.5),
        'sgu_b': gain((N_EVEN, B_GROUPS, SGU_BLOCK)),
        'od_w_in': nrm((N_ODD, d, ODD_COLS), d ** -0.5),
        'od_w_out': nrm((N_ODD, MIX_W, d), MIX_W ** -0.5),
        'hgrn_lb_logits': nrm((N_ODD, C_W), 0.5),
        'hgrn_norm_g': gain((N_ODD, C_W)),
        'gdn_conv_w': nrm((N_ODD, D_CONV, 3 * D_W), D_CONV ** -0.5),
        'gdn_a_log': jnp.log(jax.random.uniform(nk(), (N_ODD, D_HEADS), jnp.float32, minval=1.0, maxval=16.0)),
        'gdn_dt_bias': dt + jnp.log(-jnp.expm1(-dt)),
        'gdn_norm_g': gain((N_ODD, D_HD)),
        'ffn_w_in': nrm((DEPTH, d, 2 * D_FF), d ** -0.5),
        'ffn_conv_w': nrm((DEPTH, FFN_CONV, D_FF), FFN_CONV ** -0.5),
        'ffn_conv_b': nrm((DEPTH, D_FF), 0.02),
        'ffn_w_out': nrm((DEPTH, D_FF, d), D_FF ** -0.5),
    }


def reference(x, norm_mix_pre, norm_mix_post, norm_ffn_pre, norm_ffn_post,
              ev_w_in, ev_w_out, rwkv_mu, rwkv_w0, rwkv_w_up, rwkv_a0, rwkv_a_up, rwkv_g_up,
              rwkv_k_k, rwkv_k_a, rwkv_r_k, rwkv_ln_g, rwkv_ln_b,
              rwkv_vres_down, rwkv_vres_up, rwkv_v0,
              sgu_ln_g, sgu_ln_b, sgu_w, sgu_b,
              od_w_in, od_w_out, hgrn_lb_logits, hgrn_norm_g,
              gdn_conv_w, gdn_a_log, gdn_dt_bias, gdn_norm_g,
              ffn_w_in, ffn_conv_w, ffn_conv_b, ffn_w_out):
    p = jax.nn.softmax(hgrn_lb_logits.astype(jnp.float32), axis=0)
    lower_bounds = jnp.cumsum(p, axis=0) - p[0]
    v_first = None
    for l in range(DEPTH):
        h = rms_norm(x, norm_mix_pre[l])
        if l % 2 == 0:
            e = l // 2
            z = h @ ev_w_in[e]
            za, zb = z[..., :A_COLS], z[..., A_COLS:]
            vres = None if e == 0 else (h @ rwkv_vres_down[e - 1], rwkv_vres_up[e - 1], rwkv_v0[e - 1])
            ya, v_a = rwkv7_mix(za, v_first, vres, rwkv_mu[e], rwkv_w0[e], rwkv_w_up[e], rwkv_a0[e],
                                rwkv_a_up[e], rwkv_g_up[e], rwkv_k_k[e], rwkv_k_a[e], rwkv_r_k[e],
                                rwkv_ln_g[e], rwkv_ln_b[e])
            if e == 0:
                v_first = v_a
            yb = sgu_mix(zb, sgu_ln_g[e], sgu_ln_b[e], sgu_w[e], sgu_b[e])
            mix = jnp.concatenate([ya, yb], axis=-1) @ ev_w_out[e]
        else:
            o = l // 2
            z = h @ od_w_in[o]
            zc, zd = z[..., :4 * C_W], z[..., 4 * C_W:]
            yc = hgrn2_mix(zc, lower_bounds[o], hgrn_norm_g[o])
            yd = gdn_mix(zd, gdn_conv_w[o], gdn_a_log[o], gdn_dt_bias[o], gdn_norm_g[o])
            mix = jnp.concatenate([yc, yd], axis=-1) @ od_w_out[o]
        x = x + rms_norm(mix, norm_mix_post[l])
        h = rms_norm(x, norm_ffn_pre[l])
        x = x + rms_norm(conv_glu_ffn(h, ffn_w_in[l], ffn_conv_w[l], ffn_conv_b[l], ffn_w_out[l]),
                         norm_ffn_post[l])
    return x
```

```python
import numpy as np
from contextlib import ExitStack
import concourse.bass as bass
import concourse.mybir as mybir
from concourse.bass_utils import run_bass_kernel_spmd

F32 = mybir.dt.float32
BF16 = mybir.dt.bfloat16
AF = mybir.ActivationFunctionType
ALU = mybir.AluOpType
AX = mybir.AxisListType

COMPUTE = ("tensor", "vector", "scalar", "gpsimd")
NPOOL = 24


class Prog:
    def __init__(self):
        self.nc = bass.Bass("TRN2", target_bir_lowering=False)
        self.es = ExitStack()
        self.ops = {e: [] for e in COMPUTE + ("sync",)}
        self.cnt = {e: 0 for e in COMPUTE}
        self.sem = {}
        for e in COMPUTE:
            self.sem[e] = self.nc.alloc_semaphore("s_" + e)
        self.dpool = {q: [self.nc.alloc_semaphore(f"d_{q}_{i}") for i in range(NPOOL)] for q in ("sync", "gpsimd")}
        self.dcnt = {"sync": 0, "gpsimd": 0}
        self.known = {e: {} for e in COMPUTE + ("sync",)}
        self.lastw = {}
        self.readers = {}
        self.out_tokens = []
        self.nuniq = 0

    def sb(self, name, shape, dtype=F32):
        return self.es.enter_context(self.nc.sbuf_tensor(name, list(shape), dtype))

    def ps(self, name, shape, dtype=F32):
        return self.es.enter_context(self.nc.psum_tensor(name, list(shape), dtype))

    def dram(self, name, shape, dtype=F32, kind="ExternalInput"):
        return self.nc.dram_tensor(name, list(shape), dtype, kind=kind).ap()

    @staticmethod
    def _key(r):
        if isinstance(r, tuple):
            return r[0], r[1]
        return r, None

    def _conf(self, table, name, sub):
        d = table.get(name, {})
        if sub is None:
            return list(d.values())
        out = []
        if sub in d:
            out.append(d[sub])
        if None in d:
            out.append(d[None])
        return out

    def _deps(self, reads, writes):
        toks = []
        for r in reads:
            n, s = self._key(r)
            toks += self._conf(self.lastw, n, s)
        for w in writes:
            n, s = self._key(w)
            toks += self._conf(self.lastw, n, s)
            for lst in self._conf(self.readers, n, s):
                toks += lst
        return toks

    def _commit(self, reads, writes, tok):
        for r in reads:
            n, s = self._key(r)
            self.readers.setdefault(n, {}).setdefault(s, []).append(tok)
        for w in writes:
            n, s = self._key(w)
            if s is None:
                self.lastw[n] = {None: tok}
                self.readers[n] = {}
            else:
                self.lastw.setdefault(n, {})[s] = tok
                self.readers.setdefault(n, {})[s] = []

    def _waits(self, eng, toks, skip_self=False):
        need = {}
        for (sname, sem, val, src) in toks:
            if skip_self and src == eng:
                continue
            if val > need.get(sname, (None, 0))[1]:
                need[sname] = (sem, val)
        out = []
        kn = self.known[eng]
        for sname, (sem, val) in need.items():
            if kn.get(sname, 0) >= val:
                continue
            kn[sname] = val
            out.append((sem, val))
        return out

    def op(self, eng, meth, reads=(), writes=(), **kw):
        toks = self._deps(reads, writes)
        waits = self._waits(eng, toks, skip_self=(eng == "tensor"))
        self.cnt[eng] += 1
        tok = ("s_" + eng, self.sem[eng], self.cnt[eng], eng)

        def fn(e, meth=meth, kw=kw):
            return getattr(e, meth)(**kw)

        self.ops[eng].append((waits, fn, (self.sem[eng], 1)))
        self._commit(reads, writes, tok)
        return tok

    def mm(self, out, lhsT, rhs, start=True, stop=True, reads=(), writes=()):
        return self.op("tensor", "matmul", reads, writes, out=out, lhsT=lhsT, rhs=rhs, start=start, stop=stop)

    def tr(self, out, in_, identity, reads=(), writes=()):
        return self.op("tensor", "transpose", reads, writes, out=out, in_=in_, identity=identity)

    def act(self, out, in_, func, reads=(), writes=(), **kw):
        return self.op("scalar", "activation", reads, writes, out=out, in_=in_, func=func, **kw)

    def tt(self, out, in0, in1, op, reads=(), writes=(), eng="vector"):
        return self.op(eng, "tensor_tensor", reads, writes, out=out, in0=in0, in1=in1, op=op)

    def stt(self, out, in0, scalar, in1, op0, op1, reads=(), writes=()):
        return self.op("vector", "scalar_tensor_tensor", reads, writes, out=out, in0=in0, scalar=scalar, in1=in1, op0=op0, op1=op1)

    def ts(self, out, in0, scalar1, scalar2, op0, op1=None, reads=(), writes=(), eng="vector"):
        kw = dict(out=out, in0=in0, scalar1=scalar1, scalar2=scalar2, op0=op0)
        if op1 is not None:
            kw["op1"] = op1
        return self.op(eng, "tensor_scalar", reads, writes, **kw)

    def cp(self, out, in_, reads=(), writes=(), eng="vector"):
        if eng == "scalar":
            return self.op("scalar", "copy", reads, writes, out=out, in_=in_)
        return self.op(eng, "tensor_copy", reads, writes, out=out, in_=in_)

    def memset(self, ap, val, writes=(), eng="vector"):
        return self.op(eng, "memset", (), writes, ap=ap, constant=val)

    def dma(self, out, in_, reads=(), writes=(), q="sync", is_output=False, **kw):
        toks = self._deps(reads, writes)
        j = self.dcnt[q]
        self.dcnt[q] += 1
        slot, rnd = j % NPOOL, j // NPOOL
        sem = self.dpool[q][slot]
        sname = f"d_{q}_{slot}"
        if rnd > 0:
            toks = toks + [(sname, sem, 16 * rnd, "dma")]
        waits = self._waits(q, toks)
        tok = (sname, sem, 16 * (rnd + 1), "dma")

        def fn(e, out=out, in_=in_, kw=kw):
            return e.dma_start(out=out, in_=in_, **kw)

        self.ops[q].append((waits, fn, (sem, 16)))
        self._commit(reads, writes, tok)
        if is_output:
            self.out_tokens.append(tok)
        return tok

    def finish(self):
        nc = self.nc
        final = list(self.out_tokens)
        for e in COMPUTE:
            if self.cnt[e]:
                final.append(("s_" + e, self.sem[e], self.cnt[e], e))
        fw = self._waits("sync", final)
        ops = self.ops
        with nc.Block() as block:
            def emit(e, lst, extra=()):
                for waits, fn, (sem, inc) in lst:
                    for (ws, wv) in waits:
                        e.wait_ge(ws, wv)
                    fn(e).then_inc(sem, inc)
                for (ws, wv) in extra:
                    e.wait_ge(ws, wv)

            @block.sync
            def _(e):
                emit(e, ops["sync"], fw)

            @block.tensor
            def _(e):
                emit(e, ops["tensor"])

            @block.vector
            def _(e):
                emit(e, ops["vector"])

            @block.scalar
            def _(e):
                emit(e, ops["scalar"])

            @block.gpsimd
            def _(e):
                emit(e, ops["gpsimd"])
        self.es.close()
        return nc


D = 1024
DFF = 2816
NFC = DFF // 128
EPS = 1e-6


class Ring:
    def __init__(self, P, name, n, shape, dtype):
        self.tiles = [P.sb(f"{name}{i}", shape, dtype) for i in range(n)]
        self.names = [f"{name}{i}" for i in range(n)]
        self.i = 0

    def next(self):
        k = self.i % len(self.tiles)
        self.i += 1
        return self.tiles[k], self.names[k]


class PsRing:
    def __init__(self, P, name, n, width=512):
        self.tiles = [P.ps(f"{name}{i}", [128, width], F32) for i in range(n)]
        self.names = [f"{name}{i}" for i in range(n)]
        self.i = 0

    def next(self):
        k = self.i % len(self.tiles)
        self.i += 1
        return self.tiles[k], self.names[k]


def colvec(a):
    a = np.asarray(a, np.float32)
    return np.ascontiguousarray(a.reshape(-1, 128).T)


def build_dense(NT, do_post, do_pre, C):
    P = Prog()
    TM = min(1024, NT)
    nmt = NT // TM
    HAL = 2 if do_post else 0
    W = HAL + NT
    TW = TM + 2
    xT = P.dram("xT", [D, W])
    if do_post:
        yT = P.dram("yT", [D, W])
        w_o = P.dram("w_o", [D, D])
        w_f1 = P.dram("w_f1", [D, 2 * DFF])
        w_f2 = P.dram("w_f2", [DFF, D])
        gvec = P.dram("gvec", [128, 24])
        cw = P.dram("cw", [128, NFC, 4])
        hmask = P.dram("hmask", [128, 1])
        xoT = P.dram("xoT", [D, NT], F32, kind="ExternalOutput")
    if do_pre:
        gpre = P.dram("gpre", [128, 8])
        w_in = P.dram("w_in", [D, C])
        zT = P.dram("zT", [C, NT], F32, kind="ExternalOutput")

    x_sb = P.sb("x_sb", [128, 8, TW], F32)
    h_sb = P.sb("h_sb", [128, 8, TW], BF16)
    ring = Ring(P, "wr", 2, [128, NFC, 512], BF16)
    sq = P.sb("sq", [128, 8, 512], BF16)
    rstd = [P.sb(f"rstd{i}", [128, 512], F32) for i in range(2)]
    ones = P.sb("ones", [128, 128], BF16)
    P.memset(ones[:], 1.0 / D, writes=["ones"])
    pr = PsRing(P, "pp", 6)
    pn = P.ps("pn", [128, 512], F32)
    nrs = [0]
    if do_post:
        f8 = P.sb("f8", [128, 8, TW], F32)
        act = P.sb("act", [128, NFC * TM], BF16)
        act3 = act[:, :].rearrange("p (j t) -> p j t", t=TM)
        y_sb = act[:, 0:8 * TW].rearrange("p (k t) -> p k t", t=TW)
        gb = [P.sb(f"gb{i}", [128, 2 + 512], F32) for i in range(2)]
        cv = [P.sb(f"cv{i}", [128, 512], F32) for i in range(2)]
        ghalo = P.sb("ghalo", [128, NFC, 2], F32)
        gv_sb = P.sb("gv_sb", [128, 24], F32)
        cw_sb = P.sb("cw_sb", [128, NFC, 4], F32)
        hm_sb = P.sb("hm_sb", [128, 1], F32)
        P.dma(gv_sb[:], gvec, writes=["gv_sb"])
        P.dma(cw_sb[:], cw, writes=["cw_sb"])
        P.dma(hm_sb[:], hmask, writes=["hm_sb"])
    if do_pre:
        gp_sb = P.sb("gp_sb", [128, 8], F32)
        P.dma(gp_sb[:], gpre, writes=["gp_sb"])
        zst = Ring(P, "zst", 3, [128, 512], F32)

    xv = xT.rearrange("(k p) t -> p k t", p=128)
    if do_post:
        yv = yT.rearrange("(k p) t -> p k t", p=128)
        xov = xoT.rearrange("(k p) t -> p k t", p=128)

    def rms_rstd(src, sname, lo, hi):
        n = hi - lo
        for m in range(8):
            P.act(sq[:, m, 0:n], src[:, m, lo:hi], AF.Square, reads=[(sname, (m, lo))], writes=[("sq", m)])
        for m in range(8):
            P.mm(pn[:, 0:n], ones[:], sq[:, m, 0:n], start=(m == 0), stop=(m == 7), reads=["ones", ("sq", m)], writes=["pn"])
        r = rstd[nrs[0] % 2]
        rn = f"rstd{nrs[0] % 2}"
        nrs[0] += 1
        P.act(r[:, 0:n], pn[:, 0:n], AF.Ln, reads=["pn"], writes=[rn], bias=EPSB[0][:, 0:1], scale=1.0)
        P.act(r[:, 0:n], r[:, 0:n], AF.Exp, reads=[rn], writes=[rn], scale=-0.5)
        return r, rn

    epsb = P.sb("epsb", [128, 1], F32)
    P.memset(epsb[:], EPS, writes=["epsb"])
    EPSB = [epsb]

    def linear(Wd, kcn, M, in_tile, in_name, subs, consume):
        Wv = Wd.rearrange("(kc p) m -> p kc m", p=128)
        for blk in range(0, M, 512):
            bw = min(512, M - blk)
            wt, wn = ring.next()
            P.dma(wt[:, 0:kcn, 0:bw], Wv[:, :, blk:blk + bw], writes=[wn], q="gpsimd")
            for m0 in range(0, bw, 128):
                mw = min(128, bw - m0)
                for (lo, hi) in subs:
                    n = hi - lo
                    ps, psn = pr.next()
                    for kc in range(kcn):
                        P.mm(ps[0:mw, 0:n], wt[:, kc, m0:m0 + mw], in_tile[:, kc, lo:hi], start=(kc == 0), stop=(kc == kcn - 1),
                             reads=[wn, (in_name, (kc, lo))], writes=[psn])
                    consume((blk + m0) // 128, mw, lo, hi, ps, psn)

    for mt in range(nmt):
        hoff = HAL if mt == 0 else 0
        c0 = 0 if mt == 0 else HAL + mt * TM
        nsub = TM // 512
        if hoff:
            subs = [(0, 2)] + [(2 + i * 512, 2 + (i + 1) * 512) for i in range(nsub)]
        else:
            subs = [(i * 512, (i + 1) * 512) for i in range(nsub)]
        real = [s for s in subs if s[1] - s[0] > 2]

        for (lo, hi) in subs:
            P.dma(x_sb[:, :, lo:hi], xv[:, :, c0 + lo:c0 + hi], writes=[("x", (m, lo)) for m in range(8)])
        if do_post:
            for (lo, hi) in subs:
                P.dma(y_sb[:, :, lo:hi], yv[:, :, c0 + lo:c0 + hi], writes=["a"] + [("y", (m, lo)) for m in range(8)], q="gpsimd")

            def c_mix(m, mw, lo, hi, ps, psn):
                P.cp(f8[:, m, lo:hi], ps[:, 0:hi - lo], reads=[psn], writes=[("f8", (m, lo))], eng="scalar")
            linear(w_o, 8, D, y_sb, "y", subs, c_mix)
            for (lo, hi) in subs:
                n = hi - lo
                r, rn = rms_rstd(f8, "f8", lo, hi)
                for m in range(8):
                    P.stt(f8[:, m, lo:hi], f8[:, m, lo:hi], gv_sb[:, m:m + 1], r[:, 0:n], ALU.mult, ALU.mult,
                          reads=[("f8", (m, lo)), rn, "gv_sb"], writes=[("f8", (m, lo))])
                    P.tt(x_sb[:, m, lo:hi], x_sb[:, m, lo:hi], f8[:, m, lo:hi], ALU.add,
                         reads=[("f8", (m, lo)), ("x", (m, lo))], writes=[("x", (m, lo))])
                r, rn = rms_rstd(x_sb, "x", lo, hi)
                for m in range(8):
                    P.stt(h_sb[:, m, lo:hi], x_sb[:, m, lo:hi], gv_sb[:, 8 + m:9 + m], r[:, 0:n], ALU.mult, ALU.mult,
                          reads=[("x", (m, lo)), rn, "gv_sb"], writes=[("h", (m, lo))])

            Wv1 = w_f1.rearrange("(kc p) m -> p kc m", p=128)
            first_act = True
            ngb = 0
            for blk in range(0, DFF, 512):
                bw = min(512, DFF - blk)
                wt, wn = ring.next()
                P.dma(wt[:, 0:8, 0:bw], Wv1[:, :, blk:blk + bw], writes=[wn], q="gpsimd")
                P.dma(wt[:, 8:16, 0:bw], Wv1[:, :, DFF + blk:DFF + blk + bw], writes=[wn], q="gpsimd")
                for m0 in range(0, bw, 128):
                    j = (blk + m0) // 128
                    for (lo, hi) in subs:
                        n = hi - lo
                        pg, pgn = pr.next()
                        for kc in range(8):
                            P.mm(pg[:, 0:n], wt[:, kc, m0:m0 + 128], h_sb[:, kc, lo:hi], start=(kc == 0), stop=(kc == 7),
                                 reads=[wn, ("h", (kc, lo))], writes=[pgn])
                        if n == 2:
                            P.ts(ghalo[:, j, :], pg[:, 0:2], hm_sb[:, 0:1], None, ALU.mult, reads=[pgn, "hm_sb"], writes=[("ghalo", j)])
                            continue
                        pu, pun = pr.next()
                        for kc in range(8):
                            P.mm(pu[:, 0:n], wt[:, 8 + kc, m0:m0 + 128], h_sb[:, kc, lo:hi], start=(kc == 0), stop=(kc == 7),
                                 reads=[wn, ("h", (kc, lo))], writes=[pun])
                        g_t, gname = gb[ngb % 2], f"gb{ngb % 2}"
                        c_t, cname = cv[ngb % 2], f"cv{ngb % 2}"
                        ngb += 1
                        P.cp(g_t[:, 0:2], ghalo[:, j, :], reads=[("ghalo", j)], writes=[(gname, 0)], eng="gpsimd")
                        P.cp(g_t[:, 2:2 + n], pg[:, 0:n], reads=[pgn], writes=[(gname, 1)], eng="scalar")
                        P.act(c_t[:, 0:n], pg[:, 0:n], AF.Identity, reads=[pgn, "cw_sb"], writes=[cname], scale=cw_sb[:, j, 2:3], bias=cw_sb[:, j, 3:4])
                        P.cp(ghalo[:, j, :], g_t[:, n:n + 2], reads=[(gname, 1)], writes=[("ghalo", j)], eng="gpsimd")
                        P.stt(c_t[:, 0:n], g_t[:, 1:1 + n], cw_sb[:, j, 1:2], c_t[:, 0:n], ALU.mult, ALU.add, reads=[gname, cname, "cw_sb"], writes=[cname])
                        P.stt(c_t[:, 0:n], g_t[:, 0:n], cw_sb[:, j, 0:1], c_t[:, 0:n], ALU.mult, ALU.add, reads=[gname, cname, "cw_sb"], writes=[cname])
                        P.act(c_t[:, 0:n], c_t[:, 0:n], AF.Gelu_apprx_tanh, reads=[cname], writes=[cname])
                        tlo = lo - hoff
                        wr = [("a", (j, tlo))]
                        if first_act:
                            wr.append("y")
                            first_act = False
                        P.tt(act3[:, j, tlo:tlo + n], c_t[:, 0:n], pu[:, 0:n], ALU.mult, reads=[cname, pun], writes=wr)

            rsubs = [(lo - hoff, hi - hoff) for (lo, hi) in real]

            def c_ffo(m, mw, lo, hi, ps, psn):
                P.cp(f8[:, m, lo:hi], ps[:, 0:hi - lo], reads=[psn], writes=[("f8", (m, lo))], eng="scalar")
            linear(w_f2, NFC, D, act3, "a", rsubs, c_ffo)
            for (lo, hi) in rsubs:
                n = hi - lo
                r, rn = rms_rstd(f8, "f8", lo, hi)
                for m in range(8):
                    P.stt(f8[:, m, lo:hi], f8[:, m, lo:hi], gv_sb[:, 16 + m:17 + m], r[:, 0:n], ALU.mult, ALU.mult,
                          reads=[("f8", (m, lo)), rn, "gv_sb"], writes=[("f8", (m, lo))])
                    P.tt(x_sb[:, m, hoff + lo:hoff + hi], x_sb[:, m, hoff + lo:hoff + hi], f8[:, m, lo:hi], ALU.add,
                         reads=[("f8", (m, lo)), ("x", (m, hoff + lo))], writes=[("x", (m, hoff + lo))])
                P.dma(xov[:, :, mt * TM + lo:mt * TM + hi], x_sb[:, :, hoff + lo:hoff + hi], reads=[("x", (m, hoff + lo)) for m in range(8)], is_output=True)
        if do_pre:
            ph = hoff if do_post else 0
            for (lo, hi) in real:
                n = hi - lo
                r, rn = rms_rstd(x_sb, "x", lo, hi)
                for m in range(8):
                    P.stt(h_sb[:, m, lo:hi], x_sb[:, m, lo:hi], gp_sb[:, m:m + 1], r[:, 0:n], ALU.mult, ALU.mult,
                          reads=[("x", (m, lo)), rn, "gp_sb"], writes=[("h", (m, lo))])

            def c_z(m, mw, lo, hi, ps, psn):
                n = hi - lo
                zt, zn = zst.next()
                P.cp(zt[0:mw, 0:n], ps[0:mw, 0:n], reads=[psn], writes=[zn], eng="scalar")
                oc = mt * TM + lo - ph
                P.dma(zT[m * 128:m * 128 + mw, oc:oc + n], zt[0:mw, 0:n], reads=[zn], is_output=True)
            linear(w_in, 8, C, h_sb, "h", real, c_z)
    return P.finish()


def make_consts(P, need_strict=False):
    c = {}
    ident = P.sb("ident", [128, 128], F32)
    P.memset(ident[:], 1.0, writes=["ident"], eng="gpsimd")
    P.op("gpsimd", "affine_select", reads=["ident"], writes=["ident"], out=ident[:], in_=ident[:], pattern=[[1, 128]],
         compare_op=ALU.is_equal, fill=0.0, base=0, channel_multiplier=-1)
    c["ident"] = ident
    mi = P.sb("mask_i", [128, 128], F32)
    P.memset(mi[:], 1.0, writes=["mask_i"], eng="gpsimd")
    P.op("gpsimd", "affine_select", reads=["mask_i"], writes=["mask_i"], out=mi[:], in_=mi[:], pattern=[[1, 128]],
         compare_op=ALU.is_ge, fill=0.0, base=0, channel_multiplier=-1)
    P.memset(mi[0:64, 64:128], 0.0, writes=["mask_i"], eng="gpsimd")
    c["mask_i"] = mi
    if need_strict:
        ms = P.sb("mask_s", [128, 128], F32)
        P.memset(ms[:], 1.0, writes=["mask_s"], eng="gpsimd")
        P.op("gpsimd", "affine_select", reads=["mask_s"], writes=["mask_s"], out=ms[:], in_=ms[:], pattern=[[1, 128]],
             compare_op=ALU.is_gt, fill=0.0, base=0, channel_multiplier=-1)
        P.memset(ms[0:64, 64:128], 0.0, writes=["mask_s"], eng="gpsimd")
        c["mask_s"] = ms
    ob = P.sb("ones_bd", [128, 128], F32)
    P.memset(ob[:], 0.0, writes=["ones_bd"], eng="gpsimd")
    P.memset(ob[0:64, 0:64], 1.0 / 64, writes=["ones_bd"], eng="gpsimd")
    P.memset(ob[64:128, 64:128], 1.0 / 64, writes=["ones_bd"], eng="gpsimd")
    c["ones_bd"] = ob
    cm = P.sb("cmask", [128, 8, 64], F32)
    P.memset(cm[:], 1.0, writes=["cmask"], eng="gpsimd")
    P.memset(cm[:, :, 0:1], 0.0, writes=["cmask"], eng="gpsimd")
    c["cmask"] = cm
    return c


def bd_write(P, dst, dname, src, sname, mul, mname, nch, extra_reads=()):
    for h in range(2):
        pp = slice(64 * h, 64 * h + 64)
        s3 = src[pp, 0:nch * 64].rearrange("p (c t) -> p c t", t=64)
        m3 = mul[pp, 0:nch * 64].rearrange("p (c t) -> p c t", t=64)
        P.tt(dst[pp, 0:nch, 64 * h:64 * h + 64], s3, m3, ALU.mult, reads=[sname, mname] + list(extra_reads), writes=[(dname, h)],
             eng=("vector" if h == 0 else "gpsimd"))


def build_hgrn(T, layer_o):
    P = Prog()
    TT = 512
    NCH = TT // 64
    ntile = T // TT
    zin = P.dram("zin", [512, T])
    vtok = P.dram("vtok", [2, 64, T // 64, 64])
    prm = P.dram("prm", [128, 4])
    yT = P.dram("yT", [128, T], F32, kind="ExternalOutput")
    C = make_consts(P)
    prm_sb = P.sb("prm_sb", [128, 4], F32)
    P.dma(prm_sb[:], prm, writes=["prm"])
    lbv = P.sb("lbv", [128, 2], F32)
    if layer_o == 0:
        P.memset(lbv[:, 0:1], 0.0, writes=["lbv"])
        P.memset(lbv[:, 1:2], 1.0, writes=["lbv"])
    else:
        P.tt(lbv[:, 0:1], prm_sb[:, 1:2], prm_sb[:, 0:1], ALU.subtract, reads=["prm"], writes=["lbv"])
        P.act(lbv[:, 0:1], lbv[:, 0:1], AF.Sigmoid, reads=["lbv"], writes=["lbv"])
        P.ts(lbv[:, 1:2], lbv[:, 0:1], -1.0, 1.0, ALU.mult, ALU.add, reads=["lbv"], writes=["lbv"])
    epsb = P.sb("epsb", [128, 1], F32)
    P.memset(epsb[:], EPS, writes=["epsb"])

    zv = zin.rearrange("(q p) t -> p q t", p=128)
    NB = 2
    z_sb = [P.sb(f"z_sb{i}", [128, 4, TT], F32) for i in range(NB)]
    f_sb = [P.sb(f"f_sb{i}", [128, TT], F32) for i in range(NB)]
    k_sb = [P.sb(f"k_sb{i}", [128, TT], F32) for i in range(NB)]
    b_sb = [P.sb(f"b_sb{i}", [128, TT], F32) for i in range(NB)]
    eb_sb = [P.sb(f"eb_sb{i}", [128, TT], F32) for i in range(NB)]
    enb_sb = [P.sb(f"enb_sb{i}", [128, TT], F32) for i in range(NB)]
    qF = [P.sb(f"qF{i}", [128, NCH, 128], F32) for i in range(NB)]
    kF = [P.sb(f"kF{i}", [128, NCH, 128], F32) for i in range(NB)]
    Vb = [P.sb(f"Vb{i}", [128, NCH, 128], F32) for i in range(NB)]
    y_sb = [P.sb(f"y_sb{i}", [128, TT], F32) for i in range(NB)]
    for i in range(NB):
        for t_, n_ in ((qF[i], f"qF{i}"), (kF[i], f"kF{i}"), (Vb[i], f"Vb{i}")):
            P.memset(t_[:], 0.0, writes=[n_], eng="gpsimd")
    Tst = [P.sb(f"Tst{i}", [128, 128], F32) for i in range(2)]
    P.memset(Tst[0][:], 0.0, writes=["Tst0"])
    tmpT = P.sb("tmpT", [128, 128], F32)
    MTm = [P.sb(f"MTm{i}", [128, 128], F32) for i in range(2)]
    kTok = [P.sb(f"kTok{i}", [128, 128], F32) for i in range(2)]
    p_mt = [P.ps(f"p_mt{i}", [128, 512], F32) for i in range(2)]
    p_tr = [P.ps(f"p_tr{i}", [128, 512], F32) for i in range(2)]
    p_su = P.ps("p_su", [128, 512], F32)
    p_o = [P.ps(f"p_o{i}", [128, 512], F32) for i in range(2)]
    p_n = P.ps("p_n", [128, 512], F32)
    nst = 0
    for ti in range(ntile):
        i = ti % NB
        t0 = ti * TT
        zs, zn = z_sb[i], f"z{i}"
        P.dma(zs[:], zv[:, :, t0:t0 + TT], writes=[zn])
        c0 = ti * NCH
        for h in range(2):
            P.dma(Vb[i][64 * h:64 * h + 64, :, 64 * h:64 * h + 64], vtok[h, :, c0:c0 + NCH, :], writes=[(f"Vb{i}", h)], q="gpsimd")
        f, fn = f_sb[i], f"f{i}"
        P.act(f[:], zs[:, 1, :], AF.Sigmoid, reads=[zn], writes=[fn])
        P.ts(f[:], f[:], lbv[:, 1:2], lbv[:, 0:1], ALU.mult, ALU.add, reads=[fn, "lbv"], writes=[fn])
        k, kn = k_sb[i], f"k{i}"
        P.ts(k[:], f[:], -1.0, 1.0, ALU.mult, ALU.add, reads=[fn], writes=[kn])
        P.act(f[:], f[:], AF.Ln, reads=[fn], writes=[fn])
        bb, bn = b_sb[i], f"b{i}"
        P.op("vector", "tensor_tensor_scan", reads=[fn, "cmask"], writes=[bn], out=bb[:], data0=C["cmask"][:].rearrange("p c t -> p (c t)"),
             data1=f[:], initial=0.0, op0=ALU.mult, op1=ALU.add)
        eb, ebn = eb_sb[i], f"eb{i}"
        enb, enbn = enb_sb[i], f"enb{i}"
        P.act(eb[:], bb[:], AF.Exp, reads=[bn], writes=[ebn])
        P.act(enb[:], bb[:], AF.Exp, reads=[bn], writes=[enbn], scale=-1.0)
        P.act(zs[:, 0, :], zs[:, 0, :], AF.Silu, reads=[zn], writes=[zn])
        P.act(zs[:, 3, :], zs[:, 3, :], AF.Silu, reads=[zn], writes=[zn])
        bd_write(P, qF[i], f"qF{i}", zs[:, 0, :], zn, eb, ebn, NCH)
        bd_write(P, kF[i], f"kF{i}", k, kn, enb, enbn, NCH)
        y, yn = y_sb[i], f"y{i}"
        for c in range(NCH):
            j = nst % 2
            Tc, Tcn = Tst[j], f"Tst{j}"
            Tn, Tnn = Tst[1 - j], f"Tst{1 - j}"
            pm, pmn = p_mt[nst % 2], f"p_mt{nst % 2}"
            po, pon = p_o[nst % 2], f"p_o{nst % 2}"
            mtm, mtn = MTm[nst % 2], f"MTm{nst % 2}"
            kt, ktn = kTok[nst % 2], f"kTok{nst % 2}"
            nst += 1
            ptr, ptrn = p_tr[(nst - 1) % 2], f"p_tr{(nst - 1) % 2}"
            P.mm(pm[:, 0:128], kF[i][:, c, :], qF[i][:, c, :], reads=[f"kF{i}", f"qF{i}"], writes=[pmn])
            P.tt(mtm[:], pm[:, 0:128], C["mask_i"][:], ALU.mult, reads=[pmn, "mask_i"], writes=[mtn])
            P.tr(ptr[:, 0:128], kF[i][:, c, :], C["ident"][:], reads=[f"kF{i}", "ident"], writes=[ptrn])
            P.cp(kt[:], ptr[:, 0:128], reads=[ptrn], writes=[ktn], eng="scalar")
            P.mm(po[:, 0:128], Tc[:], qF[i][:, c, :], start=True, stop=False, reads=[Tcn, f"qF{i}"], writes=[pon])
            P.mm(po[:, 0:128], Vb[i][:, c, :], mtm[:], start=False, stop=True, reads=[f"Vb{i}", mtn], writes=[pon])
            for h in range(2):
                P.cp(y[64 * h:64 * h + 64, c * 64:c * 64 + 64], po[64 * h:64 * h + 64, 64 * h:64 * h + 64], reads=[pon], writes=[(yn, (c, h))], eng="scalar")
            P.mm(p_su[:, 0:128], kt[:], Vb[i][:, c, :], reads=[ktn, f"Vb{i}"], writes=["p_su"])
            pl = eb[:, c * 64 + 63:c * 64 + 64]
            P.ts(tmpT[:], Tc[:], pl, None, ALU.mult, reads=[Tcn, ebn], writes=["tmpT"])
            P.stt(Tn[:], p_su[:, 0:128], pl, tmpT[:], ALU.mult, ALU.add, reads=["p_su", ebn, "tmpT"], writes=[Tnn])
        P.act(k[:], y[:], AF.Square, reads=[yn], writes=[kn])
        P.mm(p_n[:, 0:TT], C["ones_bd"][:], k[:], reads=["ones_bd", kn], writes=["p_n"])
        P.act(k[:], p_n[:, 0:TT], AF.Ln, reads=["p_n"], writes=[kn], bias=epsb[:, 0:1], scale=1.0)
        P.act(k[:], k[:], AF.Exp, reads=[kn], writes=[kn], scale=-0.5)
        P.stt(y[:], y[:], prm_sb[:, 2:3], k[:], ALU.mult, ALU.mult, reads=[yn, kn, "prm"], writes=[yn])
        P.tt(y[:], y[:], zs[:, 3, :], ALU.mult, reads=[yn, zn], writes=[yn])
        P.dma(yT[:, t0:t0 + TT], y[:], reads=[yn], is_output=True)
    return P.finish()


DEC = 0.6065306597126334


def build_rwkv(T, has_vres):
    P = Prog()
    TT = 512
    NCH = TT // 64
    ntile = T // TT
    zin = P.dram("zin", [384, T])
    lrin = P.dram("lrin", [160, T])
    prm = P.dram("prm", [128, 16])
    prm2 = P.dram("prm2", [96, 4])
    wlr = P.dram("wlr", [96, 4, 128])
    if has_vres:
        vlr = P.dram("vlr", [32, T])
        vfirst = P.dram("vfirst", [128, T])
    yT = P.dram("yT", [128, T], F32, kind="ExternalOutput")
    vout = P.dram("vout", [128, T], F32, kind="ExternalOutput")
    C = make_consts(P, need_strict=True)
    mA = P.sb("mA", [128, 256], F32)
    mK = P.sb("mK", [128, 256], F32)
    P.ts(mA[:, 0:128], C["mask_s"][:], -1.0, None, ALU.mult, reads=["mask_s"], writes=["mA"])
    P.cp(mA[:, 128:256], C["mask_i"][:], reads=["mask_i"], writes=["mA"])
    P.cp(mK[:, 0:128], C["mask_s"][:], reads=["mask_s"], writes=["mK"])
    P.cp(mK[:, 128:256], C["mask_i"][:], reads=["mask_i"], writes=["mK"])
    prm_sb = P.sb("prm_sb", [128, 16], F32)
    prm2_sb = P.sb("prm2_sb", [96, 4], F32)
    wlr_sb = P.sb("wlr_sb", [96, 4, 128], F32)
    P.dma(prm_sb[:], prm, writes=["prm"])
    P.dma(prm2_sb[:], prm2, writes=["prm2"])
    P.dma(wlr_sb[:], wlr, writes=["wlr"])
    omk = P.sb("omk", [128, 1], F32)
    P.ts(omk[:], prm_sb[:, 6:7], -1.0, 1.0, ALU.mult, ALU.add, reads=["prm"], writes=["omk"])
    cb = P.sb("cb", [128, 2], F32)
    P.memset(cb[:, 0:1], 1e-6, writes=["cb"])
    P.memset(cb[:, 1:2], 64e-5, writes=["cb"])

    def col(j):
        return prm_sb[:, j:j + 1]

    zv = zin.rearrange("(q p) t -> p q t", p=128)
    z_sb = [P.sb(f"z_sb{i}", [128, 3, 1 + TT], F32) for i in range(2)]
    l_sb = [P.sb(f"l_sb{i}", [96, 3, 1 + TT], F32) for i in range(2)]
    zl = [P.sb(f"zl{i}", [128, 3, TT], F32) for i in range(2)]
    ll = P.sb("ll", [96, 3, TT], F32)
    if has_vres:
        vl_sb = P.sb("vl_sb", [32, TT], F32)
        vf_sb = P.sb("vf_sb", [128, TT], F32)
    names1 = ["sw", "bb", "bx", "a_s", "kkr", "t1", "kk", "kmod", "alpha", "enb", "ebx"]
    S1 = {n: P.sb("s_" + n, [128, TT], F32) for n in names1}
    eb = [P.sb(f"eb{i}", [128, TT], F32) for i in range(2)]
    g_sb = [P.sb(f"g_sb{i}", [128, TT], F32) for i in range(2)]
    bv = [P.sb(f"bv{i}", [128, TT], F32) for i in range(2)]
    y_sb = [P.sb(f"y_sb{i}", [128, TT], F32) for i in range(2)]
    aF = [P.sb(f"aF{i}", [128, NCH, 128], F32) for i in range(2)]
    kF = [P.sb(f"kF{i}", [128, NCH, 128], F32) for i in range(2)]
    cq = [P.sb(f"cq{i}", [128, NCH, 256], F32) for i in range(2)]
    vTb = P.sb("vTb", [128, NCH, 128], F32)
    Vb = [P.sb(f"Vb{i}", [128, NCH, 128], F32) for i in range(2)]
    aTok = [P.sb(f"aTok{i}", [128, NCH, 128], F32) for i in range(2)]
    kTok = [P.sb(f"kTok{i}", [128, NCH, 128], F32) for i in range(2)]
    for i in range(2):
        for t_, n_ in ((aF[i], f"aF{i}"), (kF[i], f"kF{i}"), (cq[i], f"cq{i}")):
            P.memset(t_[:], 0.0, writes=[n_], eng="gpsimd")
    P.memset(vTb[:], 0.0, writes=["vTb"], eng="gpsimd")
    Tst = [P.sb(f"Tst{i}", [128, 128], F32) for i in range(2)]
    P.memset(Tst[0][:], 0.0, writes=["Tst0"])
    tmpT = P.sb("tmpT", [128, 128], F32)
    MA = [P.sb(f"MA{i}", [128, 256], F32) for i in range(2)]
    MK = [P.sb(f"MK{i}", [128, 256], F32) for i in range(2)]
    YY = [P.sb(f"YY{i}", [128, 256], F32) for i in range(2)]
    PI = [P.sb(f"PI{i}", [128, 128], F32) for i in range(2)]
    W0s = P.sb("W0s", [128, 128], F32)
    Us = P.sb("Us", [128, 128], F32)
    pA = P.ps("pA", [128, 512], F32)
    pB = P.ps("pB", [128, 512], F32)
    pMA = P.ps("pMA", [128, 512], F32)
    pMK = P.ps("pMK", [128, 512], F32)
    pY = P.ps("pY", [128, 512], F32)
    pW = P.ps("pW", [128, 512], F32)
    pU = P.ps("pU", [128, 512], F32)
    pO = P.ps("pO", [128, 512], F32)

    def prep(ti):
        i = ti % 2
        t0 = ti * TT
        zs, zn = z_sb[i], f"z{i}"
        ls, ln_ = l_sb[i], f"l{i}"
        if ti == 0:
            P.memset(zs[:, :, 0:1], 0.0, writes=[zn])
            P.memset(ls[:, :, 0:1], 0.0, writes=[ln_])
            P.dma(zs[:, :, 1:1 + TT], zv[:, :, 0:TT], writes=[zn])
            P.dma(ls[0:32, 0, 1:1 + TT], lrin[0:32, 0:TT], writes=[ln_])
            P.dma(ls[0:32, 1, 1:1 + TT], lrin[32:64, 0:TT], writes=[ln_])
            P.dma(ls[0:96, 2, 1:1 + TT], lrin[64:160, 0:TT], writes=[ln_])
        else:
            P.dma(zs[:], zv[:, :, t0 - 1:t0 + TT], writes=[zn])
            P.dma(ls[0:32, 0, :], lrin[0:32, t0 - 1:t0 + TT], writes=[ln_])
            P.dma(ls[0:32, 1, :], lrin[32:64, t0 - 1:t0 + TT], writes=[ln_])
            P.dma(ls[0:96, 2, :], lrin[64:160, t0 - 1:t0 + TT], writes=[ln_])
        z, zln = zl[i], f"zl{i}"
        P.tt(z[:], zs[:, :, 0:TT], zs[:, :, 1:1 + TT], ALU.subtract, reads=[zn], writes=[zln])
        for q in range(3):
            P.stt(z[:, q, :], z[:, q, :], col(q), zs[:, q, 1:1 + TT], ALU.mult, ALU.add, reads=[zln, zn, "prm"], writes=[zln])
        for q, rows in ((0, 32), (1, 32), (2, 96)):
            P.tt(ll[0:rows, q, :], ls[0:rows, q, 0:TT], ls[0:rows, q, 1:1 + TT], ALU.subtract, reads=[ln_], writes=["ll"])
            P.stt(ll[0:rows, q, :], ll[0:rows, q, :], prm2_sb[0:rows, q:q + 1], ls[0:rows, q, 1:1 + TT], ALU.mult, ALU.add, reads=["ll", ln_, "prm2"], writes=["ll"])
        r_, k_, v_ = z[:, 0, :], z[:, 1, :], z[:, 2, :]
        if has_vres:
            P.dma(vl_sb[:], vlr[:, t0:t0 + TT], writes=["vl"])
            P.dma(vf_sb[:], vfirst[:, t0:t0 + TT], writes=["vf"])
            P.mm(pA[:, 0:TT], wlr_sb[0:32, 3, :], vl_sb[:], reads=["wlr", "vl"], writes=["pA"])
            P.act(S1["t1"][:], pA[:, 0:TT], AF.Sigmoid, reads=["pA", "prm"], writes=["t1"], bias=col(10), scale=1.0)
            P.tt(vf_sb[:], vf_sb[:], v_, ALU.subtract, reads=["vf", zln], writes=["vf"])
            P.tt(vf_sb[:], vf_sb[:], S1["t1"][:], ALU.mult, reads=["vf", "t1"], writes=["vf"])
            P.tt(v_, v_, vf_sb[:], ALU.add, reads=["vf", zln], writes=[zln])
        P.dma(vout[:, t0:t0 + TT], v_, reads=[zln], is_output=True)
        P.act(ll[0:32, 0, :], ll[0:32, 0, :], AF.Tanh, reads=["ll"], writes=["ll"])
        P.mm(pA[:, 0:TT], wlr_sb[0:32, 0, :], ll[0:32, 0, :], reads=["wlr", "ll"], writes=["pA"])
        P.act(S1["sw"][:], pA[:, 0:TT], AF.Sigmoid, reads=["pA", "prm"], writes=["sw"], bias=col(3), scale=1.0)
        P.op("vector", "tensor_tensor_scan", reads=["sw", "cmask"], writes=["bb"], out=S1["bb"][:], data0=C["cmask"][:].rearrange("p c t -> p (c t)"),
             data1=S1["sw"][:], initial=0.0, op0=ALU.mult, op1=ALU.add)
        P.tt(S1["bx"][:], S1["bb"][:], S1["sw"][:], ALU.subtract, reads=["bb", "sw"], writes=["bx"])
        e_, en = eb[i], f"eb{i}"
        P.act(e_[:], S1["bb"][:], AF.Exp, reads=["bb"], writes=[en], scale=-DEC)
        P.act(S1["enb"][:], S1["bb"][:], AF.Exp, reads=["bb"], writes=["enb"], scale=DEC)
        P.act(S1["ebx"][:], S1["bx"][:], AF.Exp, reads=["bx"], writes=["ebx"], scale=-DEC)
        P.mm(pB[:, 0:TT], wlr_sb[0:32, 1, :], ll[0:32, 1, :], reads=["wlr", "ll"], writes=["pB"])
        P.act(S1["a_s"][:], pB[:, 0:TT], AF.Sigmoid, reads=["pB", "prm"], writes=["a_s"], bias=col(4), scale=1.0)
        P.act(ll[0:96, 2, :], ll[0:96, 2, :], AF.Sigmoid, reads=["ll"], writes=["ll"])
        P.mm(pA[:, 0:TT], wlr_sb[0:96, 2, :], ll[0:96, 2, :], reads=["wlr", "ll"], writes=["pA"])
        P.cp(g_sb[i][:], pA[:, 0:TT], reads=["pA"], writes=[f"g{i}"], eng="scalar")
        P.ts(S1["kkr"][:], k_, col(5), None, ALU.mult, reads=[zln, "prm"], writes=["kkr"])
        P.act(S1["t1"][:], S1["kkr"][:], AF.Square, reads=["kkr"], writes=["t1"])
        P.mm(pB[:, 0:TT], C["ones_bd"][:], S1["t1"][:], reads=["ones_bd", "t1"], writes=["pB"])
        P.act(S1["t1"][:], pB[:, 0:TT], AF.Ln, reads=["pB", "cb"], writes=["t1"], bias=cb[:, 0:1], scale=64.0)
        P.act(S1["t1"][:], S1["t1"][:], AF.Exp, reads=["t1"], writes=["t1"], scale=-0.5)
        P.tt(S1["kk"][:], S1["kkr"][:], S1["t1"][:], ALU.mult, reads=["kkr", "t1"], writes=["kk"])
        P.ts(S1["kmod"][:], S1["a_s"][:], col(6), omk[:, 0:1], ALU.mult, ALU.add, reads=["a_s", "prm", "omk"], writes=["kmod"])
        P.tt(S1["kmod"][:], S1["kmod"][:], k_, ALU.mult, reads=["kmod", zln], writes=["kmod"])
        P.tt(S1["alpha"][:], S1["kk"][:], S1["a_s"][:], ALU.mult, reads=["kk", "a_s"], writes=["alpha"])
        P.stt(S1["t1"][:], r_, col(7), S1["kmod"][:], ALU.mult, ALU.mult, reads=[zln, "prm", "kmod"], writes=["t1"])
        P.mm(pA[:, 0:TT], C["ones_bd"][:], S1["t1"][:], reads=["ones_bd", "t1"], writes=["pA"])
        P.stt(bv[i][:], pA[:, 0:TT], 64.0, v_, ALU.mult, ALU.mult, reads=["pA", zln], writes=[f"bv{i}"])
        bd_write(P, aF[i], f"aF{i}", S1["alpha"], "alpha", S1["enb"], "enb", NCH)
        bd_write(P, kF[i], f"kF{i}", S1["kmod"], "kmod", S1["enb"], "enb", NCH)
        bd_write(P, cq[i][:, :, 0:128], f"cq{i}", S1["kk"], "kk", S1["ebx"], "ebx", NCH)
        bd_write(P, cq[i][:, :, 128:256], f"cq{i}", z[:, 0, :], zln, e_, en, NCH)
        for h in range(2):
            pp = slice(64 * h, 64 * h + 64)
            P.cp(vTb[pp, :, 64 * h:64 * h + 64], z[pp, 2, :].rearrange("p (c t) -> p c t", t=64), reads=[zln], writes=[("vTb", h)])
        for (src, sn, dst, dn) in ((vTb, "vTb", Vb[i], f"Vb{i}"), (aF[i], f"aF{i}", aTok[i], f"aTok{i}"), (kF[i], f"kF{i}", kTok[i], f"kTok{i}")):
            for half in range(2):
                pt, ptn = (pA, "pA") if half == 0 else (pB, "pB")
                for cc in range(4):
                    c = half * 4 + cc
                    P.tr(pt[:, cc * 128:(cc + 1) * 128], src[:, c, :], C["ident"][:], reads=[sn, "ident"], writes=[ptn])
                P.cp(dst[:, half * 4:half * 4 + 4, :], pt[:, :].rearrange("p (c t) -> p c t", t=128), reads=[ptn], writes=[(dn, half)], eng="scalar")

    nst = [0]

    def chunks(ti):
        i = ti % 2
        y, yn = y_sb[i], f"y{i}"
        for c in range(NCH):
            j = nst[0] % 2
            nst[0] += 1
            Tc, Tcn = Tst[j], f"Tst{j}"
            Tn, Tnn = Tst[1 - j], f"Tst{1 - j}"
            ma, man = MA[j], f"MA{j}"
            mk, mkn = MK[j], f"MK{j}"
            P.mm(pMA[:, 0:256], aF[i][:, c, :], cq[i][:, c, :], reads=[f"aF{i}", f"cq{i}"], writes=["pMA"])
            P.tt(ma[:], pMA[:, 0:256], mA[:], ALU.mult, reads=["pMA", "mA"], writes=[man])
            P.mm(pMK[:, 0:256], kF[i][:, c, :], cq[i][:, c, :], reads=[f"kF{i}", f"cq{i}"], writes=["pMK"])
            P.tt(mk[:], pMK[:, 0:256], mK[:], ALU.mult, reads=["pMK", "mK"], writes=[mkn])
            yy, yyn = YY[0], "YY0"
            P.tr(pY[:, 0:128], ma[:, 0:128], C["ident"][:], reads=[man, "ident"], writes=["pY"])
            P.cp(yy[:, 128:256], pY[:, 0:128], reads=["pY"], writes=[(yyn, 1)], eng="scalar")
            P.cp(yy[:, 0:128], ma[:, 0:128], reads=[man], writes=[(yyn, 0)], eng="gpsimd")
            pi, pin = PI[0], "PI0"
            P.tt(pi[:], ma[:, 0:128], C["ident"][:], ALU.add, reads=[man, "ident"], writes=[pin])
            for k in range(1, 6):
                y2, y2n = YY[k % 2], f"YY{k % 2}"
                P.mm(pY[:, 0:128], yy[:, 128:256], yy[:, 0:128], reads=[yyn], writes=["pY"])
                P.mm(pY[:, 128:256], yy[:, 0:128], yy[:, 128:256], reads=[yyn], writes=["pY"])
                P.cp(y2[:], pY[:, 0:256], reads=["pY"], writes=[y2n], eng="scalar")
                pi2, pi2n = PI[k % 2], f"PI{k % 2}"
                P.mm(pW[:, 0:128], y2[:, 128:256], pi[:], reads=[y2n, pin], writes=["pW"])
                P.tt(pi2[:], pW[:, 0:128], pi[:], ALU.add, reads=["pW", pin], writes=[pi2n])
                yy, yyn, pi, pin = y2, y2n, pi2, pi2n
            P.mm(pW[:, 0:128], cq[i][:, c, 0:128], Tc[:], start=True, stop=False, reads=[f"cq{i}", Tcn], writes=["pW"])
            P.mm(pW[:, 0:128], mk[:, 0:128], Vb[i][:, c, :], start=False, stop=True, reads=[mkn, f"Vb{i}"], writes=["pW"])
            P.act(W0s[:], pW[:, 0:128], AF.Identity, reads=["pW"], writes=["W0s"], scale=-1.0)
            P.mm(pU[:, 0:128], pi[:], W0s[:], reads=[pin, "W0s"], writes=["pU"])
            P.cp(Us[:], pU[:, 0:128], reads=["pU"], writes=["Us"], eng="scalar")
            P.mm(pO[:, 0:128], Tc[:], cq[i][:, c, 128:256], start=True, stop=False, reads=[Tcn, f"cq{i}"], writes=["pO"])
            P.mm(pO[:, 0:128], Us[:], ma[:, 128:256], start=False, stop=False, reads=["Us", man], writes=["pO"])
            P.mm(pO[:, 0:128], Vb[i][:, c, :], mk[:, 128:256], start=False, stop=True, reads=[f"Vb{i}", mkn], writes=["pO"])
            for h in range(2):
                P.cp(y[64 * h:64 * h + 64, c * 64:c * 64 + 64], pO[64 * h:64 * h + 64, 64 * h:64 * h + 64], reads=["pO"], writes=[(yn, (c, h))], eng="scalar")
            P.mm(pU[:, 128:256], aTok[i][:, c, :], Us[:], start=True, stop=False, reads=[f"aTok{i}", "Us"], writes=["pU"])
            P.mm(pU[:, 128:256], kTok[i][:, c, :], Vb[i][:, c, :], start=False, stop=True, reads=[f"kTok{i}", f"Vb{i}"], writes=["pU"])
            pl = eb[i][:, c * 64 + 63:c * 64 + 64]
            P.ts(tmpT[:], Tc[:], pl, None, ALU.mult, reads=[Tcn, f"eb{i}"], writes=["tmpT"])
            P.stt(Tn[:], pU[:, 128:256], pl, tmpT[:], ALU.mult, ALU.add, reads=["pU", f"eb{i}", "tmpT"], writes=[Tnn])

    def post(ti):
        i = ti % 2
        t0 = ti * TT
        y, yn = y_sb[i], f"y{i}"
        t1 = S1["t1"]
        P.mm(pA[:, 0:TT], C["ones_bd"][:], y[:], reads=["ones_bd", yn], writes=["pA"])
        P.tt(y[:], y[:], pA[:, 0:TT], ALU.subtract, reads=[yn, "pA"], writes=[yn])
        P.act(t1[:], y[:], AF.Square, reads=[yn], writes=["t1"])
        P.mm(pB[:, 0:TT], C["ones_bd"][:], t1[:], reads=["ones_bd", "t1"], writes=["pB"])
        P.act(t1[:], pB[:, 0:TT], AF.Ln, reads=["pB", "cb"], writes=["t1"], bias=cb[:, 1:2], scale=1.0)
        P.act(t1[:], t1[:], AF.Exp, reads=["t1"], writes=["t1"], scale=-0.5)
        P.stt(y[:], y[:], col(8), t1[:], ALU.mult, ALU.mult, reads=[yn, "prm", "t1"], writes=[yn])
        P.stt(y[:], y[:], col(9), bv[i][:], ALU.add, ALU.add, reads=[yn, "prm", f"bv{i}"], writes=[yn])
        P.tt(y[:], y[:], g_sb[i][:], ALU.mult, reads=[yn, f"g{i}"], writes=[yn])
        P.dma(yT[:, t0:t0 + TT], y[:], reads=[yn], is_output=True)

    prep(0)
    for ti in range(ntile):
        if ti + 1 < ntile:
            prep(ti + 1)
        chunks(ti)
        post(ti)
    return P.finish()


def build_gdn(T, stage=9):
    P = Prog()
    TT = 512
    NCH = TT // 64
    ntile = T // TT
    NC_ALL = T // 64
    zin = P.dram("zin", [512, T])
    abrow = P.dram("abrow", [2, T])
    abcol = P.dram("abcol", [64, 2, NC_ALL])
    prm = P.dram("prm", [128, 16])
    yT = P.dram("yT", [128, T], F32, kind="ExternalOutput")
    ident = P.sb("ident", [128, 128], F32)
    P.memset(ident[:], 1.0, writes=["ident"], eng="gpsimd")
    P.op("gpsimd", "affine_select", reads=["ident"], writes=["ident"], out=ident[:], in_=ident[:], pattern=[[1, 128]],
         compare_op=ALU.is_equal, fill=0.0, base=0, channel_multiplier=-1)
    m2 = P.sb("m2", [64, 2, 64], F32)
    P.memset(m2[:], 1.0, writes=["m2"], eng="gpsimd")
    P.op("gpsimd", "affine_select", reads=["m2"], writes=["m2"], out=m2[:, 0, :], in_=m2[:, 0, :], pattern=[[1, 64]],
         compare_op=ALU.is_gt, fill=0.0, base=0, channel_multiplier=-1)
    P.op("gpsimd", "affine_select", reads=["m2"], writes=["m2"], out=m2[:, 1, :], in_=m2[:, 1, :], pattern=[[1, 64]],
         compare_op=ALU.is_ge, fill=0.0, base=0, channel_multiplier=-1)
    P.ts(m2[:, 0, :], m2[:, 0, :], -1.0, None, ALU.mult, reads=["m2"], writes=["m2"])
    tri = P.sb("tri", [64, 64], F32)
    P.cp(tri[:], m2[:, 1, :], reads=["m2"], writes=["tri"])
    o64 = P.sb("o64", [64, 64], F32)
    P.memset(o64[:], 1.0, writes=["o64"])
    o128 = P.sb("o128", [128, 128], F32)
    P.memset(o128[:], 1.0, writes=["o128"])
    cm = P.sb("cmask", [128, NCH, 64], F32)
    P.memset(cm[:], 1.0, writes=["cmask"], eng="gpsimd")
    P.memset(cm[:, :, 0:1], 0.0, writes=["cmask"], eng="gpsimd")
    prm_sb = P.sb("prm_sb", [128, 16], F32)
    P.dma(prm_sb[:], prm, writes=["prm"])
    cb = P.sb("cb", [128, 4], F32)
    P.memset(cb[:, 0:1], 1e-6, writes=["cb"])
    P.memset(cb[:, 1:2], 1.0, writes=["cb"])
    P.memset(cb[:, 2:3], -0.5 * float(np.log(128.0)), writes=["cb"])
    P.act(cb[:, 3:4], prm_sb[:, 12:13], AF.Exp, reads=["prm"], writes=["cb"])
    P.ts(cb[:, 3:4], cb[:, 3:4], -1.0, None, ALU.mult, reads=["cb"], writes=["cb"])

    def col(j):
        return prm_sb[:, j:j + 1]

    pA = P.ps("pA", [128, 512], F32)
    pB = P.ps("pB", [128, 512], F32)
    pM = P.ps("pM", [128, 512], F32)
    pY = P.ps("pY", [128, 512], F32)
    pP = P.ps("pP", [128, 512], F32)
    pUW = P.ps("pUW", [128, 512], F32)
    pV = P.ps("pV", [128, 512], F32)
    pO = P.ps("pO", [128, 512], F32)

    ac = P.sb("ac", [64, 2, NC_ALL], F32)
    P.dma(ac[:], abcol, writes=["ac"])
    betac = P.sb("betac", [64, NC_ALL], F32)
    gcol = P.sb("gcol", [64, NC_ALL], F32)
    gccol = P.sb("gccol", [64, NC_ALL], F32)
    c2 = P.sb("c2", [64, NC_ALL], F32)
    c3 = P.sb("c3", [64, NC_ALL], F32)
    P.act(betac[:], ac[:, 0, :], AF.Sigmoid, reads=["ac"], writes=["betac"])
    P.act(gcol[:], ac[:, 1, :], AF.Exp, reads=["ac", "prm"], writes=["gcol"], bias=prm_sb[0:64, 13:14], scale=1.0)
    P.act(gcol[:], gcol[:], AF.Ln, reads=["gcol", "cb"], writes=["gcol"], bias=cb[0:64, 1:2], scale=1.0)
    P.ts(gcol[:], gcol[:], cb[0:64, 3:4], None, ALU.mult, reads=["gcol", "cb"], writes=["gcol"])
    for c0 in range(0, NC_ALL, 512):
        n = min(512, NC_ALL - c0)
        P.mm(pA[0:64, 0:n], tri[:], gcol[:, c0:c0 + n], reads=["tri", "gcol"], writes=["pA"])
        P.cp(gccol[:, c0:c0 + n], pA[0:64, 0:n], reads=["pA"], writes=["gccol"], eng="scalar")
        P.mm(pB[0:64, 0:n], o64[:], gcol[:, c0:c0 + n], reads=["o64", "gcol"], writes=["pB"])
        P.tt(c3[:, c0:c0 + n], pB[0:64, 0:n], gccol[:, c0:c0 + n], ALU.subtract, reads=["pB", "gccol"], writes=["c3"])
    P.act(c3[:], c3[:], AF.Exp, reads=["c3"], writes=["c3"])
    P.act(c2[:], gccol[:], AF.Exp, reads=["gccol"], writes=["c2"])
    P.tt(c2[:], c2[:], betac[:], ALU.mult, reads=["c2", "betac"], writes=["c2"])

    zv = zin.rearrange("(q p) t -> p q t", p=128)
    z_sb = [P.sb(f"z_sb{i}", [128, 4, 3 + TT], F32) for i in range(2)]
    ab_sb = P.sb("ab_sb", [128, 2, TT], F32)
    cv_sb = P.sb("cv_sb", [128, 3, TT], F32)
    sq_sb = P.sb("sq_sb", [128, TT], F32)
    rn_sb = P.sb("rn_sb", [128, TT], F32)
    gcb = [P.sb(f"gcb{i}", [128, TT], F32) for i in range(2)]
    egc = [P.sb(f"egc{i}", [128, TT], F32) for i in range(2)]
    kT = [P.sb(f"kT{i}", [128, TT], F32) for i in range(2)]
    kq = [P.sb(f"kq{i}", [128, NCH, 128], F32) for i in range(2)]
    qg = [P.sb(f"qg{i}", [128, TT], F32) for i in range(2)]
    sg = [P.sb(f"sg{i}", [128, TT], F32) for i in range(2)]
    DTm = [P.sb(f"DTm{i}", [64, NCH, 2, 64], F32) for i in range(2)]
    dt_tmp = P.sb("dt_tmp", [64, NCH, 64], F32)
    RV = [P.sb(f"RV{i}", [64, NCH, 128], F32) for i in range(2)]
    RK = [P.sb(f"RK{i}", [64, NCH, 128], F32) for i in range(2)]
    KS = [P.sb(f"KS{i}", [64, NCH, 128], F32) for i in range(2)]
    y_sb = [P.sb(f"y_sb{i}", [128, TT], F32) for i in range(2)]
    Sst = [P.sb(f"Sst{i}", [128, 128], F32) for i in range(2)]
    P.memset(Sst[0][:], 0.0, writes=["Sst0"])
    tmpS = P.sb("tmpS", [128, 128], F32)
    M2 = [P.sb(f"M2{i}", [64, 128], F32) for i in range(2)]
    YY = [P.sb(f"YY{i}", [64, 128], F32) for i in range(2)]
    PI = [P.sb(f"PI{i}", [64, 64], F32) for i in range(2)]
    u_sb = P.sb("u_sb", [64, 128], F32)
    wT_sb = P.sb("wT_sb", [128, 64], F32)
    vn_sb = P.sb("vn_sb", [64, 128], F32)

    def prep(ti):
        i = ti % 2
        t0 = ti * TT
        c0 = ti * NCH
        zs, zn = z_sb[i], f"z{i}"
        if ti == 0:
            P.memset(zs[:, :, 0:3], 0.0, writes=[zn])
            P.dma(zs[:, :, 3:3 + TT], zv[:, :, 0:TT], writes=[zn])
        else:
            P.dma(zs[:], zv[:, :, t0 - 3:t0 + TT], writes=[zn])
        P.dma(ab_sb[:, 0, :], abrow[0:1, t0:t0 + TT].to_broadcast([128, TT]), writes=["ab"])
        P.dma(ab_sb[:, 1, :], abrow[1:2, t0:t0 + TT].to_broadcast([128, TT]), writes=["ab"])
        for q in range(3):
            P.ts(cv_sb[:, q, :], zs[:, q, 3:3 + TT], col(4 * q + 3), None, ALU.mult, reads=[zn, "prm"], writes=[("cv", q)])
            for j in range(3):
                P.stt(cv_sb[:, q, :], zs[:, q, j:j + TT], col(4 * q + j), cv_sb[:, q, :], ALU.mult, ALU.add, reads=[zn, "prm", ("cv", q)], writes=[("cv", q)])
        P.act(cv_sb[:], cv_sb[:], AF.Silu, reads=["cv"], writes=["cv"])
        P.act(sg[i][:], zs[:, 3, 3:3 + TT], AF.Silu, reads=[zn], writes=[f"sg{i}"])
        P.act(ab_sb[:, 0, :], ab_sb[:, 0, :], AF.Sigmoid, reads=["ab"], writes=["ab"])
        P.act(ab_sb[:, 1, :], ab_sb[:, 1, :], AF.Exp, reads=["ab", "prm"], writes=["ab"], bias=col(13), scale=1.0)
        P.act(ab_sb[:, 1, :], ab_sb[:, 1, :], AF.Ln, reads=["ab", "cb"], writes=["ab"], bias=cb[:, 1:2], scale=1.0)
        P.ts(ab_sb[:, 1, :], ab_sb[:, 1, :], cb[:, 3:4], None, ALU.mult, reads=["ab", "cb"], writes=["ab"])
        P.op("vector", "tensor_tensor_scan", reads=["ab", "cmask"], writes=[f"gcb{i}"], out=gcb[i][:], data0=cm[:].rearrange("p c t -> p (c t)"),
             data1=ab_sb[:, 1, :], initial=0.0, op0=ALU.mult, op1=ALU.add)
        P.act(egc[i][:], gcb[i][:], AF.Exp, reads=[f"gcb{i}"], writes=[f"egc{i}"])
        for q in range(2):
            P.act(sq_sb[:], cv_sb[:, q, :], AF.Square, reads=["cv"], writes=["sq"])
            pp, ppn = (pA, "pA") if q == 0 else (pB, "pB")
            P.mm(pp[:, 0:TT], o128[:], sq_sb[:], reads=["o128", "sq"], writes=[ppn])
            P.act(rn_sb[:], pp[:, 0:TT], AF.Ln, reads=[ppn, "cb"], writes=["rn"], bias=cb[:, 0:1], scale=1.0)
            if q == 0:
                P.act(rn_sb[:], rn_sb[:], AF.Exp, reads=["rn", "cb"], writes=["rn"], bias=cb[:, 2:3], scale=-0.5)
                P.tt(kq[i][:, :, 64:128], cv_sb[:, 0, :].rearrange("p (c t) -> p c t", t=64), rn_sb[:].rearrange("p (c t) -> p c t", t=64), ALU.mult,
                     reads=["cv", "rn"], writes=[(f"kq{i}", 1)])
            else:
                P.act(rn_sb[:], rn_sb[:], AF.Exp, reads=["rn"], writes=["rn"], scale=-0.5)
                P.tt(kT[i][:], cv_sb[:, 1, :], rn_sb[:], ALU.mult, reads=["cv", "rn"], writes=[f"kT{i}"])
        P.tt(qg[i][:].rearrange("p (c t) -> p c t", t=64), kq[i][:, :, 64:128], egc[i][:].rearrange("p (c t) -> p c t", t=64), ALU.mult,
             reads=[f"kq{i}", f"egc{i}"], writes=[f"qg{i}"])
        P.tt(kq[i][:, :, 0:64], kT[i][:].rearrange("p (c t) -> p c t", t=64), ab_sb[:, 0, :].rearrange("p (c t) -> p c t", t=64), ALU.mult,
             reads=[f"kT{i}", "ab"], writes=[(f"kq{i}", 0)])
        P.tt(dt_tmp[:], gcb[i][0:64, :].rearrange("p (c t) -> p c t", t=64), gccol[:, c0:c0 + NCH].unsqueeze(2).to_broadcast([64, NCH, 64]), ALU.subtract,
             reads=[f"gcb{i}", "gccol"], writes=["dt_tmp"])
        P.ts(dt_tmp[:], dt_tmp[:], 0.0, None, ALU.min, reads=["dt_tmp"], writes=["dt_tmp"])
        P.act(dt_tmp[:], dt_tmp[:], AF.Exp, reads=["dt_tmp"], writes=["dt_tmp"])
        for w in range(2):
            P.tt(DTm[i][:, :, w, :], dt_tmp[:], m2[:, w, :].unsqueeze(1).to_broadcast([64, NCH, 64]), ALU.mult, reads=["dt_tmp", "m2"], writes=[(f"DTm{i}", w)])
        for half in range(2):
            for cc in range(4):
                c = half * 4 + cc
                P.tr(pA[0:64, cc * 128:(cc + 1) * 128], kT[i][:, c * 64:(c + 1) * 64], ident[:], reads=[f"kT{i}", "ident"], writes=["pA"])
                P.tr(pB[0:64, cc * 128:(cc + 1) * 128], cv_sb[:, 2, c * 64:(c + 1) * 64], ident[:], reads=["cv", "ident"], writes=["pB"])
            cs = slice(c0 + half * 4, c0 + half * 4 + 4)
            hs = slice(half * 4, half * 4 + 4)
            pa3 = pA[0:64, :].rearrange("p (c t) -> p c t", t=128)
            pb3 = pB[0:64, :].rearrange("p (c t) -> p c t", t=128)
            P.tt(RK[i][:, hs, :], pa3, c2[:, cs].unsqueeze(2).to_broadcast([64, 4, 128]), ALU.mult, reads=["pA", "c2"], writes=[(f"RK{i}", half)])
            P.tt(KS[i][:, hs, :], pa3, c3[:, cs].unsqueeze(2).to_broadcast([64, 4, 128]), ALU.mult, reads=["pA", "c3"], writes=[(f"KS{i}", half)])
            P.tt(RV[i][:, hs, :], pb3, betac[:, cs].unsqueeze(2).to_broadcast([64, 4, 128]), ALU.mult, reads=["pB", "betac"], writes=[(f"RV{i}", half)])

    nst = [0]

    def chunks(ti):
        i = ti % 2
        y, yn = y_sb[i], f"y{i}"
        for c in range(NCH):
            j = nst[0] % 2
            nst[0] += 1
            Sc, Scn = Sst[j], f"Sst{j}"
            Sn, Snn = Sst[1 - j], f"Sst{1 - j}"
            mm2, m2n = M2[j], f"M2{j}"
            cs = slice(c * 64, c * 64 + 64)
            P.mm(pM[0:64, 0:128], kT[i][:, cs], kq[i][:, c, :], reads=[f"kT{i}", f"kq{i}"], writes=["pM"])
            P.tt(mm2[:], pM[0:64, 0:128], DTm[i][:, c, :, :].rearrange("p w t -> p (w t)"), ALU.mult, reads=["pM", f"DTm{i}"], writes=[m2n])
            if stage == 2:
                return
            yy, yyn = YY[0], "YY0"
            P.tr(pY[0:64, 0:64], mm2[:, 0:64], ident[0:64, 0:64], reads=[m2n, "ident"], writes=["pY"])
            P.cp(yy[:, 64:128], pY[0:64, 0:64], reads=["pY"], writes=[(yyn, 1)], eng="scalar")
            P.cp(yy[:, 0:64], mm2[:, 0:64], reads=[m2n], writes=[(yyn, 0)], eng="gpsimd")
            pi, pin = PI[0], "PI0"
            P.tt(pi[:], mm2[:, 0:64], ident[0:64, 0:64], ALU.add, reads=[m2n, "ident"], writes=[pin])
            for k in range(1, 6):
                y2, y2n = YY[k % 2], f"YY{k % 2}"
                P.mm(pY[0:64, 0:64], yy[:, 64:128], yy[:, 0:64], reads=[yyn], writes=["pY"])
                P.mm(pY[0:64, 64:128], yy[:, 0:64], yy[:, 64:128], reads=[yyn], writes=["pY"])
                P.cp(y2[:], pY[0:64, 0:128], reads=["pY"], writes=[y2n], eng="scalar")
                pi2, pi2n = PI[k % 2], f"PI{k % 2}"
                P.mm(pP[0:64, 0:64], y2[:, 64:128], pi[:], reads=[y2n, pin], writes=["pP"])
                P.tt(pi2[:], pP[0:64, 0:64], pi[:], ALU.add, reads=["pP", pin], writes=[pi2n])
                yy, yyn, pi, pin = y2, y2n, pi2, pi2n
            if stage == 3:
                return
            P.mm(pUW[0:64, 0:128], pi[:], RV[i][:, c, :], reads=[pin, f"RV{i}"], writes=["pUW"])
            P.mm(pUW[:, 128:192], RK[i][:, c, :], pi[:], reads=[pin, f"RK{i}"], writes=["pUW"])
            P.cp(u_sb[:], pUW[0:64, 0:128], reads=["pUW"], writes=["u"], eng="scalar")
            P.cp(wT_sb[:], pUW[:, 128:192], reads=["pUW"], writes=["wT"], eng="scalar")
            if stage == 4:
                return
            P.mm(pV[0:64, 0:128], wT_sb[:], Sc[:], reads=["wT", Scn], writes=["pV"])
            P.tt(vn_sb[:], u_sb[:], pV[0:64, 0:128], ALU.subtract, reads=["u", "pV"], writes=["vn"])
            if stage == 5:
                return
            P.mm(pO[:, 0:64], Sc[:], qg[i][:, cs], start=True, stop=False, reads=[Scn, f"qg{i}"], writes=["pO"])
            P.mm(pO[:, 0:64], vn_sb[:], mm2[:, 64:128], start=False, stop=True, reads=["vn", m2n], writes=["pO"])
            P.cp(y[:, cs], pO[:, 0:64], reads=["pO"], writes=[(yn, c)], eng="scalar")
            if stage == 6:
                return
            P.mm(pV[:, 128:256], KS[i][:, c, :], vn_sb[:], reads=[f"KS{i}", "vn"], writes=["pV"])
            el = egc[i][:, c * 64 + 63:c * 64 + 64]
            P.ts(tmpS[:], Sc[:], el, None, ALU.mult, reads=[Scn, f"egc{i}"], writes=["tmpS"])
            P.tt(Sn[:], pV[:, 128:256], tmpS[:], ALU.add, reads=["pV", "tmpS"], writes=[Snn])

    def post(ti):
        i = ti % 2
        t0 = ti * TT
        y, yn = y_sb[i], f"y{i}"
        P.act(sq_sb[:], y[:], AF.Square, reads=[yn], writes=["sq"])
        P.mm(pA[:, 0:TT], o128[:], sq_sb[:], reads=["o128", "sq"], writes=["pA"])
        P.act(rn_sb[:], pA[:, 0:TT], AF.Ln, reads=["pA", "cb"], writes=["rn"], bias=cb[:, 0:1], scale=1.0 / 128)
        P.act(rn_sb[:], rn_sb[:], AF.Exp, reads=["rn"], writes=["rn"], scale=-0.5)
        P.stt(y[:], y[:], col(14), rn_sb[:], ALU.mult, ALU.mult, reads=[yn, "prm", "rn"], writes=[yn])
        P.tt(y[:], y[:], sg[i][:], ALU.mult, reads=[yn, f"sg{i}"], writes=[yn])
        P.dma(yT[:, t0:t0 + TT], y[:], reads=[yn], is_output=True)

    if stage < 9:
        if stage >= 1:
            prep(0)
        if stage >= 2:
            chunks(0)
        P.memset(u_sb[:], 0.0, writes=["u"])
        P.dma(yT[0:64, 0:128], u_sb[:], reads=["u"], is_output=True)
        return P.finish()
    prep(0)
    if stage == 12:
        prep(1)
        P.memset(u_sb[:], 0.0, writes=["u"])
        P.dma(yT[0:64, 0:128], u_sb[:], reads=["u"], is_output=True)
        return P.finish()
    for ti in range(ntile):
        if ti + 1 < ntile:
            prep(ti + 1)
        chunks(ti)
        post(ti)
        if stage == 10:
            break
    return P.finish()


def build_sgu(T):
    P = Prog()
    NB = T // 128
    TB = 4
    ntile = NB // TB
    uT = P.dram("uT", [128, T])
    vtok = P.dram("vtok", [128, NB, 128])
    lnp = P.dram("lnp", [1, 256])
    wT = P.dram("wT", [2, 128, 128])
    bs = P.dram("bs", [2, 128])
    yT = P.dram("yT", [128, T], F32, kind="ExternalOutput")
    lnp_sb = P.sb("lnp_sb", [128, 256], F32)
    P.dma(lnp_sb[:], lnp.to_broadcast([128, 256]), writes=["lnp"])
    w_sb = P.sb("w_sb", [128, 2, 128], F32)
    for g in range(2):
        P.dma(w_sb[:, g, :], wT[g], writes=["w"])
    P.memset(w_sb[64:128, :, 0:64], 0.0, writes=["w"])
    b_sb = P.sb("b_sb", [128, 128], F32)
    for g in range(2):
        P.dma(b_sb[64 * g:64 * g + 64, :], bs[g:g + 1, :].to_broadcast([64, 128]), writes=["b"])
    cb = P.sb("cb", [128, 1], F32)
    P.memset(cb[:], 1e-5, writes=["cb"])
    v_sb = [P.sb(f"v_sb{i}", [128, TB, 128], F32) for i in range(2)]
    sq_sb = P.sb("sq_sb", [128, TB, 128], F32)
    st = P.sb("st", [128, 4, TB * 2], F32)
    u_sb = [P.sb(f"u_sb{i}", [128, TB * 128], F32) for i in range(2)]
    y_sb = [P.sb(f"y_sb{i}", [128, TB * 128], F32) for i in range(2)]
    po = [P.ps(f"po{g}", [128, 512], F32) for g in range(2)]
    for ti in range(ntile):
        i = ti % 2
        n0 = ti * TB
        v, vn = v_sb[i], f"v{i}"
        u, un = u_sb[i], f"u{i}"
        y, yn = y_sb[i], f"y{i}"
        P.dma(v[:], vtok[:, n0:n0 + TB, :], writes=[vn])
        P.dma(u[:], uT[:, n0 * 128:(n0 + TB) * 128], writes=[un])
        P.act(v[:], v[:], AF.Gelu, reads=[vn], writes=[vn])
        P.act(u[:], u[:], AF.Gelu, reads=[un], writes=[un])
        v3 = v[:].rearrange("p n (g c) -> p (n g) c", c=64)
        s3 = sq_sb[:].rearrange("p n (g c) -> p (n g) c", c=64)
        P.op("vector", "tensor_reduce", reads=[vn], writes=[("st", 0)], out=st[:, 0, :], in_=v3, axis=AX.X, op=ALU.add)
        P.ts(st[:, 0, :], st[:, 0, :], 1.0 / 64, None, ALU.mult, reads=[("st", 0)], writes=[("st", 0)])
        P.tt(v3, v3, st[:, 0, :].unsqueeze(2).to_broadcast([128, TB * 2, 64]), ALU.subtract, reads=[vn, ("st", 0)], writes=[vn])
        P.act(sq_sb[:], v[:], AF.Square, reads=[vn], writes=["sq"])
        P.op("vector", "tensor_reduce", reads=["sq"], writes=[("st", 1)], out=st[:, 1, :], in_=s3, axis=AX.X, op=ALU.add)
        P.act(st[:, 2, :], st[:, 1, :], AF.Ln, reads=[("st", 1), "cb"], writes=[("st", 2)], bias=cb[:, 0:1], scale=1.0 / 64)
        P.act(st[:, 2, :], st[:, 2, :], AF.Exp, reads=[("st", 2)], writes=[("st", 2)], scale=-0.5)
        P.tt(v3, v3, st[:, 2, :].unsqueeze(2).to_broadcast([128, TB * 2, 64]), ALU.mult, reads=[vn, ("st", 2)], writes=[vn])
        P.tt(v[:], v[:], lnp_sb[:, 0:128].unsqueeze(1).to_broadcast([128, TB, 128]), ALU.mult, reads=[vn, "lnp"], writes=[vn])
        P.tt(v[:], v[:], lnp_sb[:, 128:256].unsqueeze(1).to_broadcast([128, TB, 128]), ALU.add, reads=[vn, "lnp"], writes=[vn])
        for g in range(2):
            for n in range(TB):
                P.mm(po[g][:, n * 128:(n + 1) * 128], v[:, n, :], w_sb[:, g, :], reads=[vn, "w"], writes=[f"po{g}"])
            pp = slice(64 * g, 64 * g + 64)
            P.tt(y[pp, :].rearrange("p (n i) -> p n i", i=128), po[g][pp, :].rearrange("p (n i) -> p n i", i=128),
                 b_sb[pp, :].unsqueeze(1).to_broadcast([64, TB, 128]), ALU.add, reads=[f"po{g}", "b"], writes=[(yn, g)])
            P.tt(y[pp, :], y[pp, :], u[pp, :], ALU.mult, reads=[(yn, g), un], writes=[(yn, g)])
        P.dma(yT[:, n0 * 128:(n0 + TB) * 128], y[:], reads=[yn], is_output=True)
    return P.finish()


def prep_hgrn(zTb, hp, lb_logits, norm_g):
    T = zTb.shape[1]
    rows = [zTb[q * 512 + 128 * hp:q * 512 + 128 * hp + 128] for q in range(4)]
    zin = np.ascontiguousarray(np.concatenate(rows, 0))
    iT = rows[2]
    vtok = iT.reshape(2, 64, T // 64, 64).transpose(0, 3, 2, 1)
    prm = np.zeros((128, 4), np.float32)
    prm[:, 0] = lb_logits[0, 128 * hp:128 * hp + 128]
    prm[:, 1] = lb_logits[1, 128 * hp:128 * hp + 128]
    prm[:, 2] = norm_g[128 * hp:128 * hp + 128]
    return {"zin": zin, "vtok": np.ascontiguousarray(vtok), "prm": prm}


def prep_rwkv(zTb, hp, e, prm_in, vfirstT=None):
    T = zTb.shape[1]
    f = slice(128 * hp, 128 * hp + 128)
    zin = np.ascontiguousarray(np.concatenate([zTb[q * 512 + 128 * hp:q * 512 + 128 * hp + 128] for q in range(3)], 0))
    lrin = np.ascontiguousarray(zTb[1536:1696])
    mu = prm_in["rwkv_mu"][e]
    prm = np.zeros((128, 16), np.float32)
    prm[:, 0] = mu[0:512][f]; prm[:, 1] = mu[512:1024][f]; prm[:, 2] = mu[1024:1536][f]
    prm[:, 3] = prm_in["rwkv_w0"][e][f]; prm[:, 4] = prm_in["rwkv_a0"][e][f]
    prm[:, 5] = prm_in["rwkv_k_k"][e][f]; prm[:, 6] = prm_in["rwkv_k_a"][e][f]
    prm[:, 7] = prm_in["rwkv_r_k"][e].reshape(512)[f]
    prm[:, 8] = prm_in["rwkv_ln_g"][e][f]; prm[:, 9] = prm_in["rwkv_ln_b"][e][f]
    prm2 = np.zeros((96, 4), np.float32)
    prm2[0:32, 0] = mu[1536:1568]; prm2[0:32, 1] = mu[1568:1600]; prm2[0:96, 2] = mu[1600:1696]
    wlr = np.zeros((96, 4, 128), np.float32)
    wlr[0:32, 0] = prm_in["rwkv_w_up"][e][:, f]; wlr[0:32, 1] = prm_in["rwkv_a_up"][e][:, f]; wlr[0:96, 2] = prm_in["rwkv_g_up"][e][:, f]
    d = {"zin": zin, "lrin": lrin, "prm": prm, "prm2": prm2, "wlr": wlr}
    if e > 0:
        prm[:, 10] = prm_in["rwkv_v0"][e - 1][f]
        wlr[0:32, 3] = prm_in["rwkv_vres_up"][e - 1][:, f]
        d["vlr"] = np.ascontiguousarray(zTb[2720:2752])
        d["vfirst"] = np.ascontiguousarray(vfirstT)
    return d


def prep_gdn(zTb, hd, o, prm_in):
    T = zTb.shape[1]
    base = 2048
    rows = [zTb[base + q * 512 + 128 * hd:base + q * 512 + 128 * hd + 128] for q in range(4)]
    zin = np.ascontiguousarray(np.concatenate(rows, 0))
    brow = zTb[base + 2048 + hd]
    arow = zTb[base + 2052 + hd]
    abrow = np.ascontiguousarray(np.stack([brow, arow], 0))
    abcol = np.ascontiguousarray(np.stack([brow.reshape(T // 64, 64).T, arow.reshape(T // 64, 64).T], 1))
    prm = np.zeros((128, 16), np.float32)
    cw = prm_in["gdn_conv_w"][o]
    for q in range(3):
        prm[:, 4 * q:4 * q + 4] = cw[:, q * 512 + 128 * hd:q * 512 + 128 * hd + 128].T
    prm[:, 12] = prm_in["gdn_a_log"][o][hd]
    prm[:, 13] = prm_in["gdn_dt_bias"][o][hd]
    prm[:, 14] = prm_in["gdn_norm_g"][o]
    return {"zin": zin, "abrow": abrow, "abcol": abcol, "prm": prm}


def prep_sgu(zTb, gp, e, prm_in):
    T = zTb.shape[1]
    base = 1696
    uT = np.ascontiguousarray(zTb[base + 128 * gp:base + 128 * gp + 128])
    vT = zTb[base + 512 + 128 * gp:base + 512 + 128 * gp + 128]
    vtok = np.ascontiguousarray(vT.reshape(128, T // 128, 128).transpose(2, 1, 0))
    f = slice(128 * gp, 128 * gp + 128)
    lnp = np.concatenate([prm_in["sgu_ln_g"][e][f], prm_in["sgu_ln_b"][e][f]])[None, :].astype(np.float32)
    w = prm_in["sgu_w"][e][2 * gp:2 * gp + 2]
    wT = np.ascontiguousarray(w.transpose(0, 2, 1))
    bs = np.ascontiguousarray(prm_in["sgu_b"][e][2 * gp:2 * gp + 2])
    return {"uT": uT, "vtok": vtok, "lnp": np.ascontiguousarray(lnp), "wT": wT, "bs": bs}


_PROGS = {}


def _prog(key, fn):
    if key not in _PROGS:
        _PROGS[key] = fn()
    return _PROGS[key]


def _run(nc, in_maps):
    res = run_bass_kernel_spmd(nc, in_maps, core_ids=list(range(8)))
    return res.results


NTOK = 2048


def _dense_inputs(xTb, yTb, l, p, do_pre, w_in_next, g_pre_next):
    w_o = p["ev_w_out"][l // 2] if l % 2 == 0 else p["od_w_out"][l // 2]
    gvec = np.ascontiguousarray(np.concatenate([colvec(p["norm_mix_post"][l]), colvec(p["norm_ffn_pre"][l]), colvec(p["norm_ffn_post"][l])], 1))
    cw = np.concatenate([p["ffn_conv_w"][l].T, p["ffn_conv_b"][l][:, None]], 1).reshape(NFC, 128, 4).transpose(1, 0, 2)
    cw = np.ascontiguousarray(cw)
    maps = []
    for b in range(2):
        for q in range(4):
            lo = q * NTOK
            if q == 0:
                xs = np.concatenate([np.zeros((D, 2), np.float32), xTb[b][:, 0:NTOK]], 1)
                ys = np.concatenate([np.zeros((D, 2), np.float32), yTb[b][:, 0:NTOK]], 1)
            else:
                xs = xTb[b][:, lo - 2:lo + NTOK]
                ys = yTb[b][:, lo - 2:lo + NTOK]
            m = {"xT": np.ascontiguousarray(xs), "yT": np.ascontiguousarray(ys), "w_o": w_o, "w_f1": p["ffn_w_in"][l], "w_f2": p["ffn_w_out"][l],
                 "gvec": gvec, "cw": cw, "hmask": np.full((128, 1), 0.0 if q == 0 else 1.0, np.float32)}
            if do_pre:
                m["gpre"] = g_pre_next
                m["w_in"] = w_in_next
            maps.append(m)
    return maps


def _w_in(l, p):
    if l % 2 == 1:
        return p["od_w_in"][l // 2]
    e = l // 2
    if e == 0:
        return p["ev_w_in"][0]
    return np.ascontiguousarray(np.concatenate([p["ev_w_in"][e], p["rwkv_vres_down"][e - 1]], 1))


def kernel(**inputs):
    p = {k: np.ascontiguousarray(np.asarray(v, dtype=np.float32)) for k, v in inputs.items()}
    x = p["x"]
    B, T, _ = x.shape
    xTb = [np.ascontiguousarray(x[b].T) for b in range(B)]
    w_in0 = _w_in(0, p)
    nc = _prog(("dense", False, True, w_in0.shape[1]), lambda: build_dense(NTOK, False, True, w_in0.shape[1]))
    maps = [{"xT": np.ascontiguousarray(xTb[b][:, q * NTOK:(q + 1) * NTOK]), "gpre": colvec(p["norm_mix_pre"][0]), "w_in": w_in0}
            for b in range(B) for q in range(4)]
    res = _run(nc, maps)
    zTb = [np.concatenate([res[4 * b + q]["zT"] for q in range(4)], 1) for b in range(B)]
    vfirst = None
    for l in range(4):
        yTb = [np.empty((D, T), np.float32) for _ in range(B)]
        if l % 2 == 0:
            e = l // 2
            nc = _prog(("rwkv", e > 0), lambda: build_rwkv(T, e > 0))
            maps = [prep_rwkv(zTb[b], hp, e, p, None if vfirst is None else vfirst[b][128 * hp:128 * hp + 128]) for b in range(B) for hp in range(4)]
            res = _run(nc, maps)
            for b in range(B):
                for hp in range(4):
                    yTb[b][128 * hp:128 * hp + 128] = res[4 * b + hp]["yT"]
            if e == 0:
                vfirst = [np.concatenate([res[4 * b + hp]["vout"] for hp in range(4)], 0) for b in range(B)]
            nc = _prog(("sgu",), lambda: build_sgu(T))
            maps = [prep_sgu(zTb[b], gp, e, p) for b in range(B) for gp in range(4)]
            res = _run(nc, maps)
            for b in range(B):
                for gp in range(4):
                    yTb[b][512 + 128 * gp:512 + 128 * gp + 128] = res[4 * b + gp]["yT"]
        else:
            o = l // 2
            nc = _prog(("hgrn", o), lambda: build_hgrn(T, o))
            maps = [prep_hgrn(zTb[b], hp, p["hgrn_lb_logits"], p["hgrn_norm_g"][o]) for b in range(B) for hp in range(4)]
            res = _run(nc, maps)
            for b in range(B):
                for hp in range(4):
                    yTb[b][128 * hp:128 * hp + 128] = res[4 * b + hp]["yT"]
            nc = _prog(("gdn",), lambda: build_gdn(T))
            maps = [prep_gdn(zTb[b], hd, o, p) for b in range(B) for hd in range(4)]
            res = _run(nc, maps)
            for b in range(B):
                for hd in range(4):
                    yTb[b][512 + 128 * hd:512 + 128 * hd + 128] = res[4 * b + hd]["yT"]
        do_pre = l < 3
        w_next = _w_in(l + 1, p) if do_pre else None
        C = w_next.shape[1] if do_pre else 0
        nc = _prog(("dense", True, do_pre, C), lambda: build_dense(NTOK, True, do_pre, C))
        maps = _dense_inputs(xTb, yTb, l, p, do_pre, w_next, colvec(p["norm_mix_pre"][l + 1]) if do_pre else None)
        res = _run(nc, maps)
        xTb = [np.concatenate([res[4 * b + q]["xoT"] for q in range(4)], 1) for b in range(B)]
        if do_pre:
            zTb = [np.concatenate([res[4 * b + q]["zT"] for q in range(4)], 1) for b in range(B)]
    return np.ascontiguousarray(np.stack([xTb[b].T for b in range(B)], 0)).astype(np.float32)
```

```python
import numpy as np
from contextlib import ExitStack
import concourse.bass as bass
import concourse.mybir as mybir
from concourse.bass_utils import run_bass_kernel_spmd

F32 = mybir.dt.float32
BF16 = mybir.dt.bfloat16
AF = mybir.ActivationFunctionType
ALU = mybir.AluOpType
AX = mybir.AxisListType

COMPUTE = ("tensor", "vector", "scalar", "gpsimd")
NPOOL = 24
SKIP_SELF = ("tensor",)


class Prog:
    def __init__(self):
        self.nc = bass.Bass("TRN2", target_bir_lowering=False)
        self.es = ExitStack()
        self.ops = {e: [] for e in COMPUTE + ("sync",)}
        self.cnt = {e: 0 for e in COMPUTE}
        self.sem = {}
        for e in COMPUTE:
            self.sem[e] = self.nc.alloc_semaphore("s_" + e)
        self.dpool = {q: [self.nc.alloc_semaphore(f"d_{q}_{i}") for i in range(NPOOL)] for q in ("sync", "gpsimd")}
        self.dcnt = {"sync": 0, "gpsimd": 0}
        self.known = {e: {} for e in COMPUTE + ("sync",)}
        self.lastw = {}
        self.readers = {}
        self.out_tokens = []
        self.nuniq = 0

    def sb(self, name, shape, dtype=F32):
        return self.es.enter_context(self.nc.sbuf_tensor(name, list(shape), dtype))

    def ps(self, name, shape, dtype=F32):
        return self.es.enter_context(self.nc.psum_tensor(name, list(shape), dtype))

    def dram(self, name, shape, dtype=F32, kind="ExternalInput"):
        return self.nc.dram_tensor(name, list(shape), dtype, kind=kind).ap()

    @staticmethod
    def _key(r):
        if isinstance(r, tuple):
            return r[0], r[1]
        return r, None

    def _conf(self, table, name, sub):
        d = table.get(name, {})
        if sub is None:
            return list(d.values())
        out = []
        if sub in d:
            out.append(d[sub])
        if None in d:
            out.append(d[None])
        return out

    def _deps(self, reads, writes):
        toks = []
        for r in reads:
            n, s = self._key(r)
            toks += self._conf(self.lastw, n, s)
        for w in writes:
            n, s = self._key(w)
            toks += self._conf(self.lastw, n, s)
            for lst in self._conf(self.readers, n, s):
                toks += lst
        return toks

    def _commit(self, reads, writes, tok):
        for r in reads:
            n, s = self._key(r)
            self.readers.setdefault(n, {}).setdefault(s, []).append(tok)
        for w in writes:
            n, s = self._key(w)
            if s is None:
                self.lastw[n] = {None: tok}
                self.readers[n] = {}
            else:
                self.lastw.setdefault(n, {})[s] = tok
                self.readers.setdefault(n, {})[s] = []

    def _waits(self, eng, toks, skip_self=False):
        need = {}
        for (sname, sem, val, src) in toks:
            if skip_self and src == eng:
                continue
            if val > need.get(sname, (None, 0))[1]:
                need[sname] = (sem, val)
        out = []
        kn = self.known[eng]
        for sname, (sem, val) in need.items():
            if kn.get(sname, 0) >= val:
                continue
            kn[sname] = val
            out.append((sem, val))
        return out

    def op(self, eng, meth, reads=(), writes=(), **kw):
        toks = self._deps(reads, writes)
        waits = self._waits(eng, toks, skip_self=(eng in SKIP_SELF))
        self.cnt[eng] += 1
        tok = ("s_" + eng, self.sem[eng], self.cnt[eng], eng)

        def fn(e, meth=meth, kw=kw):
            return getattr(e, meth)(**kw)

        self.ops[eng].append((waits, fn, (self.sem[eng], 1)))
        self._commit(reads, writes, tok)
        return tok

    def mm(self, out, lhsT, rhs, start=True, stop=True, reads=(), writes=()):
        return self.op("tensor", "matmul", reads, writes, out=out, lhsT=lhsT, rhs=rhs, start=start, stop=stop)

    def tr(self, out, in_, identity, reads=(), writes=()):
        return self.op("tensor", "transpose", reads, writes, out=out, in_=in_, identity=identity)

    def act(self, out, in_, func, reads=(), writes=(), **kw):
        return self.op("scalar", "activation", reads, writes, out=out, in_=in_, func=func, **kw)

    def tt(self, out, in0, in1, op, reads=(), writes=(), eng="vector"):
        return self.op(eng, "tensor_tensor", reads, writes, out=out, in0=in0, in1=in1, op=op)

    def stt(self, out, in0, scalar, in1, op0, op1, reads=(), writes=()):
        return self.op("vector", "scalar_tensor_tensor", reads, writes, out=out, in0=in0, scalar=scalar, in1=in1, op0=op0, op1=op1)

    def ts(self, out, in0, scalar1, scalar2, op0, op1=None, reads=(), writes=(), eng="vector"):
        kw = dict(out=out, in0=in0, scalar1=scalar1, scalar2=scalar2, op0=op0)
        if op1 is not None:
            kw["op1"] = op1
        return self.op(eng, "tensor_scalar", reads, writes, **kw)

    def cp(self, out, in_, reads=(), writes=(), eng="vector"):
        if eng == "scalar":
            return self.op("scalar", "copy", reads, writes, out=out, in_=in_)
        return self.op(eng, "tensor_copy", reads, writes, out=out, in_=in_)

    def memset(self, ap, val, writes=(), eng="vector"):
        return self.op(eng, "memset", (), writes, ap=ap, constant=val)

    def dma(self, out, in_, reads=(), writes=(), q="sync", is_output=False, **kw):
        toks = self._deps(reads, writes)
        j = self.dcnt[q]
        self.dcnt[q] += 1
        slot, rnd = j % NPOOL, j // NPOOL
        sem = self.dpool[q][slot]
        sname = f"d_{q}_{slot}"
        if rnd > 0:
            toks = toks + [(sname, sem, 16 * rnd, "dma")]
        waits = self._waits(q, toks)
        tok = (sname, sem, 16 * (rnd + 1), "dma")

        def fn(e, out=out, in_=in_, kw=kw):
            return e.dma_start(out=out, in_=in_, **kw)

        self.ops[q].append((waits, fn, (sem, 16)))
        self._commit(reads, writes, tok)
        if is_output:
            self.out_tokens.append(tok)
        return tok

    def finish(self):
        nc = self.nc
        final = list(self.out_tokens)
        for e in COMPUTE:
            if self.cnt[e]:
                final.append(("s_" + e, self.sem[e], self.cnt[e], e))
        fw = self._waits("sync", final)
        ops = self.ops
        with nc.Block() as block:
            def emit(e, lst, extra=()):
                for waits, fn, (sem, inc) in lst:
                    for (ws, wv) in waits:
                        e.wait_ge(ws, wv)
                    fn(e).then_inc(sem, inc)
                for (ws, wv) in extra:
                    e.wait_ge(ws, wv)

            @block.sync
            def _(e):
                emit(e, ops["sync"], fw)

            @block.tensor
            def _(e):
                emit(e, ops["tensor"])

            @block.vector
            def _(e):
                emit(e, ops["vector"])

            @block.scalar
            def _(e):
                emit(e, ops["scalar"])

            @block.gpsimd
            def _(e):
                emit(e, ops["gpsimd"])
        self.es.close()
        return nc


D = 1024
DFF = 2816
NFC = DFF // 128
EPS = 1e-6


class Ring:
    def __init__(self, P, name, n, shape, dtype):
        self.tiles = [P.sb(f"{name}{i}", shape, dtype) for i in range(n)]
        self.names = [f"{name}{i}" for i in range(n)]
        self.i = 0

    def next(self):
        k = self.i % len(self.tiles)
        self.i += 1
        return self.tiles[k], self.names[k]


class PsRing:
    def __init__(self, P, name, n, width=512):
        self.tiles = [P.ps(f"{name}{i}", [128, width], F32) for i in range(n)]
        self.names = [f"{name}{i}" for i in range(n)]
        self.i = 0

    def next(self):
        k = self.i % len(self.tiles)
        self.i += 1
        return self.tiles[k], self.names[k]


def colvec(a):
    a = np.asarray(a, np.float32)
    return np.ascontiguousarray(a.reshape(-1, 128).T)


def build_dense(NT, do_post, do_pre, C):
    P = Prog()
    TM = min(1024, NT)
    nmt = NT // TM
    HAL = 2 if do_post else 0
    W = HAL + NT
    TW = TM + 2
    xT = P.dram("xT", [D, W])
    if do_post:
        yT = P.dram("yT", [D, W])
        w_o = P.dram("w_o", [D, D])
        w_f1 = P.dram("w_f1", [D, 2 * DFF])
        w_f2 = P.dram("w_f2", [DFF, D])
        gvec = P.dram("gvec", [128, 24])
        cw = P.dram("cw", [128, NFC, 4])
        hmask = P.dram("hmask", [128, 1])
        xoT = P.dram("xoT", [D, NT], F32, kind="ExternalOutput")
    if do_pre:
        gpre = P.dram("gpre", [128, 8])
        w_in = P.dram("w_in", [D, C])
        zT = P.dram("zT", [C, NT], F32, kind="ExternalOutput")

    x_sb = P.sb("x_sb", [128, 8, TW], F32)
    h_sb = P.sb("h_sb", [128, 8, TW], BF16)
    ring = Ring(P, "wr", 2, [128, NFC, 512], BF16)
    sq = P.sb("sq", [128, 8, 512], BF16)
    rstd = [P.sb(f"rstd{i}", [128, 512], F32) for i in range(2)]
    ones = P.sb("ones", [128, 128], BF16)
    P.memset(ones[:], 1.0 / D, writes=["ones"])
    pr = PsRing(P, "pp", 6)
    pn = P.ps("pn", [128, 512], F32)
    nrs = [0]
    if do_post:
        f8 = P.sb("f8", [128, 8, TW], F32)
        act = P.sb("act", [128, NFC * TM], BF16)
        act3 = act[:, :].rearrange("p (j t) -> p j t", t=TM)
        y_sb = act[:, 0:8 * TW].rearrange("p (k t) -> p k t", t=TW)
        gb = [P.sb(f"gb{i}", [128, 2 + 512], F32) for i in range(2)]
        cv = [P.sb(f"cv{i}", [128, 512], F32) for i in range(2)]
        ghalo = P.sb("ghalo", [128, NFC, 2], F32)
        gv_sb = P.sb("gv_sb", [128, 24], F32)
        cw_sb = P.sb("cw_sb", [128, NFC, 4], F32)
        hm_sb = P.sb("hm_sb", [128, 1], F32)
        P.dma(gv_sb[:], gvec, writes=["gv_sb"])
        P.dma(cw_sb[:], cw, writes=["cw_sb"])
        P.dma(hm_sb[:], hmask, writes=["hm_sb"])
    if do_pre:
        gp_sb = P.sb("gp_sb", [128, 8], F32)
        P.dma(gp_sb[:], gpre, writes=["gp_sb"])
        zst = Ring(P, "zst", 3, [128, 512], F32)

    xv = xT.rearrange("(k p) t -> p k t", p=128)
    if do_post:
        yv = yT.rearrange("(k p) t -> p k t", p=128)
        xov = xoT.rearrange("(k p) t -> p k t", p=128)

    def rms_rstd(src, sname, lo, hi):
        n = hi - lo
        for m in range(8):
            P.act(sq[:, m, 0:n], src[:, m, lo:hi], AF.Square, reads=[(sname, (m, lo))], writes=[("sq", m)])
        for m in range(8):
            P.mm(pn[:, 0:n], ones[:], sq[:, m, 0:n], start=(m == 0), stop=(m == 7), reads=["ones", ("sq", m)], writes=["pn"])
        r = rstd[nrs[0] % 2]
        rn = f"rstd{nrs[0] % 2}"
        nrs[0] += 1
        P.act(r[:, 0:n], pn[:, 0:n], AF.Ln, reads=["pn"], writes=[rn], bias=EPSB[0][:, 0:1], scale=1.0)
        P.act(r[:, 0:n], r[:, 0:n], AF.Exp, reads=[rn], writes=[rn], scale=-0.5)
        return r, rn

    epsb = P.sb("epsb", [128, 1], F32)
    P.memset(epsb[:], EPS, writes=["epsb"])
    EPSB = [epsb]

    def linear(Wd, kcn, M, in_tile, in_name, subs, consume):
        Wv = Wd.rearrange("(kc p) m -> p kc m", p=128)
        for blk in range(0, M, 512):
            bw = min(512, M - blk)
            wt, wn = ring.next()
            P.dma(wt[:, 0:kcn, 0:bw], Wv[:, :, blk:blk + bw], writes=[wn], q="gpsimd")
            for m0 in range(0, bw, 128):
                mw = min(128, bw - m0)
                for (lo, hi) in subs:
                    n = hi - lo
                    ps, psn = pr.next()
                    for kc in range(kcn):
                        P.mm(ps[0:mw, 0:n], wt[:, kc, m0:m0 + mw], in_tile[:, kc, lo:hi], start=(kc == 0), stop=(kc == kcn - 1),
                             reads=[wn, (in_name, (kc, lo))], writes=[psn])
                    consume((blk + m0) // 128, mw, lo, hi, ps, psn)

    for mt in range(nmt):
        hoff = HAL if mt == 0 else 0
        c0 = 0 if mt == 0 else HAL + mt * TM
        nsub = TM // 512
        if hoff:
            subs = [(0, 2)] + [(2 + i * 512, 2 + (i + 1) * 512) for i in range(nsub)]
        else:
            subs = [(i * 512, (i + 1) * 512) for i in range(nsub)]
        real = [s for s in subs if s[1] - s[0] > 2]

        for (lo, hi) in subs:
            P.dma(x_sb[:, :, lo:hi], xv[:, :, c0 + lo:c0 + hi], writes=[("x", (m, lo)) for m in range(8)])
        if do_post:
            for (lo, hi) in subs:
                P.dma(y_sb[:, :, lo:hi], yv[:, :, c0 + lo:c0 + hi], writes=["a"] + [("y", (m, lo)) for m in range(8)], q="gpsimd")

            def c_mix(m, mw, lo, hi, ps, psn):
                P.cp(f8[:, m, lo:hi], ps[:, 0:hi - lo], reads=[psn], writes=[("f8", (m, lo))], eng="scalar")
            linear(w_o, 8, D, y_sb, "y", subs, c_mix)
            for (lo, hi) in subs:
                n = hi - lo
                r, rn = rms_rstd(f8, "f8", lo, hi)
                for m in range(8):
                    P.stt(f8[:, m, lo:hi], f8[:, m, lo:hi], gv_sb[:, m:m + 1], r[:, 0:n], ALU.mult, ALU.mult,
                          reads=[("f8", (m, lo)), rn, "gv_sb"], writes=[("f8", (m, lo))])
                    P.tt(x_sb[:, m, lo:hi], x_sb[:, m, lo:hi], f8[:, m, lo:hi], ALU.add,
                         reads=[("f8", (m, lo)), ("x", (m, lo))], writes=[("x", (m, lo))])
                r, rn = rms_rstd(x_sb, "x", lo, hi)
                for m in range(8):
                    P.stt(h_sb[:, m, lo:hi], x_sb[:, m, lo:hi], gv_sb[:, 8 + m:9 + m], r[:, 0:n], ALU.mult, ALU.mult,
                          reads=[("x", (m, lo)), rn, "gv_sb"], writes=[("h", (m, lo))])

            Wv1 = w_f1.rearrange("(kc p) m -> p kc m", p=128)
            first_act = True
            ngb = 0
            for blk in range(0, DFF, 512):
                bw = min(512, DFF - blk)
                wt, wn = ring.next()
                P.dma(wt[:, 0:8, 0:bw], Wv1[:, :, blk:blk + bw], writes=[wn], q="gpsimd")
                P.dma(wt[:, 8:16, 0:bw], Wv1[:, :, DFF + blk:DFF + blk + bw], writes=[wn], q="gpsimd")
                for m0 in range(0, bw, 128):
                    j = (blk + m0) // 128
                    for (lo, hi) in subs:
                        n = hi - lo
                        pg, pgn = pr.next()
                        for kc in range(8):
                            P.mm(pg[:, 0:n], wt[:, kc, m0:m0 + 128], h_sb[:, kc, lo:hi], start=(kc == 0), stop=(kc == 7),
                                 reads=[wn, ("h", (kc, lo))], writes=[pgn])
                        if n == 2:
                            P.ts(ghalo[:, j, :], pg[:, 0:2], hm_sb[:, 0:1], None, ALU.mult, reads=[pgn, "hm_sb"], writes=[("ghalo", j)])
                            continue
                        pu, pun = pr.next()
                        for kc in range(8):
                            P.mm(pu[:, 0:n], wt[:, 8 + kc, m0:m0 + 128], h_sb[:, kc, lo:hi], start=(kc == 0), stop=(kc == 7),
                                 reads=[wn, ("h", (kc, lo))], writes=[pun])
                        g_t, gname = gb[ngb % 2], f"gb{ngb % 2}"
                        c_t, cname = cv[ngb % 2], f"cv{ngb % 2}"
                        ngb += 1
                        P.cp(g_t[:, 0:2], ghalo[:, j, :], reads=[("ghalo", j)], writes=[(gname, 0)], eng="gpsimd")
                        P.cp(g_t[:, 2:2 + n], pg[:, 0:n], reads=[pgn], writes=[(gname, 1)], eng="scalar")
                        P.act(c_t[:, 0:n], pg[:, 0:n], AF.Identity, reads=[pgn, "cw_sb"], writes=[cname], scale=cw_sb[:, j, 2:3], bias=cw_sb[:, j, 3:4])
                        P.cp(ghalo[:, j, :], g_t[:, n:n + 2], reads=[(gname, 1)], writes=[("ghalo", j)], eng="gpsimd")
                        P.stt(c_t[:, 0:n], g_t[:, 1:1 + n], cw_sb[:, j, 1:2], c_t[:, 0:n], ALU.mult, ALU.add, reads=[gname, cname, "cw_sb"], writes=[cname])
                        P.stt(c_t[:, 0:n], g_t[:, 0:n], cw_sb[:, j, 0:1], c_t[:, 0:n], ALU.mult, ALU.add, reads=[gname, cname, "cw_sb"], writes=[cname])
                        P.act(c_t[:, 0:n], c_t[:, 0:n], AF.Gelu_apprx_tanh, reads=[cname], writes=[cname])
                        tlo = lo - hoff
                        wr = [("a", (j, tlo))]
                        if first_act:
                            wr.append("y")
                            first_act = False
                        P.tt(act3[:, j, tlo:tlo + n], c_t[:, 0:n], pu[:, 0:n], ALU.mult, reads=[cname, pun], writes=wr)

            rsubs = [(lo - hoff, hi - hoff) for (lo, hi) in real]

            def c_ffo(m, mw, lo, hi, ps, psn):
                P.cp(f8[:, m, lo:hi], ps[:, 0:hi - lo], reads=[psn], writes=[("f8", (m, lo))], eng="scalar")
            linear(w_f2, NFC, D, act3, "a", rsubs, c_ffo)
            for (lo, hi) in rsubs:
                n = hi - lo
                r, rn = rms_rstd(f8, "f8", lo, hi)
                for m in range(8):
                    P.stt(f8[:, m, lo:hi], f8[:, m, lo:hi], gv_sb[:, 16 + m:17 + m], r[:, 0:n], ALU.mult, ALU.mult,
                          reads=[("f8", (m, lo)), rn, "gv_sb"], writes=[("f8", (m, lo))])
                    P.tt(x_sb[:, m, hoff + lo:hoff + hi], x_sb[:, m, hoff + lo:hoff + hi], f8[:, m, lo:hi], ALU.add,
                         reads=[("f8", (m, lo)), ("x", (m, hoff + lo))], writes=[("x", (m, hoff + lo))])
                P.dma(xov[:, :, mt * TM + lo:mt * TM + hi], x_sb[:, :, hoff + lo:hoff + hi], reads=[("x", (m, hoff + lo)) for m in range(8)], is_output=True)
        if do_pre:
            ph = hoff if do_post else 0
            for (lo, hi) in real:
                n = hi - lo
                r, rn = rms_rstd(x_sb, "x", lo, hi)
                for m in range(8):
                    P.stt(h_sb[:, m, lo:hi], x_sb[:, m, lo:hi], gp_sb[:, m:m + 1], r[:, 0:n], ALU.mult, ALU.mult,
                          reads=[("x", (m, lo)), rn, "gp_sb"], writes=[("h", (m, lo))])

            def c_z(m, mw, lo, hi, ps, psn):
                n = hi - lo
                zt, zn = zst.next()
                P.cp(zt[0:mw, 0:n], ps[0:mw, 0:n], reads=[psn], writes=[zn], eng="scalar")
                oc = mt * TM + lo - ph
                P.dma(zT[m * 128:m * 128 + mw, oc:oc + n], zt[0:mw, 0:n], reads=[zn], is_output=True)
            linear(w_in, 8, C, h_sb, "h", real, c_z)
    return P.finish()


def make_consts(P, need_strict=False):
    c = {}
    ident = P.sb("ident", [128, 128], F32)
    P.memset(ident[:], 1.0, writes=["ident"], eng="gpsimd")
    P.op("gpsimd", "affine_select", reads=["ident"], writes=["ident"], out=ident[:], in_=ident[:], pattern=[[1, 128]],
         compare_op=ALU.is_equal, fill=0.0, base=0, channel_multiplier=-1)
    c["ident"] = ident
    mi = P.sb("mask_i", [128, 128], F32)
    P.memset(mi[:], 1.0, writes=["mask_i"], eng="gpsimd")
    P.op("gpsimd", "affine_select", reads=["mask_i"], writes=["mask_i"], out=mi[:], in_=mi[:], pattern=[[1, 128]],
         compare_op=ALU.is_ge, fill=0.0, base=0, channel_multiplier=-1)
    P.memset(mi[0:64, 64:128], 0.0, writes=["mask_i"], eng="gpsimd")
    c["mask_i"] = mi
    if need_strict:
        ms = P.sb("mask_s", [128, 128], F32)
        P.memset(ms[:], 1.0, writes=["mask_s"], eng="gpsimd")
        P.op("gpsimd", "affine_select", reads=["mask_s"], writes=["mask_s"], out=ms[:], in_=ms[:], pattern=[[1, 128]],
             compare_op=ALU.is_gt, fill=0.0, base=0, channel_multiplier=-1)
        P.memset(ms[0:64, 64:128], 0.0, writes=["mask_s"], eng="gpsimd")
        c["mask_s"] = ms
    ob = P.sb("ones_bd", [128, 128], F32)
    P.memset(ob[:], 0.0, writes=["ones_bd"], eng="gpsimd")
    P.memset(ob[0:64, 0:64], 1.0 / 64, writes=["ones_bd"], eng="gpsimd")
    P.memset(ob[64:128, 64:128], 1.0 / 64, writes=["ones_bd"], eng="gpsimd")
    c["ones_bd"] = ob
    cm = P.sb("cmask", [128, 8, 64], F32)
    P.memset(cm[:], 1.0, writes=["cmask"], eng="gpsimd")
    P.memset(cm[:, :, 0:1], 0.0, writes=["cmask"], eng="gpsimd")
    c["cmask"] = cm
    return c


def zipper(a, b, ra=1, rb=1):
    da = db = False
    while not (da and db):
        for _ in range(ra):
            if not da:
                try:
                    next(a)
                except StopIteration:
                    da = True
        for _ in range(rb):
            if not db:
                try:
                    next(b)
                except StopIteration:
                    db = True


def _adv(gen, n):
    for _ in range(n):
        try:
            next(gen)
        except StopIteration:
            return True
    return False


def pipeline3(seq, genA, genB, prep_next, post, NCH):
    n = len(seq)
    gens = {}

    def start(g):
        if g < n:
            gens[g] = genA(seq[g][0], seq[g][1], g)

    start(0)
    while not _adv(gens[0], 1000):
        pass
    start(1)
    for g, (ti, c) in enumerate(seq):
        start(g + 2)
        b = genB(ti, c, g)
        a1 = gens.get(g + 1)
        a2 = gens.get(g + 2)
        d1 = a1 is None
        db = False
        while not (d1 and db):
            if not d1:
                d1 = _adv(a1, 2)
            if a2 is not None:
                if _adv(a2, 1):
                    a2 = None
            if not db:
                db = _adv(b, 1)
        gens.pop(g, None)
        if c == NCH - 1:
            post(ti)
            prep_next(ti)


def bd_write(P, dst, dname, src, sname, mul, mname, nch, extra_reads=()):
    for h in range(2):
        pp = slice(64 * h, 64 * h + 64)
        s3 = src[pp, 0:nch * 64].rearrange("p (c t) -> p c t", t=64)
        m3 = mul[pp, 0:nch * 64].rearrange("p (c t) -> p c t", t=64)
        P.tt(dst[pp, 0:nch, 64 * h:64 * h + 64], s3, m3, ALU.mult, reads=[sname, mname] + list(extra_reads), writes=[(dname, h)],
             eng=("vector" if h == 0 else "gpsimd"))


def build_hgrn(T, layer_o):
    P = Prog()
    TT = 512
    NCH = TT // 64
    ntile = T // TT
    zin = P.dram("zin", [512, T])
    vtok = P.dram("vtok", [2, 64, T // 64, 64])
    prm = P.dram("prm", [128, 4])
    yT = P.dram("yT", [128, T], F32, kind="ExternalOutput")
    C = make_consts(P)
    prm_sb = P.sb("prm_sb", [128, 4], F32)
    P.dma(prm_sb[:], prm, writes=["prm"])
    lbv = P.sb("lbv", [128, 2], F32)
    if layer_o == 0:
        P.memset(lbv[:, 0:1], 0.0, writes=["lbv"])
        P.memset(lbv[:, 1:2], 1.0, writes=["lbv"])
    else:
        P.tt(lbv[:, 0:1], prm_sb[:, 1:2], prm_sb[:, 0:1], ALU.subtract, reads=["prm"], writes=["lbv"])
        P.act(lbv[:, 0:1], lbv[:, 0:1], AF.Sigmoid, reads=["lbv"], writes=["lbv"])
        P.ts(lbv[:, 1:2], lbv[:, 0:1], -1.0, 1.0, ALU.mult, ALU.add, reads=["lbv"], writes=["lbv"])
    epsb = P.sb("epsb", [128, 1], F32)
    P.memset(epsb[:], EPS, writes=["epsb"])

    zv = zin.rearrange("(q p) t -> p q t", p=128)
    NB = 2
    z_sb = [P.sb(f"z_sb{i}", [128, 4, TT], F32) for i in range(NB)]
    f_sb = [P.sb(f"f_sb{i}", [128, TT], F32) for i in range(NB)]
    k_sb = [P.sb(f"k_sb{i}", [128, TT], F32) for i in range(NB)]
    b_sb = [P.sb(f"b_sb{i}", [128, TT], F32) for i in range(NB)]
    eb_sb = [P.sb(f"eb_sb{i}", [128, TT], F32) for i in range(NB)]
    enb_sb = [P.sb(f"enb_sb{i}", [128, TT], F32) for i in range(NB)]
    qF = [P.sb(f"qF{i}", [128, NCH, 128], F32) for i in range(NB)]
    kF = [P.sb(f"kF{i}", [128, NCH, 128], F32) for i in range(NB)]
    Vb = [P.sb(f"Vb{i}", [128, NCH, 128], F32) for i in range(NB)]
    y_sb = [P.sb(f"y_sb{i}", [128, TT], F32) for i in range(NB)]
    for i in range(NB):
        for t_, n_ in ((qF[i], f"qF{i}"), (kF[i], f"kF{i}"), (Vb[i], f"Vb{i}")):
            P.memset(t_[:], 0.0, writes=[n_], eng="gpsimd")
    Tst = [P.sb(f"Tst{i}", [128, 128], F32) for i in range(2)]
    P.memset(Tst[0][:], 0.0, writes=["Tst0"])
    tmpT = P.sb("tmpT", [128, 128], F32)
    MTm = [P.sb(f"MTm{i}", [128, 128], F32) for i in range(2)]
    kTok = [P.sb(f"kTok{i}", [128, 128], F32) for i in range(2)]
    p_mt = [P.ps(f"p_mt{i}", [128, 512], F32) for i in range(2)]
    p_tr = [P.ps(f"p_tr{i}", [128, 512], F32) for i in range(2)]
    p_su = P.ps("p_su", [128, 512], F32)
    p_o = [P.ps(f"p_o{i}", [128, 512], F32) for i in range(2)]
    p_n = P.ps("p_n", [128, 512], F32)
    nst = 0
    for ti in range(ntile):
        i = ti % NB
        t0 = ti * TT
        zs, zn = z_sb[i], f"z{i}"
        P.dma(zs[:], zv[:, :, t0:t0 + TT], writes=[zn])
        c0 = ti * NCH
        for h in range(2):
            P.dma(Vb[i][64 * h:64 * h + 64, :, 64 * h:64 * h + 64], vtok[h, :, c0:c0 + NCH, :], writes=[(f"Vb{i}", h)], q="gpsimd")
        f, fn = f_sb[i], f"f{i}"
        P.act(f[:], zs[:, 1, :], AF.Sigmoid, reads=[zn], writes=[fn])
        P.ts(f[:], f[:], lbv[:, 1:2], lbv[:, 0:1], ALU.mult, ALU.add, reads=[fn, "lbv"], writes=[fn])
        k, kn = k_sb[i], f"k{i}"
        P.ts(k[:], f[:], -1.0, 1.0, ALU.mult, ALU.add, reads=[fn], writes=[kn])
        P.act(f[:], f[:], AF.Ln, reads=[fn], writes=[fn])
        bb, bn = b_sb[i], f"b{i}"
        P.op("vector", "tensor_tensor_scan", reads=[fn, "cmask"], writes=[bn], out=bb[:], data0=C["cmask"][:].rearrange("p c t -> p (c t)"),
             data1=f[:], initial=0.0, op0=ALU.mult, op1=ALU.add)
        eb, ebn = eb_sb[i], f"eb{i}"
        enb, enbn = enb_sb[i], f"enb{i}"
        P.act(eb[:], bb[:], AF.Exp, reads=[bn], writes=[ebn])
        P.act(enb[:], bb[:], AF.Exp, reads=[bn], writes=[enbn], scale=-1.0)
        P.act(zs[:, 0, :], zs[:, 0, :], AF.Silu, reads=[zn], writes=[zn])
        P.act(zs[:, 3, :], zs[:, 3, :], AF.Silu, reads=[zn], writes=[zn])
        bd_write(P, qF[i], f"qF{i}", zs[:, 0, :], zn, eb, ebn, NCH)
        bd_write(P, kF[i], f"kF{i}", k, kn, enb, enbn, NCH)
        y, yn = y_sb[i], f"y{i}"
        for c in range(NCH):
            j = nst % 2
            Tc, Tcn = Tst[j], f"Tst{j}"
            Tn, Tnn = Tst[1 - j], f"Tst{1 - j}"
            pm, pmn = p_mt[nst % 2], f"p_mt{nst % 2}"
            po, pon = p_o[nst % 2], f"p_o{nst % 2}"
            mtm, mtn = MTm[nst % 2], f"MTm{nst % 2}"
            kt, ktn = kTok[nst % 2], f"kTok{nst % 2}"
            nst += 1
            ptr, ptrn = p_tr[(nst - 1) % 2], f"p_tr{(nst - 1) % 2}"
            P.mm(pm[:, 0:128], kF[i][:, c, :], qF[i][:, c, :], reads=[f"kF{i}", f"qF{i}"], writes=[pmn])
            P.tt(mtm[:], pm[:, 0:128], C["mask_i"][:], ALU.mult, reads=[pmn, "mask_i"], writes=[mtn])
            P.tr(ptr[:, 0:128], kF[i][:, c, :], C["ident"][:], reads=[f"kF{i}", "ident"], writes=[ptrn])
            P.cp(kt[:], ptr[:, 0:128], reads=[ptrn], writes=[ktn], eng="scalar")
            P.mm(po[:, 0:128], Tc[:], qF[i][:, c, :], start=True, stop=False, reads=[Tcn, f"qF{i}"], writes=[pon])
            P.mm(po[:, 0:128], Vb[i][:, c, :], mtm[:], start=False, stop=True, reads=[f"Vb{i}", mtn], writes=[pon])
            for h in range(2):
                P.cp(y[64 * h:64 * h + 64, c * 64:c * 64 + 64], po[64 * h:64 * h + 64, 64 * h:64 * h + 64], reads=[pon], writes=[(yn, (c, h))], eng="scalar")
            P.mm(p_su[:, 0:128], kt[:], Vb[i][:, c, :], reads=[ktn, f"Vb{i}"], writes=["p_su"])
            pl = eb[:, c * 64 + 63:c * 64 + 64]
            P.ts(tmpT[:], Tc[:], pl, None, ALU.mult, reads=[Tcn, ebn], writes=["tmpT"])
            P.stt(Tn[:], p_su[:, 0:128], pl, tmpT[:], ALU.mult, ALU.add, reads=["p_su", ebn, "tmpT"], writes=[Tnn])
        P.act(k[:], y[:], AF.Square, reads=[yn], writes=[kn])
        P.mm(p_n[:, 0:TT], C["ones_bd"][:], k[:], reads=["ones_bd", kn], writes=["p_n"])
        P.act(k[:], p_n[:, 0:TT], AF.Ln, reads=["p_n"], writes=[kn], bias=epsb[:, 0:1], scale=1.0)
        P.act(k[:], k[:], AF.Exp, reads=[kn], writes=[kn], scale=-0.5)
        P.stt(y[:], y[:], prm_sb[:, 2:3], k[:], ALU.mult, ALU.mult, reads=[yn, kn, "prm"], writes=[yn])
        P.tt(y[:], y[:], zs[:, 3, :], ALU.mult, reads=[yn, zn], writes=[yn])
        P.dma(yT[:, t0:t0 + TT], y[:], reads=[yn], is_output=True)
    return P.finish()


DEC = 0.6065306597126334


def build_rwkv(T, has_vres):
    P = Prog()
    TT = 512
    NCH = TT // 64
    ntile = T // TT
    zin = P.dram("zin", [384, T])
    lrin = P.dram("lrin", [160, T])
    prm = P.dram("prm", [128, 16])
    prm2 = P.dram("prm2", [96, 4])
    wlr = P.dram("wlr", [96, 4, 128])
    if has_vres:
        vlr = P.dram("vlr", [32, T])
        vfirst = P.dram("vfirst", [128, T])
    yT = P.dram("yT", [128, T], F32, kind="ExternalOutput")
    vout = P.dram("vout", [128, T], F32, kind="ExternalOutput")
    C = make_consts(P, need_strict=True)
    mA = P.sb("mA", [128, 256], F32)
    mK = P.sb("mK", [128, 256], F32)
    P.ts(mA[:, 0:128], C["mask_s"][:], -1.0, None, ALU.mult, reads=["mask_s"], writes=["mA"])
    P.cp(mA[:, 128:256], C["mask_i"][:], reads=["mask_i"], writes=["mA"])
    P.cp(mK[:, 0:128], C["mask_s"][:], reads=["mask_s"], writes=["mK"])
    P.cp(mK[:, 128:256], C["mask_i"][:], reads=["mask_i"], writes=["mK"])
    prm_sb = P.sb("prm_sb", [128, 16], F32)
    prm2_sb = P.sb("prm2_sb", [96, 4], F32)
    wlr_sb = P.sb("wlr_sb", [96, 4, 128], F32)
    P.dma(prm_sb[:], prm, writes=["prm"])
    P.dma(prm2_sb[:], prm2, writes=["prm2"])
    P.dma(wlr_sb[:], wlr, writes=["wlr"])
    omk = P.sb("omk", [128, 1], F32)
    P.ts(omk[:], prm_sb[:, 6:7], -1.0, 1.0, ALU.mult, ALU.add, reads=["prm"], writes=["omk"])
    cb = P.sb("cb", [128, 2], F32)
    P.memset(cb[:, 0:1], 1e-6, writes=["cb"])
    P.memset(cb[:, 1:2], 64e-5, writes=["cb"])

    def col(j):
        return prm_sb[:, j:j + 1]

    zv = zin.rearrange("(q p) t -> p q t", p=128)
    z_sb = [P.sb(f"z_sb{i}", [128, 3, 1 + TT], F32) for i in range(2)]
    l_sb = [P.sb(f"l_sb{i}", [96, 3, 1 + TT], F32) for i in range(2)]
    zl = [P.sb(f"zl{i}", [128, 3, TT], F32) for i in range(2)]
    ll = P.sb("ll", [96, 3, TT], F32)
    if has_vres:
        vl_sb = P.sb("vl_sb", [32, TT], F32)
        vf_sb = P.sb("vf_sb", [128, TT], F32)
    names1 = ["sw", "bb", "bx", "a_s", "kkr", "t1", "kk", "kmod", "alpha", "enb", "ebx"]
    S1 = {n: P.sb("s_" + n, [128, TT], F32) for n in names1}
    eb = [P.sb(f"eb{i}", [128, TT], F32) for i in range(2)]
    g_sb = [P.sb(f"g_sb{i}", [128, TT], F32) for i in range(2)]
    bv = [P.sb(f"bv{i}", [128, TT], F32) for i in range(2)]
    y_sb = [P.sb(f"y_sb{i}", [128, TT], F32) for i in range(2)]
    aF = [P.sb(f"aF{i}", [128, NCH, 128], F32) for i in range(2)]
    kF = [P.sb(f"kF{i}", [128, NCH, 128], F32) for i in range(2)]
    cq = [P.sb(f"cq{i}", [128, NCH, 256], F32) for i in range(2)]
    vTb = P.sb("vTb", [128, NCH, 128], F32)
    Vb = [P.sb(f"Vb{i}", [128, NCH, 128], F32) for i in range(2)]
    aTok = [P.sb(f"aTok{i}", [128, NCH, 128], F32) for i in range(2)]
    kTok = [P.sb(f"kTok{i}", [128, NCH, 128], F32) for i in range(2)]
    for i in range(2):
        for t_, n_ in ((aF[i], f"aF{i}"), (kF[i], f"kF{i}"), (cq[i], f"cq{i}")):
            P.memset(t_[:], 0.0, writes=[n_], eng="gpsimd")
    P.memset(vTb[:], 0.0, writes=["vTb"], eng="gpsimd")
    Tst = [P.sb(f"Tst{i}", [128, 128], F32) for i in range(2)]
    P.memset(Tst[0][:], 0.0, writes=["Tst0"])
    tmpT = P.sb("tmpT", [128, 128], F32)
    MA = [P.sb(f"MA{i}", [128, 256], F32) for i in range(2)]
    MK = [P.sb(f"MK{i}", [128, 256], F32) for i in range(2)]
    YY = [P.sb(f"YY{i}", [128, 256], F32) for i in range(2)]
    PI = [P.sb(f"PI{i}", [128, 128], F32) for i in range(2)]
    W0s = P.sb("W0s", [128, 128], F32)
    Us = P.sb("Us", [128, 128], F32)
    pA = P.ps("pA", [128, 512], F32)
    pB = P.ps("pB", [128, 512], F32)
    pMA = P.ps("pMA", [128, 512], F32)
    pPi = P.ps("pPi", [128, 512], F32)
    pY = P.ps("pY", [128, 512], F32)
    pW = P.ps("pW", [128, 512], F32)
    pU = P.ps("pU", [128, 512], F32)
    pO = P.ps("pO", [128, 512], F32)

    def prep(ti):
        i = ti % 2
        t0 = ti * TT
        zs, zn = z_sb[i], f"z{i}"
        ls, ln_ = l_sb[i], f"l{i}"
        if ti == 0:
            P.memset(zs[:, :, 0:1], 0.0, writes=[zn])
            P.memset(ls[:, :, 0:1], 0.0, writes=[ln_])
            P.dma(zs[:, :, 1:1 + TT], zv[:, :, 0:TT], writes=[zn])
            P.dma(ls[0:32, 0, 1:1 + TT], lrin[0:32, 0:TT], writes=[ln_])
            P.dma(ls[0:32, 1, 1:1 + TT], lrin[32:64, 0:TT], writes=[ln_])
            P.dma(ls[0:96, 2, 1:1 + TT], lrin[64:160, 0:TT], writes=[ln_])
        else:
            P.dma(zs[:], zv[:, :, t0 - 1:t0 + TT], writes=[zn])
            P.dma(ls[0:32, 0, :], lrin[0:32, t0 - 1:t0 + TT], writes=[ln_])
            P.dma(ls[0:32, 1, :], lrin[32:64, t0 - 1:t0 + TT], writes=[ln_])
            P.dma(ls[0:96, 2, :], lrin[64:160, t0 - 1:t0 + TT], writes=[ln_])
        z, zln = zl[i], f"zl{i}"
        P.tt(z[:], zs[:, :, 0:TT], zs[:, :, 1:1 + TT], ALU.subtract, reads=[zn], writes=[zln])
        for q in range(3):
            P.stt(z[:, q, :], z[:, q, :], col(q), zs[:, q, 1:1 + TT], ALU.mult, ALU.add, reads=[zln, zn, "prm"], writes=[zln])
        for q, rows in ((0, 32), (1, 32), (2, 96)):
            P.tt(ll[0:rows, q, :], ls[0:rows, q, 0:TT], ls[0:rows, q, 1:1 + TT], ALU.subtract, reads=[ln_], writes=["ll"])
            P.stt(ll[0:rows, q, :], ll[0:rows, q, :], prm2_sb[0:rows, q:q + 1], ls[0:rows, q, 1:1 + TT], ALU.mult, ALU.add, reads=["ll", ln_, "prm2"], writes=["ll"])
        r_, k_, v_ = z[:, 0, :], z[:, 1, :], z[:, 2, :]
        if has_vres:
            P.dma(vl_sb[:], vlr[:, t0:t0 + TT], writes=["vl"])
            P.dma(vf_sb[:], vfirst[:, t0:t0 + TT], writes=["vf"])
            P.mm(pA[:, 0:TT], wlr_sb[0:32, 3, :], vl_sb[:], reads=["wlr", "vl"], writes=["pA"])
            P.act(S1["t1"][:], pA[:, 0:TT], AF.Sigmoid, reads=["pA", "prm"], writes=["t1"], bias=col(10), scale=1.0)
            P.tt(vf_sb[:], vf_sb[:], v_, ALU.subtract, reads=["vf", zln], writes=["vf"])
            P.tt(vf_sb[:], vf_sb[:], S1["t1"][:], ALU.mult, reads=["vf", "t1"], writes=["vf"])
            P.tt(v_, v_, vf_sb[:], ALU.add, reads=["vf", zln], writes=[zln])
        P.dma(vout[:, t0:t0 + TT], v_, reads=[zln], is_output=True)
        P.act(ll[0:32, 0, :], ll[0:32, 0, :], AF.Tanh, reads=["ll"], writes=["ll"])
        P.mm(pA[:, 0:TT], wlr_sb[0:32, 0, :], ll[0:32, 0, :], reads=["wlr", "ll"], writes=["pA"])
        P.act(S1["sw"][:], pA[:, 0:TT], AF.Sigmoid, reads=["pA", "prm"], writes=["sw"], bias=col(3), scale=1.0)
        P.op("vector", "tensor_tensor_scan", reads=["sw", "cmask"], writes=["bb"], out=S1["bb"][:], data0=C["cmask"][:].rearrange("p c t -> p (c t)"),
             data1=S1["sw"][:], initial=0.0, op0=ALU.mult, op1=ALU.add)
        P.tt(S1["bx"][:], S1["bb"][:], S1["sw"][:], ALU.subtract, reads=["bb", "sw"], writes=["bx"])
        e_, en = eb[i], f"eb{i}"
        P.act(e_[:], S1["bb"][:], AF.Exp, reads=["bb"], writes=[en], scale=-DEC)
        P.act(S1["enb"][:], S1["bb"][:], AF.Exp, reads=["bb"], writes=["enb"], scale=DEC)
        P.act(S1["ebx"][:], S1["bx"][:], AF.Exp, reads=["bx"], writes=["ebx"], scale=-DEC)
        P.mm(pB[:, 0:TT], wlr_sb[0:32, 1, :], ll[0:32, 1, :], reads=["wlr", "ll"], writes=["pB"])
        P.act(S1["a_s"][:], pB[:, 0:TT], AF.Sigmoid, reads=["pB", "prm"], writes=["a_s"], bias=col(4), scale=1.0)
        P.act(ll[0:96, 2, :], ll[0:96, 2, :], AF.Sigmoid, reads=["ll"], writes=["ll"])
        P.mm(pA[:, 0:TT], wlr_sb[0:96, 2, :], ll[0:96, 2, :], reads=["wlr", "ll"], writes=["pA"])
        P.cp(g_sb[i][:], pA[:, 0:TT], reads=["pA"], writes=[f"g{i}"], eng="scalar")
        P.ts(S1["kkr"][:], k_, col(5), None, ALU.mult, reads=[zln, "prm"], writes=["kkr"])
        P.act(S1["t1"][:], S1["kkr"][:], AF.Square, reads=["kkr"], writes=["t1"])
        P.mm(pB[:, 0:TT], C["ones_bd"][:], S1["t1"][:], reads=["ones_bd", "t1"], writes=["pB"])
        P.act(S1["t1"][:], pB[:, 0:TT], AF.Ln, reads=["pB", "cb"], writes=["t1"], bias=cb[:, 0:1], scale=64.0)
        P.act(S1["t1"][:], S1["t1"][:], AF.Exp, reads=["t1"], writes=["t1"], scale=-0.5)
        P.tt(S1["kk"][:], S1["kkr"][:], S1["t1"][:], ALU.mult, reads=["kkr", "t1"], writes=["kk"])
        P.ts(S1["kmod"][:], S1["a_s"][:], col(6), omk[:, 0:1], ALU.mult, ALU.add, reads=["a_s", "prm", "omk"], writes=["kmod"])
        P.tt(S1["kmod"][:], S1["kmod"][:], k_, ALU.mult, reads=["kmod", zln], writes=["kmod"])
        P.tt(S1["alpha"][:], S1["kk"][:], S1["a_s"][:], ALU.mult, reads=["kk", "a_s"], writes=["alpha"])
        P.stt(S1["t1"][:], r_, col(7), S1["kmod"][:], ALU.mult, ALU.mult, reads=[zln, "prm", "kmod"], writes=["t1"])
        P.mm(pA[:, 0:TT], C["ones_bd"][:], S1["t1"][:], reads=["ones_bd", "t1"], writes=["pA"])
        P.stt(bv[i][:], pA[:, 0:TT], 64.0, v_, ALU.mult, ALU.mult, reads=["pA", zln], writes=[f"bv{i}"])
        bd_write(P, aF[i], f"aF{i}", S1["alpha"], "alpha", S1["enb"], "enb", NCH)
        bd_write(P, kF[i], f"kF{i}", S1["kmod"], "kmod", S1["enb"], "enb", NCH)
        bd_write(P, cq[i][:, :, 0:128], f"cq{i}", S1["kk"], "kk", S1["ebx"], "ebx", NCH)
        bd_write(P, cq[i][:, :, 128:256], f"cq{i}", z[:, 0, :], zln, e_, en, NCH)
        for h in range(2):
            pp = slice(64 * h, 64 * h + 64)
            P.cp(vTb[pp, :, 64 * h:64 * h + 64], z[pp, 2, :].rearrange("p (c t) -> p c t", t=64), reads=[zln], writes=[("vTb", h)])
        for (src, sn, dst, dn) in ((vTb, "vTb", Vb[i], f"Vb{i}"), (aF[i], f"aF{i}", aTok[i], f"aTok{i}"), (kF[i], f"kF{i}", kTok[i], f"kTok{i}")):
            for half in range(2):
                pt, ptn = (pA, "pA") if half == 0 else (pB, "pB")
                for cc in range(4):
                    c = half * 4 + cc
                    P.tr(pt[:, cc * 128:(cc + 1) * 128], src[:, c, :], C["ident"][:], reads=[sn, "ident"], writes=[ptn])
                P.cp(dst[:, half * 4:half * 4 + 4, :], pt[:, :].rearrange("p (c t) -> p c t", t=128), reads=[ptn], writes=[(dn, half)], eng="scalar")

    PIF = [P.sb(f"PIF{i}", [128, 128], F32) for i in range(3)]
    MA3 = [P.sb(f"MA3_{i}", [128, 256], F32) for i in range(3)]
    MK3 = [P.sb(f"MK3_{i}", [128, 256], F32) for i in range(3)]
    YYs = [[P.sb(f"YYs{s_}_{i}", [128, 256], F32) for i in range(2)] for s_ in range(2)]
    PIs = [[P.sb(f"PIs{s_}_{i}", [128, 128], F32) for i in range(2)] for s_ in range(2)]
    pYs = [pMA, pY]
    pPs = [pPi, pW]

    def genA(ti, c, g):
        i = ti % 2
        j = g % 3
        s_ = g % 2
        pYb, pYn = pYs[s_], f"pYs{s_}"
        pPb, pPn = pPs[s_], f"pPs{s_}"
        ma, man = MA3[j], f"MA3_{j}"
        mk, mkn = MK3[j], f"MK3_{j}"
        P.mm(pPb[:, 0:256], aF[i][:, c, :], cq[i][:, c, :], reads=[f"aF{i}", f"cq{i}"], writes=[pPn]); yield
        P.mm(pPb[:, 256:512], kF[i][:, c, :], cq[i][:, c, :], reads=[f"kF{i}", f"cq{i}"], writes=[pPn]); yield
        P.tt(ma[:], pPb[:, 0:256], mA[:], ALU.mult, reads=[pPn, "mA"], writes=[man]); yield
        P.tt(mk[:], pPb[:, 256:512], mK[:], ALU.mult, reads=[pPn, "mK"], writes=[mkn]); yield
        yt, ytn = YYs[s_][0], f"YYs{s_}_0"
        P.tr(pYb[:, 0:128], ma[:, 0:128], C["ident"][:], reads=[man, "ident"], writes=[pYn]); yield
        P.cp(yt[:, 128:256], pYb[:, 0:128], reads=[pYn], writes=[ytn], eng="scalar"); yield
        Yap, Yn_ = ma[:, 0:128], man
        YTap, YTn = yt[:, 128:256], ytn
        pi, pin = PIs[s_][0], f"PIs{s_}_0"
        P.tt(pi[:], ma[:, 0:128], C["ident"][:], ALU.add, reads=[man, "ident"], writes=[pin]); yield
        for k in range(1, 6):
            y2, y2n = YYs[s_][k % 2], f"YYs{s_}_{k % 2}"
            if k < 5:
                P.mm(pYb[:, 0:128], YTap, Yap, reads=[Yn_, YTn], writes=[pYn]); yield
            P.mm(pYb[:, 128:256], Yap, YTap, reads=[Yn_, YTn], writes=[pYn]); yield
            if k < 5:
                P.cp(y2[:], pYb[:, 0:256], reads=[pYn], writes=[y2n], eng="scalar"); yield
            else:
                P.cp(y2[:, 128:256], pYb[:, 128:256], reads=[pYn], writes=[y2n], eng="scalar"); yield
            if k < 5:
                pi2, pi2n = PIs[s_][k % 2], f"PIs{s_}_{k % 2}"
            else:
                pi2, pi2n = PIF[j], f"PIF{j}"
            P.mm(pPb[:, 0:128], y2[:, 128:256], pi[:], reads=[y2n, pin], writes=[pPn]); yield
            P.tt(pi2[:], pPb[:, 0:128], pi[:], ALU.add, reads=[pPn, pin], writes=[pi2n]); yield
            Yap, Yn_, YTap, YTn, pi, pin = y2[:, 0:128], y2n, y2[:, 128:256], y2n, pi2, pi2n

    def genB(ti, c, g):
        i = ti % 2
        j = g % 3
        y, yn = y_sb[i], f"y{i}"
        Tc, Tcn = Tst[g % 2], f"Tst{g % 2}"
        Tn, Tnn = Tst[1 - g % 2], f"Tst{1 - g % 2}"
        ma, man = MA3[j], f"MA3_{j}"
        mk, mkn = MK3[j], f"MK3_{j}"
        pi, pin = PIF[j], f"PIF{j}"
        pl = eb[i][:, c * 64 + 63:c * 64 + 64]
        P.ts(tmpT[:], Tc[:], pl, None, ALU.mult, reads=[Tcn, f"eb{i}"], writes=["tmpT"]); yield
        P.mm(pU[:, 0:128], cq[i][:, c, 0:128], Tc[:], start=True, stop=False, reads=[f"cq{i}", Tcn], writes=["pU"]); yield
        P.mm(pU[:, 0:128], mk[:, 0:128], Vb[i][:, c, :], start=False, stop=True, reads=[mkn, f"Vb{i}"], writes=["pU"]); yield
        P.act(W0s[:], pU[:, 0:128], AF.Identity, reads=["pU"], writes=["W0s"], scale=-1.0); yield
        P.mm(pU[:, 128:256], pi[:], W0s[:], reads=[pin, "W0s"], writes=["pU"]); yield
        P.cp(Us[:], pU[:, 128:256], reads=["pU"], writes=["Us"], eng="scalar"); yield
        P.mm(pU[:, 256:384], aTok[i][:, c, :], Us[:], start=True, stop=False, reads=[f"aTok{i}", "Us"], writes=["pU"]); yield
        P.mm(pU[:, 256:384], kTok[i][:, c, :], Vb[i][:, c, :], start=False, stop=True, reads=[f"kTok{i}", f"Vb{i}"], writes=["pU"]); yield
        P.stt(Tn[:], pU[:, 256:384], pl, tmpT[:], ALU.mult, ALU.add, reads=["pU", f"eb{i}", "tmpT"], writes=[Tnn]); yield
        P.mm(pO[:, 0:128], Tc[:], cq[i][:, c, 128:256], start=True, stop=False, reads=[Tcn, f"cq{i}"], writes=["pO"]); yield
        P.mm(pO[:, 0:128], Us[:], ma[:, 128:256], start=False, stop=False, reads=["Us", man], writes=["pO"]); yield
        P.mm(pO[:, 0:128], Vb[i][:, c, :], mk[:, 128:256], start=False, stop=True, reads=[f"Vb{i}", mkn], writes=["pO"]); yield
        for h in range(2):
            P.cp(y[64 * h:64 * h + 64, c * 64:c * 64 + 64], pO[64 * h:64 * h + 64, 64 * h:64 * h + 64], reads=["pO"], writes=[(yn, (c, h))], eng="scalar"); yield

    def post(ti):
        i = ti % 2
        t0 = ti * TT
        y, yn = y_sb[i], f"y{i}"
        t1 = S1["t1"]
        P.mm(pA[:, 0:TT], C["ones_bd"][:], y[:], reads=["ones_bd", yn], writes=["pA"])
        P.tt(y[:], y[:], pA[:, 0:TT], ALU.subtract, reads=[yn, "pA"], writes=[yn])
        P.act(t1[:], y[:], AF.Square, reads=[yn], writes=["t1"])
        P.mm(pB[:, 0:TT], C["ones_bd"][:], t1[:], reads=["ones_bd", "t1"], writes=["pB"])
        P.act(t1[:], pB[:, 0:TT], AF.Ln, reads=["pB", "cb"], writes=["t1"], bias=cb[:, 1:2], scale=1.0)
        P.act(t1[:], t1[:], AF.Exp, reads=["t1"], writes=["t1"], scale=-0.5)
        P.stt(y[:], y[:], col(8), t1[:], ALU.mult, ALU.mult, reads=[yn, "prm", "t1"], writes=[yn])
        P.stt(y[:], y[:], col(9), bv[i][:], ALU.add, ALU.add, reads=[yn, "prm", f"bv{i}"], writes=[yn])
        P.tt(y[:], y[:], g_sb[i][:], ALU.mult, reads=[yn, f"g{i}"], writes=[yn])
        P.dma(yT[:, t0:t0 + TT], y[:], reads=[yn], is_output=True)

    seq = [(ti, c) for ti in range(ntile) for c in range(NCH)]
    prep(0)
    if ntile > 1:
        prep(1)
    pipeline3(seq, genA, genB, lambda ti: prep(ti + 2) if ti + 2 < ntile else None, post, NCH)
    return P.finish()


def build_gdn(T, stage=9):
    P = Prog()
    TT = 512
    NCH = TT // 64
    ntile = T // TT
    NC_ALL = T // 64
    zin = P.dram("zin", [512, T])
    abrow = P.dram("abrow", [2, T])
    abcol = P.dram("abcol", [64, 2, NC_ALL])
    prm = P.dram("prm", [128, 16])
    yT = P.dram("yT", [128, T], F32, kind="ExternalOutput")
    ident = P.sb("ident", [128, 128], F32)
    P.memset(ident[:], 1.0, writes=["ident"], eng="gpsimd")
    P.op("gpsimd", "affine_select", reads=["ident"], writes=["ident"], out=ident[:], in_=ident[:], pattern=[[1, 128]],
         compare_op=ALU.is_equal, fill=0.0, base=0, channel_multiplier=-1)
    m2 = P.sb("m2", [64, 2, 64], F32)
    P.memset(m2[:], 1.0, writes=["m2"], eng="gpsimd")
    P.op("gpsimd", "affine_select", reads=["m2"], writes=["m2"], out=m2[:, 0, :], in_=m2[:, 0, :], pattern=[[1, 64]],
         compare_op=ALU.is_gt, fill=0.0, base=0, channel_multiplier=-1)
    P.op("gpsimd", "affine_select", reads=["m2"], writes=["m2"], out=m2[:, 1, :], in_=m2[:, 1, :], pattern=[[1, 64]],
         compare_op=ALU.is_ge, fill=0.0, base=0, channel_multiplier=-1)
    P.ts(m2[:, 0, :], m2[:, 0, :], -1.0, None, ALU.mult, reads=["m2"], writes=["m2"])
    tri = P.sb("tri", [64, 64], F32)
    P.cp(tri[:], m2[:, 1, :], reads=["m2"], writes=["tri"])
    o64 = P.sb("o64", [64, 64], F32)
    P.memset(o64[:], 1.0, writes=["o64"])
    o128 = P.sb("o128", [128, 128], F32)
    P.memset(o128[:], 1.0, writes=["o128"])
    cm = P.sb("cmask", [128, NCH, 64], F32)
    P.memset(cm[:], 1.0, writes=["cmask"], eng="gpsimd")
    P.memset(cm[:, :, 0:1], 0.0, writes=["cmask"], eng="gpsimd")
    prm_sb = P.sb("prm_sb", [128, 16], F32)
    P.dma(prm_sb[:], prm, writes=["prm"])
    cb = P.sb("cb", [128, 4], F32)
    P.memset(cb[:, 0:1], 1e-6, writes=["cb"])
    P.memset(cb[:, 1:2], 1.0, writes=["cb"])
    P.memset(cb[:, 2:3], -0.5 * float(np.log(128.0)), writes=["cb"])
    P.act(cb[:, 3:4], prm_sb[:, 12:13], AF.Exp, reads=["prm"], writes=["cb"])
    P.ts(cb[:, 3:4], cb[:, 3:4], -1.0, None, ALU.mult, reads=["cb"], writes=["cb"])

    def col(j):
        return prm_sb[:, j:j + 1]

    pA = P.ps("pA", [128, 512], F32)
    pB = P.ps("pB", [128, 512], F32)
    pM = P.ps("pM", [128, 512], F32)
    pY = P.ps("pY", [128, 512], F32)
    pP = P.ps("pP", [128, 512], F32)
    pUW = P.ps("pUW", [128, 512], F32)
    pV = P.ps("pV", [128, 512], F32)
    pO = P.ps("pO", [128, 512], F32)

    ac = P.sb("ac", [64, 2, NC_ALL], F32)
    P.dma(ac[:], abcol, writes=["ac"])
    betac = P.sb("betac", [64, NC_ALL], F32)
    gcol = P.sb("gcol", [64, NC_ALL], F32)
    gccol = P.sb("gccol", [64, NC_ALL], F32)
    c2 = P.sb("c2", [64, NC_ALL], F32)
    c3 = P.sb("c3", [64, NC_ALL], F32)
    P.act(betac[:], ac[:, 0, :], AF.Sigmoid, reads=["ac"], writes=["betac"])
    P.act(gcol[:], ac[:, 1, :], AF.Exp, reads=["ac", "prm"], writes=["gcol"], bias=prm_sb[0:64, 13:14], scale=1.0)
    P.act(gcol[:], gcol[:], AF.Ln, reads=["gcol", "cb"], writes=["gcol"], bias=cb[0:64, 1:2], scale=1.0)
    P.ts(gcol[:], gcol[:], cb[0:64, 3:4], None, ALU.mult, reads=["gcol", "cb"], writes=["gcol"])
    for c0 in range(0, NC_ALL, 512):
        n = min(512, NC_ALL - c0)
        P.mm(pA[0:64, 0:n], tri[:], gcol[:, c0:c0 + n], reads=["tri", "gcol"], writes=["pA"])
        P.cp(gccol[:, c0:c0 + n], pA[0:64, 0:n], reads=["pA"], writes=["gccol"], eng="scalar")
        P.mm(pB[0:64, 0:n], o64[:], gcol[:, c0:c0 + n], reads=["o64", "gcol"], writes=["pB"])
        P.tt(c3[:, c0:c0 + n], pB[0:64, 0:n], gccol[:, c0:c0 + n], ALU.subtract, reads=["pB", "gccol"], writes=["c3"])
    P.act(c3[:], c3[:], AF.Exp, reads=["c3"], writes=["c3"])
    P.act(c2[:], gccol[:], AF.Exp, reads=["gccol"], writes=["c2"])
    P.tt(c2[:], c2[:], betac[:], ALU.mult, reads=["c2", "betac"], writes=["c2"])

    zv = zin.rearrange("(q p) t -> p q t", p=128)
    z_sb = [P.sb(f"z_sb{i}", [128, 4, 3 + TT], F32) for i in range(2)]
    ab_sb = P.sb("ab_sb", [128, 2, TT], F32)
    cv_sb = P.sb("cv_sb", [128, 3, TT], F32)
    sq_sb = P.sb("sq_sb", [128, TT], F32)
    rn_sb = P.sb("rn_sb", [128, TT], F32)
    gcb = [P.sb(f"gcb{i}", [128, TT], F32) for i in range(2)]
    egc = [P.sb(f"egc{i}", [128, TT], F32) for i in range(2)]
    kT = [P.sb(f"kT{i}", [128, TT], F32) for i in range(2)]
    kq = [P.sb(f"kq{i}", [128, NCH, 128], F32) for i in range(2)]
    qg = [P.sb(f"qg{i}", [128, TT], F32) for i in range(2)]
    sg = [P.sb(f"sg{i}", [128, TT], F32) for i in range(2)]
    DTm = [P.sb(f"DTm{i}", [64, NCH, 2, 64], F32) for i in range(2)]
    dt_tmp = P.sb("dt_tmp", [64, NCH, 64], F32)
    RV = [P.sb(f"RV{i}", [64, NCH, 128], F32) for i in range(2)]
    RK = [P.sb(f"RK{i}", [64, NCH, 128], F32) for i in range(2)]
    KS = [P.sb(f"KS{i}", [64, NCH, 128], F32) for i in range(2)]
    y_sb = [P.sb(f"y_sb{i}", [128, TT], F32) for i in range(2)]
    Sst = [P.sb(f"Sst{i}", [128, 128], F32) for i in range(2)]
    P.memset(Sst[0][:], 0.0, writes=["Sst0"])
    tmpS = P.sb("tmpS", [128, 128], F32)
    M2 = [P.sb(f"M2{i}", [64, 128], F32) for i in range(2)]
    YY = [P.sb(f"YY{i}", [64, 128], F32) for i in range(2)]
    PI = [P.sb(f"PI{i}", [64, 64], F32) for i in range(2)]
    u_sb = P.sb("u_sb", [64, 128], F32)
    wT_sb = P.sb("wT_sb", [128, 64], F32)
    vn_sb = P.sb("vn_sb", [64, 128], F32)

    def prep(ti):
        i = ti % 2
        t0 = ti * TT
        c0 = ti * NCH
        zs, zn = z_sb[i], f"z{i}"
        if ti == 0:
            P.memset(zs[:, :, 0:3], 0.0, writes=[zn])
            P.dma(zs[:, :, 3:3 + TT], zv[:, :, 0:TT], writes=[zn])
        else:
            P.dma(zs[:], zv[:, :, t0 - 3:t0 + TT], writes=[zn])
        P.dma(ab_sb[:, 0, :], abrow[0:1, t0:t0 + TT].to_broadcast([128, TT]), writes=["ab"])
        P.dma(ab_sb[:, 1, :], abrow[1:2, t0:t0 + TT].to_broadcast([128, TT]), writes=["ab"])
        for q in range(3):
            P.ts(cv_sb[:, q, :], zs[:, q, 3:3 + TT], col(4 * q + 3), None, ALU.mult, reads=[zn, "prm"], writes=[("cv", q)])
            for j in range(3):
                P.stt(cv_sb[:, q, :], zs[:, q, j:j + TT], col(4 * q + j), cv_sb[:, q, :], ALU.mult, ALU.add, reads=[zn, "prm", ("cv", q)], writes=[("cv", q)])
        P.act(cv_sb[:], cv_sb[:], AF.Silu, reads=["cv"], writes=["cv"])
        P.act(sg[i][:], zs[:, 3, 3:3 + TT], AF.Silu, reads=[zn], writes=[f"sg{i}"])
        P.act(ab_sb[:, 0, :], ab_sb[:, 0, :], AF.Sigmoid, reads=["ab"], writes=["ab"])
        P.act(ab_sb[:, 1, :], ab_sb[:, 1, :], AF.Exp, reads=["ab", "prm"], writes=["ab"], bias=col(13), scale=1.0)
        P.act(ab_sb[:, 1, :], ab_sb[:, 1, :], AF.Ln, reads=["ab", "cb"], writes=["ab"], bias=cb[:, 1:2], scale=1.0)
        P.ts(ab_sb[:, 1, :], ab_sb[:, 1, :], cb[:, 3:4], None, ALU.mult, reads=["ab", "cb"], writes=["ab"])
        P.op("vector", "tensor_tensor_scan", reads=["ab", "cmask"], writes=[f"gcb{i}"], out=gcb[i][:], data0=cm[:].rearrange("p c t -> p (c t)"),
             data1=ab_sb[:, 1, :], initial=0.0, op0=ALU.mult, op1=ALU.add)
        P.act(egc[i][:], gcb[i][:], AF.Exp, reads=[f"gcb{i}"], writes=[f"egc{i}"])
        for q in range(2):
            P.act(sq_sb[:], cv_sb[:, q, :], AF.Square, reads=["cv"], writes=["sq"])
            pp, ppn = (pA, "pA") if q == 0 else (pB, "pB")
            P.mm(pp[:, 0:TT], o128[:], sq_sb[:], reads=["o128", "sq"], writes=[ppn])
            P.act(rn_sb[:], pp[:, 0:TT], AF.Ln, reads=[ppn, "cb"], writes=["rn"], bias=cb[:, 0:1], scale=1.0)
            if q == 0:
                P.act(rn_sb[:], rn_sb[:], AF.Exp, reads=["rn", "cb"], writes=["rn"], bias=cb[:, 2:3], scale=-0.5)
                P.tt(kq[i][:, :, 64:128], cv_sb[:, 0, :].rearrange("p (c t) -> p c t", t=64), rn_sb[:].rearrange("p (c t) -> p c t", t=64), ALU.mult,
                     reads=["cv", "rn"], writes=[(f"kq{i}", 1)])
            else:
                P.act(rn_sb[:], rn_sb[:], AF.Exp, reads=["rn"], writes=["rn"], scale=-0.5)
                P.tt(kT[i][:], cv_sb[:, 1, :], rn_sb[:], ALU.mult, reads=["cv", "rn"], writes=[f"kT{i}"])
        P.tt(qg[i][:].rearrange("p (c t) -> p c t", t=64), kq[i][:, :, 64:128], egc[i][:].rearrange("p (c t) -> p c t", t=64), ALU.mult,
             reads=[f"kq{i}", f"egc{i}"], writes=[f"qg{i}"])
        P.tt(kq[i][:, :, 0:64], kT[i][:].rearrange("p (c t) -> p c t", t=64), ab_sb[:, 0, :].rearrange("p (c t) -> p c t", t=64), ALU.mult,
             reads=[f"kT{i}", "ab"], writes=[(f"kq{i}", 0)])
        P.tt(dt_tmp[:], gcb[i][0:64, :].rearrange("p (c t) -> p c t", t=64), gccol[:, c0:c0 + NCH].unsqueeze(2).to_broadcast([64, NCH, 64]), ALU.subtract,
             reads=[f"gcb{i}", "gccol"], writes=["dt_tmp"])
        P.ts(dt_tmp[:], dt_tmp[:], 0.0, None, ALU.min, reads=["dt_tmp"], writes=["dt_tmp"])
        P.act(dt_tmp[:], dt_tmp[:], AF.Exp, reads=["dt_tmp"], writes=["dt_tmp"])
        for w in range(2):
            P.tt(DTm[i][:, :, w, :], dt_tmp[:], m2[:, w, :].unsqueeze(1).to_broadcast([64, NCH, 64]), ALU.mult, reads=["dt_tmp", "m2"], writes=[(f"DTm{i}", w)])
        for half in range(2):
            for cc in range(4):
                c = half * 4 + cc
                P.tr(pA[0:64, cc * 128:(cc + 1) * 128], kT[i][:, c * 64:(c + 1) * 64], ident[:], reads=[f"kT{i}", "ident"], writes=["pA"])
                P.tr(pB[0:64, cc * 128:(cc + 1) * 128], cv_sb[:, 2, c * 64:(c + 1) * 64], ident[:], reads=["cv", "ident"], writes=["pB"])
            cs = slice(c0 + half * 4, c0 + half * 4 + 4)
            hs = slice(half * 4, half * 4 + 4)
            pa3 = pA[0:64, :].rearrange("p (c t) -> p c t", t=128)
            pb3 = pB[0:64, :].rearrange("p (c t) -> p c t", t=128)
            P.tt(RK[i][:, hs, :], pa3, c2[:, cs].unsqueeze(2).to_broadcast([64, 4, 128]), ALU.mult, reads=["pA", "c2"], writes=[(f"RK{i}", half)])
            P.tt(KS[i][:, hs, :], pa3, c3[:, cs].unsqueeze(2).to_broadcast([64, 4, 128]), ALU.mult, reads=["pA", "c3"], writes=[(f"KS{i}", half)])
            P.tt(RV[i][:, hs, :], pb3, betac[:, cs].unsqueeze(2).to_broadcast([64, 4, 128]), ALU.mult, reads=["pB", "betac"], writes=[(f"RV{i}", half)])

    M23 = [P.sb(f"M23_{i}", [64, 128], F32) for i in range(3)]
    PIF = [P.sb(f"PIF{i}", [64, 64], F32) for i in range(3)]
    YYs = [[P.sb(f"YYs{s_}_{i}", [64, 128], F32) for i in range(2)] for s_ in range(2)]
    PIs = [[P.sb(f"PIs{s_}_{i}", [64, 64], F32) for i in range(2)] for s_ in range(2)]
    pYs = [pM, pY]
    pPs = [pP, pUW]

    def genA(ti, c, g):
        i = ti % 2
        j = g % 3
        s_ = g % 2
        pYb, pYn = pYs[s_], f"pYs{s_}"
        pPb, pPn = pPs[s_], f"pPs{s_}"
        mm2, m2n = M23[j], f"M23_{j}"
        cs = slice(c * 64, c * 64 + 64)
        P.mm(pPb[0:64, 0:128], kT[i][:, cs], kq[i][:, c, :], reads=[f"kT{i}", f"kq{i}"], writes=[pPn]); yield
        P.tt(mm2[:], pPb[0:64, 0:128], DTm[i][:, c, :, :].rearrange("p w t -> p (w t)"), ALU.mult, reads=[pPn, f"DTm{i}"], writes=[m2n]); yield
        yt, ytn = YYs[s_][0], f"YYs{s_}_0"
        P.tr(pYb[0:64, 0:64], mm2[:, 0:64], ident[0:64, 0:64], reads=[m2n, "ident"], writes=[pYn]); yield
        P.cp(yt[:, 64:128], pYb[0:64, 0:64], reads=[pYn], writes=[ytn], eng="scalar"); yield
        Yap, Yn_ = mm2[:, 0:64], m2n
        YTap, YTn = yt[:, 64:128], ytn
        pi, pin = PIs[s_][0], f"PIs{s_}_0"
        P.tt(pi[:], mm2[:, 0:64], ident[0:64, 0:64], ALU.add, reads=[m2n, "ident"], writes=[pin]); yield
        for k in range(1, 6):
            y2, y2n = YYs[s_][k % 2], f"YYs{s_}_{k % 2}"
            if k < 5:
                P.mm(pYb[0:64, 0:64], YTap, Yap, reads=[Yn_, YTn], writes=[pYn]); yield
            P.mm(pYb[0:64, 64:128], Yap, YTap, reads=[Yn_, YTn], writes=[pYn]); yield
            if k < 5:
                P.cp(y2[:], pYb[0:64, 0:128], reads=[pYn], writes=[y2n], eng="scalar"); yield
            else:
                P.cp(y2[:, 64:128], pYb[0:64, 64:128], reads=[pYn], writes=[y2n], eng="scalar"); yield
            if k < 5:
                pi2, pi2n = PIs[s_][k % 2], f"PIs{s_}_{k % 2}"
            else:
                pi2, pi2n = PIF[j], f"PIF{j}"
            P.mm(pPb[0:64, 0:64], y2[:, 64:128], pi[:], reads=[y2n, pin], writes=[pPn]); yield
            P.tt(pi2[:], pPb[0:64, 0:64], pi[:], ALU.add, reads=[pPn, pin], writes=[pi2n]); yield
            Yap, Yn_, YTap, YTn, pi, pin = y2[:, 0:64], y2n, y2[:, 64:128], y2n, pi2, pi2n

    def genB(ti, c, g):
        i = ti % 2
        j = g % 3
        y, yn = y_sb[i], f"y{i}"
        Sc, Scn = Sst[g % 2], f"Sst{g % 2}"
        Sn, Snn = Sst[1 - g % 2], f"Sst{1 - g % 2}"
        mm2, m2n = M23[j], f"M23_{j}"
        pi, pin = PIF[j], f"PIF{j}"
        cs = slice(c * 64, c * 64 + 64)
        el = egc[i][:, c * 64 + 63:c * 64 + 64]
        P.mm(pV[0:64, 0:128], pi[:], RV[i][:, c, :], reads=[pin, f"RV{i}"], writes=["pV"]); yield
        P.mm(pV[:, 128:192], RK[i][:, c, :], pi[:], reads=[pin, f"RK{i}"], writes=["pV"]); yield
        P.cp(u_sb[:], pV[0:64, 0:128], reads=["pV"], writes=["u"], eng="scalar"); yield
        P.cp(wT_sb[:], pV[:, 128:192], reads=["pV"], writes=["wT"], eng="scalar"); yield
        P.ts(tmpS[:], Sc[:], el, None, ALU.mult, reads=[Scn, f"egc{i}"], writes=["tmpS"]); yield
        P.mm(pV[0:64, 256:384], wT_sb[:], Sc[:], reads=["wT", Scn], writes=["pV"]); yield
        P.tt(vn_sb[:], u_sb[:], pV[0:64, 256:384], ALU.subtract, reads=["u", "pV"], writes=["vn"]); yield
        P.mm(pV[:, 384:512], KS[i][:, c, :], vn_sb[:], reads=[f"KS{i}", "vn"], writes=["pV"]); yield
        P.tt(Sn[:], pV[:, 384:512], tmpS[:], ALU.add, reads=["pV", "tmpS"], writes=[Snn]); yield
        P.mm(pO[:, 0:64], Sc[:], qg[i][:, cs], start=True, stop=False, reads=[Scn, f"qg{i}"], writes=["pO"]); yield
        P.mm(pO[:, 0:64], vn_sb[:], mm2[:, 64:128], start=False, stop=True, reads=["vn", m2n], writes=["pO"]); yield
        P.cp(y[:, cs], pO[:, 0:64], reads=["pO"], writes=[(yn, c)], eng="scalar"); yield

    def post(ti):
        i = ti % 2
        t0 = ti * TT
        y, yn = y_sb[i], f"y{i}"
        P.act(sq_sb[:], y[:], AF.Square, reads=[yn], writes=["sq"])
        P.mm(pA[:, 0:TT], o128[:], sq_sb[:], reads=["o128", "sq"], writes=["pA"])
        P.act(rn_sb[:], pA[:, 0:TT], AF.Ln, reads=["pA", "cb"], writes=["rn"], bias=cb[:, 0:1], scale=1.0 / 128)
        P.act(rn_sb[:], rn_sb[:], AF.Exp, reads=["rn"], writes=["rn"], scale=-0.5)
        P.stt(y[:], y[:], col(14), rn_sb[:], ALU.mult, ALU.mult, reads=[yn, "prm", "rn"], writes=[yn])
        P.tt(y[:], y[:], sg[i][:], ALU.mult, reads=[yn, f"sg{i}"], writes=[yn])
        P.dma(yT[:, t0:t0 + TT], y[:], reads=[yn], is_output=True)

    seq = [(ti, c) for ti in range(ntile) for c in range(NCH)]
    prep(0)
    if ntile > 1:
        prep(1)
    pipeline3(seq, genA, genB, lambda ti: prep(ti + 2) if ti + 2 < ntile else None, post, NCH)
    return P.finish()


def build_sgu(T):
    P = Prog()
    NB = T // 128
    TB = 4
    ntile = NB // TB
    uT = P.dram("uT", [128, T])
    vtok = P.dram("vtok", [128, NB, 128])
    lnp = P.dram("lnp", [1, 256])
    wT = P.dram("wT", [2, 128, 128])
    bs = P.dram("bs", [2, 128])
    yT = P.dram("yT", [128, T], F32, kind="ExternalOutput")
    lnp_sb = P.sb("lnp_sb", [128, 256], F32)
    P.dma(lnp_sb[:], lnp.to_broadcast([128, 256]), writes=["lnp"])
    w_sb = P.sb("w_sb", [128, 2, 128], F32)
    for g in range(2):
        P.dma(w_sb[:, g, :], wT[g], writes=["w"])
    P.memset(w_sb[64:128, :, 0:64], 0.0, writes=["w"])
    b_sb = P.sb("b_sb", [128, 128], F32)
    for g in range(2):
        P.dma(b_sb[64 * g:64 * g + 64, :], bs[g:g + 1, :].to_broadcast([64, 128]), writes=["b"])
    cb = P.sb("cb", [128, 1], F32)
    P.memset(cb[:], 1e-5, writes=["cb"])
    v_sb = [P.sb(f"v_sb{i}", [128, TB, 128], F32) for i in range(2)]
    sq_sb = P.sb("sq_sb", [128, TB, 128], F32)
    st = P.sb("st", [128, 4, TB * 2], F32)
    u_sb = [P.sb(f"u_sb{i}", [128, TB * 128], F32) for i in range(2)]
    y_sb = [P.sb(f"y_sb{i}", [128, TB * 128], F32) for i in range(2)]
    po = [P.ps(f"po{g}", [128, 512], F32) for g in range(2)]
    for ti in range(ntile):
        i = ti % 2
        n0 = ti * TB
        v, vn = v_sb[i], f"v{i}"
        u, un = u_sb[i], f"u{i}"
        y, yn = y_sb[i], f"y{i}"
        P.dma(v[:], vtok[:, n0:n0 + TB, :], writes=[vn])
        P.dma(u[:], uT[:, n0 * 128:(n0 + TB) * 128], writes=[un])
        P.act(v[:], v[:], AF.Gelu, reads=[vn], writes=[vn])
        P.act(u[:], u[:], AF.Gelu, reads=[un], writes=[un])
        v3 = v[:].rearrange("p n (g c) -> p (n g) c", c=64)
        s3 = sq_sb[:].rearrange("p n (g c) -> p (n g) c", c=64)
        P.op("vector", "tensor_reduce", reads=[vn], writes=[("st", 0)], out=st[:, 0, :], in_=v3, axis=AX.X, op=ALU.add)
        P.ts(st[:, 0, :], st[:, 0, :], 1.0 / 64, None, ALU.mult, reads=[("st", 0)], writes=[("st", 0)])
        P.tt(v3, v3, st[:, 0, :].unsqueeze(2).to_broadcast([128, TB * 2, 64]), ALU.subtract, reads=[vn, ("st", 0)], writes=[vn])
        P.act(sq_sb[:], v[:], AF.Square, reads=[vn], writes=["sq"])
        P.op("vector", "tensor_reduce", reads=["sq"], writes=[("st", 1)], out=st[:, 1, :], in_=s3, axis=AX.X, op=ALU.add)
        P.act(st[:, 2, :], st[:, 1, :], AF.Ln, reads=[("st", 1), "cb"], writes=[("st", 2)], bias=cb[:, 0:1], scale=1.0 / 64)
        P.act(st[:, 2, :], st[:, 2, :], AF.Exp, reads=[("st", 2)], writes=[("st", 2)], scale=-0.5)
        P.tt(v3, v3, st[:, 2, :].unsqueeze(2).to_broadcast([128, TB * 2, 64]), ALU.mult, reads=[vn, ("st", 2)], writes=[vn])
        P.tt(v[:], v[:], lnp_sb[:, 0:128].unsqueeze(1).to_broadcast([128, TB, 128]), ALU.mult, reads=[vn, "lnp"], writes=[vn])
        P.tt(v[:], v[:], lnp_sb[:, 128:256].unsqueeze(1).to_broadcast([128, TB, 128]), ALU.add, reads=[vn, "lnp"], writes=[vn])
        for g in range(2):
            for n in range(TB):
                P.mm(po[g][:, n * 128:(n + 1) * 128], v[:, n, :], w_sb[:, g, :], reads=[vn, "w"], writes=[f"po{g}"])
            pp = slice(64 * g, 64 * g + 64)
            P.tt(y[pp, :].rearrange("p (n i) -> p n i", i=128), po[g][pp, :].rearrange("p (n i) -> p n i", i=128),
                 b_sb[pp, :].unsqueeze(1).to_broadcast([64, TB, 128]), ALU.add, reads=[f"po{g}", "b"], writes=[(yn, g)])
            P.tt(y[pp, :], y[pp, :], u[pp, :], ALU.mult, reads=[(yn, g), un], writes=[(yn, g)])
        P.dma(yT[:, n0 * 128:(n0 + TB) * 128], y[:], reads=[yn], is_output=True)
    return P.finish()


def prep_hgrn(zTb, hp, lb_logits, norm_g):
    T = zTb.shape[1]
    rows = [zTb[q * 512 + 128 * hp:q * 512 + 128 * hp + 128] for q in range(4)]
    zin = np.ascontiguousarray(np.concatenate(rows, 0))
    iT = rows[2]
    vtok = iT.reshape(2, 64, T // 64, 64).transpose(0, 3, 2, 1)
    prm = np.zeros((128, 4), np.float32)
    prm[:, 0] = lb_logits[0, 128 * hp:128 * hp + 128]
    prm[:, 1] = lb_logits[1, 128 * hp:128 * hp + 128]
    prm[:, 2] = norm_g[128 * hp:128 * hp + 128]
    return {"zin": zin, "vtok": np.ascontiguousarray(vtok), "prm": prm}


def prep_rwkv(zTb, hp, e, prm_in, vfirstT=None):
    T = zTb.shape[1]
    f = slice(128 * hp, 128 * hp + 128)
    zin = np.ascontiguousarray(np.concatenate([zTb[q * 512 + 128 * hp:q * 512 + 128 * hp + 128] for q in range(3)], 0))
    lrin = np.ascontiguousarray(zTb[1536:1696])
    mu = prm_in["rwkv_mu"][e]
    prm = np.zeros((128, 16), np.float32)
    prm[:, 0] = mu[0:512][f]; prm[:, 1] = mu[512:1024][f]; prm[:, 2] = mu[1024:1536][f]
    prm[:, 3] = prm_in["rwkv_w0"][e][f]; prm[:, 4] = prm_in["rwkv_a0"][e][f]
    prm[:, 5] = prm_in["rwkv_k_k"][e][f]; prm[:, 6] = prm_in["rwkv_k_a"][e][f]
    prm[:, 7] = prm_in["rwkv_r_k"][e].reshape(512)[f]
    prm[:, 8] = prm_in["rwkv_ln_g"][e][f]; prm[:, 9] = prm_in["rwkv_ln_b"][e][f]
    prm2 = np.zeros((96, 4), np.float32)
    prm2[0:32, 0] = mu[1536:1568]; prm2[0:32, 1] = mu[1568:1600]; prm2[0:96, 2] = mu[1600:1696]
    wlr = np.zeros((96, 4, 128), np.float32)
    wlr[0:32, 0] = prm_in["rwkv_w_up"][e][:, f]; wlr[0:32, 1] = prm_in["rwkv_a_up"][e][:, f]; wlr[0:96, 2] = prm_in["rwkv_g_up"][e][:, f]
    d = {"zin": zin, "lrin": lrin, "prm": prm, "prm2": prm2, "wlr": wlr}
    if e > 0:
        prm[:, 10] = prm_in["rwkv_v0"][e - 1][f]
        wlr[0:32, 3] = prm_in["rwkv_vres_up"][e - 1][:, f]
        d["vlr"] = np.ascontiguousarray(zTb[2720:2752])
        d["vfirst"] = np.ascontiguousarray(vfirstT)
    return d


def prep_gdn(zTb, hd, o, prm_in):
    T = zTb.shape[1]
    base = 2048
    rows = [zTb[base + q * 512 + 128 * hd:base + q * 512 + 128 * hd + 128] for q in range(4)]
    zin = np.ascontiguousarray(np.concatenate(rows, 0))
    brow = zTb[base + 2048 + hd]
    arow = zTb[base + 2052 + hd]
    abrow = np.ascontiguousarray(np.stack([brow, arow], 0))
    abcol = np.ascontiguousarray(np.stack([brow.reshape(T // 64, 64).T, arow.reshape(T // 64, 64).T], 1))
    prm = np.zeros((128, 16), np.float32)
    cw = prm_in["gdn_conv_w"][o]
    for q in range(3):
        prm[:, 4 * q:4 * q + 4] = cw[:, q * 512 + 128 * hd:q * 512 + 128 * hd + 128].T
    prm[:, 12] = prm_in["gdn_a_log"][o][hd]
    prm[:, 13] = prm_in["gdn_dt_bias"][o][hd]
    prm[:, 14] = prm_in["gdn_norm_g"][o]
    return {"zin": zin, "abrow": abrow, "abcol": abcol, "prm": prm}


def prep_sgu(zTb, gp, e, prm_in):
    T = zTb.shape[1]
    base = 1696
    uT = np.ascontiguousarray(zTb[base + 128 * gp:base + 128 * gp + 128])
    vT = zTb[base + 512 + 128 * gp:base + 512 + 128 * gp + 128]
    vtok = np.ascontiguousarray(vT.reshape(128, T // 128, 128).transpose(2, 1, 0))
    f = slice(128 * gp, 128 * gp + 128)
    lnp = np.concatenate([prm_in["sgu_ln_g"][e][f], prm_in["sgu_ln_b"][e][f]])[None, :].astype(np.float32)
    w = prm_in["sgu_w"][e][2 * gp:2 * gp + 2]
    wT = np.ascontiguousarray(w.transpose(0, 2, 1))
    bs = np.ascontiguousarray(prm_in["sgu_b"][e][2 * gp:2 * gp + 2])
    return {"uT": uT, "vtok": vtok, "lnp": np.ascontiguousarray(lnp), "wT": wT, "bs": bs}


_PROGS = {}


def _prog(key, fn):
    if key not in _PROGS:
        _PROGS[key] = fn()
    return _PROGS[key]


def _run(nc, in_maps):
    res = run_bass_kernel_spmd(nc, in_maps, core_ids=list(range(8)))
    return res.results


NTOK = 2048


def _dense_inputs(xTb, yTb, l, p, do_pre, w_in_next, g_pre_next):
    w_o = p["ev_w_out"][l // 2] if l % 2 == 0 else p["od_w_out"][l // 2]
    gvec = np.ascontiguousarray(np.concatenate([colvec(p["norm_mix_post"][l]), colvec(p["norm_ffn_pre"][l]), colvec(p["norm_ffn_post"][l])], 1))
    cw = np.concatenate([p["ffn_conv_w"][l].T, p["ffn_conv_b"][l][:, None]], 1).reshape(NFC, 128, 4).transpose(1, 0, 2)
    cw = np.ascontiguousarray(cw)
    maps = []
    for b in range(2):
        for q in range(4):
            lo = q * NTOK
            if q == 0:
                xs = np.concatenate([np.zeros((D, 2), np.float32), xTb[b][:, 0:NTOK]], 1)
                ys = np.concatenate([np.zeros((D, 2), np.float32), yTb[b][:, 0:NTOK]], 1)
            else:
                xs = xTb[b][:, lo - 2:lo + NTOK]
                ys = yTb[b][:, lo - 2:lo + NTOK]
            m = {"xT": np.ascontiguousarray(xs), "yT": np.ascontiguousarray(ys), "w_o": w_o, "w_f1": p["ffn_w_in"][l], "w_f2": p["ffn_w_out"][l],
                 "gvec": gvec, "cw": cw, "hmask": np.full((128, 1), 0.0 if q == 0 else 1.0, np.float32)}
            if do_pre:
                m["gpre"] = g_pre_next
                m["w_in"] = w_in_next
            maps.append(m)
    return maps


def _w_in(l, p):
    if l % 2 == 1:
        return p["od_w_in"][l // 2]
    e = l // 2
    if e == 0:
        return p["ev_w_in"][0]
    return np.ascontiguousarray(np.concatenate([p["ev_w_in"][e], p["rwkv_vres_down"][e - 1]], 1))


def kernel(**inputs):
    p = {k: np.ascontiguousarray(np.asarray(v, dtype=np.float32)) for k, v in inputs.items()}
    x = p["x"]
    B, T, _ = x.shape
    xTb = [np.ascontiguousarray(x[b].T) for b in range(B)]
    w_in0 = _w_in(0, p)
    nc = _prog(("dense", False, True, w_in0.shape[1]), lambda: build_dense(NTOK, False, True, w_in0.shape[1]))
    maps = [{"xT": np.ascontiguousarray(xTb[b][:, q * NTOK:(q + 1) * NTOK]), "gpre": colvec(p["norm_mix_pre"][0]), "w_in": w_in0}
            for b in range(B) for q in range(4)]
    res = _run(nc, maps)
    zTb = [np.concatenate([res[4 * b + q]["zT"] for q in range(4)], 1) for b in range(B)]
    vfirst = None
    for l in range(4):
        yTb = [np.empty((D, T), np.float32) for _ in range(B)]
        if l % 2 == 0:
            e = l // 2
            nc = _prog(("rwkv", e > 0), lambda: build_rwkv(T, e > 0))
            maps = [prep_rwkv(zTb[b], hp, e, p, None if vfirst is None else vfirst[b][128 * hp:128 * hp + 128]) for b in range(B) for hp in range(4)]
            res = _run(nc, maps)
            for b in range(B):
                for hp in range(4):
                    yTb[b][128 * hp:128 * hp + 128] = res[4 * b + hp]["yT"]
            if e == 0:
                vfirst = [np.concatenate([res[4 * b + hp]["vout"] for hp in range(4)], 0) for b in range(B)]
            nc = _prog(("sgu",), lambda: build_sgu(T))
            maps = [prep_sgu(zTb[b], gp, e, p) for b in range(B) for gp in range(4)]
            res = _run(nc, maps)
            for b in range(B):
                for gp in range(4):
                    yTb[b][512 + 128 * gp:512 + 128 * gp + 128] = res[4 * b + gp]["yT"]
        else:
            o = l // 2
            nc = _prog(("hgrn", o), lambda: build_hgrn(T, o))
            maps = [prep_hgrn(zTb[b], hp, p["hgrn_lb_logits"], p["hgrn_norm_g"][o]) for b in range(B) for hp in range(4)]
            res = _run(nc, maps)
            for b in range(B):
                for hp in range(4):
                    yTb[b][128 * hp:128 * hp + 128] = res[4 * b + hp]["yT"]
            nc = _prog(("gdn",), lambda: build_gdn(T))
            maps = [prep_gdn(zTb[b], hd, o, p) for b in range(B) for hd in range(4)]
            res = _run(nc, maps)
            for b in range(B):
                for hd in range(4):
                    yTb[b][512 + 128 * hd:512 + 128 * hd + 128] = res[4 * b + hd]["yT"]
        do_pre = l < 3
        w_next = _w_in(l + 1, p) if do_pre else None
        C = w_next.shape[1] if do_pre else 0
        nc = _prog(("dense", True, do_pre, C), lambda: build_dense(NTOK, True, do_pre, C))
        maps = _dense_inputs(xTb, yTb, l, p, do_pre, w_next, colvec(p["norm_mix_pre"][l + 1]) if do_pre else None)
        res = _run(nc, maps)
        xTb = [np.concatenate([res[4 * b + q]["xoT"] for q in range(4)], 1) for b in range(B)]
        if do_pre:
            zTb = [np.concatenate([res[4 * b + q]["zT"] for q in range(4)], 1) for b in range(B)]
    return np.ascontiguousarray(np.stack([xTb[b].T for b in range(B)], 0)).astype(np.float32)
```

```python
import numpy as np
from contextlib import ExitStack
import concourse.bass as bass
import concourse.mybir as mybir
from concourse.bass_utils import run_bass_kernel_spmd

F32 = mybir.dt.float32
BF16 = mybir.dt.bfloat16
F32R = mybir.dt.float32r
AF = mybir.ActivationFunctionType
ALU = mybir.AluOpType
AX = mybir.AxisListType

COMPUTE = ("tensor", "vector", "scalar", "gpsimd")
NPOOL = 24
SKIP_SELF = ("tensor",)


class Prog:
    def __init__(self):
        self.nc = bass.Bass("TRN2", target_bir_lowering=False)
        self.es = ExitStack()
        self.ops = {e: [] for e in COMPUTE + ("sync",)}
        self.cnt = {e: 0 for e in COMPUTE}
        self.sem = {}
        for e in COMPUTE:
            self.sem[e] = self.nc.alloc_semaphore("s_" + e)
        self.dpool = {q: [self.nc.alloc_semaphore(f"d_{q}_{i}") for i in range(NPOOL)] for q in ("sync", "gpsimd")}
        self.dcnt = {"sync": 0, "gpsimd": 0}
        self.known = {e: {} for e in COMPUTE + ("sync",)}
        self.lastw = {}
        self.readers = {}
        self.out_tokens = []
        self.nuniq = 0
        self.fast_fp32 = False
        self.defer = None

    def sb(self, name, shape, dtype=F32):
        return self.es.enter_context(self.nc.sbuf_tensor(name, list(shape), dtype))

    def ps(self, name, shape, dtype=F32):
        return self.es.enter_context(self.nc.psum_tensor(name, list(shape), dtype))

    def dram(self, name, shape, dtype=F32, kind="ExternalInput"):
        return self.nc.dram_tensor(name, list(shape), dtype, kind=kind).ap()

    @staticmethod
    def _key(r):
        if isinstance(r, tuple):
            return r[0], r[1]
        return r, None

    def _conf(self, table, name, sub):
        d = table.get(name, {})
        if sub is None:
            return list(d.values())
        out = []
        if sub in d:
            out.append(d[sub])
        if None in d:
            out.append(d[None])
        return out

    def _deps(self, reads, writes):
        toks = []
        for r in reads:
            n, s = self._key(r)
            toks += self._conf(self.lastw, n, s)
        for w in writes:
            n, s = self._key(w)
            toks += self._conf(self.lastw, n, s)
            for lst in self._conf(self.readers, n, s):
                toks += lst
        return toks

    def _commit(self, reads, writes, tok):
        for r in reads:
            n, s = self._key(r)
            self.readers.setdefault(n, {}).setdefault(s, []).append(tok)
        for w in writes:
            n, s = self._key(w)
            if s is None:
                self.lastw[n] = {None: tok}
                self.readers[n] = {}
            else:
                self.lastw.setdefault(n, {})[s] = tok
                self.readers.setdefault(n, {})[s] = []

    def _waits(self, eng, toks, skip_self=False):
        need = {}
        for (sname, sem, val, src) in toks:
            if skip_self and src == eng:
                continue
            if val > need.get(sname, (None, 0))[1]:
                need[sname] = (sem, val)
        out = []
        kn = self.known[eng]
        for sname, (sem, val) in need.items():
            if kn.get(sname, 0) >= val:
                continue
            kn[sname] = val
            out.append((sem, val))
        return out

    def op(self, eng, meth, reads=(), writes=(), **kw):
        if self.defer is not None:
            self.defer.append(("op", eng, meth, tuple(reads), tuple(writes), kw))
            return None
        toks = self._deps(reads, writes)
        waits = self._waits(eng, toks, skip_self=(eng in SKIP_SELF))
        self.cnt[eng] += 1
        tok = ("s_" + eng, self.sem[eng], self.cnt[eng], eng)

        def fn(e, meth=meth, kw=kw):
            return getattr(e, meth)(**kw)

        self.ops[eng].append((waits, fn, (self.sem[eng], 1)))
        self._commit(reads, writes, tok)
        return tok

    def mm(self, out, lhsT, rhs, start=True, stop=True, reads=(), writes=()):
        if self.fast_fp32 and lhsT.dtype == F32 and rhs.dtype == F32:
            lhsT = lhsT.bitcast(F32R)
            rhs = rhs.bitcast(F32R)
        return self.op("tensor", "matmul", reads, writes, out=out, lhsT=lhsT, rhs=rhs, start=start, stop=stop)

    def tr(self, out, in_, identity, reads=(), writes=()):
        return self.op("tensor", "transpose", reads, writes, out=out, in_=in_, identity=identity)

    def act(self, out, in_, func, reads=(), writes=(), **kw):
        return self.op("scalar", "activation", reads, writes, out=out, in_=in_, func=func, **kw)

    def tt(self, out, in0, in1, op, reads=(), writes=(), eng="vector"):
        return self.op(eng, "tensor_tensor", reads, writes, out=out, in0=in0, in1=in1, op=op)

    def stt(self, out, in0, scalar, in1, op0, op1, reads=(), writes=()):
        return self.op("vector", "scalar_tensor_tensor", reads, writes, out=out, in0=in0, scalar=scalar, in1=in1, op0=op0, op1=op1)

    def ts(self, out, in0, scalar1, scalar2, op0, op1=None, reads=(), writes=(), eng="vector"):
        kw = dict(out=out, in0=in0, scalar1=scalar1, scalar2=scalar2, op0=op0)
        if op1 is not None:
            kw["op1"] = op1
        return self.op(eng, "tensor_scalar", reads, writes, **kw)

    def cp(self, out, in_, reads=(), writes=(), eng="vector"):
        if eng == "scalar":
            return self.op("scalar", "copy", reads, writes, out=out, in_=in_)
        return self.op(eng, "tensor_copy", reads, writes, out=out, in_=in_)

    def memset(self, ap, val, writes=(), eng="vector"):
        return self.op(eng, "memset", (), writes, ap=ap, constant=val)

    def record(self, fn, *a):
        assert self.defer is None
        self.defer = []
        fn(*a)
        lst, self.defer = self.defer, None
        return lst

    def run_deferred(self, item):
        if item[0] == "op":
            _, eng, meth, reads, writes, kw = item
            self.op(eng, meth, reads, writes, **kw)
        else:
            _, out, in_, reads, writes, q, is_output, kw = item
            self.dma(out, in_, reads, writes, q, is_output, **kw)

    def dma(self, out, in_, reads=(), writes=(), q="sync", is_output=False, **kw):
        if self.defer is not None:
            self.defer.append(("dma", out, in_, tuple(reads), tuple(writes), q, is_output, kw))
            return None
        toks = self._deps(reads, writes)
        j = self.dcnt[q]
        self.dcnt[q] += 1
        slot, rnd = j % NPOOL, j // NPOOL
        sem = self.dpool[q][slot]
        sname = f"d_{q}_{slot}"
        if rnd > 0:
            toks = toks + [(sname, sem, 16 * rnd, "dma")]
        waits = self._waits(q, toks)
        tok = (sname, sem, 16 * (rnd + 1), "dma")

        def fn(e, out=out, in_=in_, kw=kw):
            return e.dma_start(out=out, in_=in_, **kw)

        self.ops[q].append((waits, fn, (sem, 16)))
        self._commit(reads, writes, tok)
        if is_output:
            self.out_tokens.append(tok)
        return tok

    def finish(self):
        nc = self.nc
        final = list(self.out_tokens)
        for e in COMPUTE:
            if self.cnt[e]:
                final.append(("s_" + e, self.sem[e], self.cnt[e], e))
        fw = self._waits("sync", final)
        ops = self.ops
        with nc.Block() as block:
            def emit(e, lst, extra=()):
                for waits, fn, (sem, inc) in lst:
                    for (ws, wv) in waits:
                        e.wait_ge(ws, wv)
                    fn(e).then_inc(sem, inc)
                for (ws, wv) in extra:
                    e.wait_ge(ws, wv)

            @block.sync
            def _(e):
                emit(e, ops["sync"], fw)

            @block.tensor
            def _(e):
                emit(e, ops["tensor"])

            @block.vector
            def _(e):
                emit(e, ops["vector"])

            @block.scalar
            def _(e):
                emit(e, ops["scalar"])

            @block.gpsimd
            def _(e):
                emit(e, ops["gpsimd"])
        self.es.close()
        return nc


D = 1024
DFF = 2816
NFC = DFF // 128
EPS = 1e-6


class Ring:
    def __init__(self, P, name, n, shape, dtype):
        self.tiles = [P.sb(f"{name}{i}", shape, dtype) for i in range(n)]
        self.names = [f"{name}{i}" for i in range(n)]
        self.i = 0

    def next(self):
        k = self.i % len(self.tiles)
        self.i += 1
        return self.tiles[k], self.names[k]


class PsRing:
    def __init__(self, P, name, n, width=512):
        self.tiles = [P.ps(f"{name}{i}", [128, width], F32) for i in range(n)]
        self.names = [f"{name}{i}" for i in range(n)]
        self.i = 0

    def next(self):
        k = self.i % len(self.tiles)
        self.i += 1
        return self.tiles[k], self.names[k]


def colvec(a):
    a = np.asarray(a, np.float32)
    return np.ascontiguousarray(a.reshape(-1, 128).T)


def build_dense(NT, do_post, do_pre, C):
    P = Prog()
    TM = min(1024, NT)
    nmt = NT // TM
    HAL = 2 if do_post else 0
    W = HAL + NT
    TW = TM + 2
    xT = P.dram("xT", [D, W])
    if do_post:
        yT = P.dram("yT", [D, W])
        w_o = P.dram("w_o", [D, D])
        w_f1 = P.dram("w_f1", [D, 2 * DFF])
        w_f2 = P.dram("w_f2", [DFF, D])
        gvec = P.dram("gvec", [128, 24])
        cw = P.dram("cw", [128, NFC, 4])
        hmask = P.dram("hmask", [128, 1])
        xoT = P.dram("xoT", [D, NT], F32, kind="ExternalOutput")
    if do_pre:
        gpre = P.dram("gpre", [128, 8])
        w_in = P.dram("w_in", [D, C])
        zT = P.dram("zT", [C, NT], F32, kind="ExternalOutput")

    x_sb = P.sb("x_sb", [128, 8, TW], F32)
    h_sb = P.sb("h_sb", [128, 8, TW], BF16)
    ring = Ring(P, "wr", 2, [128, NFC, 512], BF16)
    sq = P.sb("sq", [128, 8, 512], BF16)
    rstd = [P.sb(f"rstd{i}", [128, 512], F32) for i in range(2)]
    ones = P.sb("ones", [128, 128], BF16)
    P.memset(ones[:], 1.0 / D, writes=["ones"])
    pr = PsRing(P, "pp", 6)
    pn = P.ps("pn", [128, 512], F32)
    nrs = [0]
    if do_post:
        f8 = P.sb("f8", [128, 8, TW], F32)
        act = P.sb("act", [128, NFC * TM], BF16)
        act3 = act[:, :].rearrange("p (j t) -> p j t", t=TM)
        y_sb = act[:, 0:8 * TW].rearrange("p (k t) -> p k t", t=TW)
        gb = [P.sb(f"gb{i}", [128, 2 + 512], F32) for i in range(2)]
        cv = [P.sb(f"cv{i}", [128, 512], F32) for i in range(2)]
        ghalo = P.sb("ghalo", [128, NFC, 2], F32)
        gv_sb = P.sb("gv_sb", [128, 24], F32)
        cw_sb = P.sb("cw_sb", [128, NFC, 4], F32)
        hm_sb = P.sb("hm_sb", [128, 1], F32)
        P.dma(gv_sb[:], gvec, writes=["gv_sb"])
        P.dma(cw_sb[:], cw, writes=["cw_sb"])
        P.dma(hm_sb[:], hmask, writes=["hm_sb"])
    if do_pre:
        gp_sb = P.sb("gp_sb", [128, 8], F32)
        P.dma(gp_sb[:], gpre, writes=["gp_sb"])
        zst = Ring(P, "zst", 3, [128, 512], F32)

    xv = xT.rearrange("(k p) t -> p k t", p=128)
    if do_post:
        yv = yT.rearrange("(k p) t -> p k t", p=128)
        xov = xoT.rearrange("(k p) t -> p k t", p=128)

    def rms_rstd(src, sname, lo, hi):
        n = hi - lo
        for m in range(8):
            P.act(sq[:, m, 0:n], src[:, m, lo:hi], AF.Square, reads=[(sname, (m, lo))], writes=[("sq", m)])
        for m in range(8):
            P.mm(pn[:, 0:n], ones[:], sq[:, m, 0:n], start=(m == 0), stop=(m == 7), reads=["ones", ("sq", m)], writes=["pn"])
        r = rstd[nrs[0] % 2]
        rn = f"rstd{nrs[0] % 2}"
        nrs[0] += 1
        P.act(r[:, 0:n], pn[:, 0:n], AF.Ln, reads=["pn"], writes=[rn], bias=EPSB[0][:, 0:1], scale=1.0)
        P.act(r[:, 0:n], r[:, 0:n], AF.Exp, reads=[rn], writes=[rn], scale=-0.5)
        return r, rn

    epsb = P.sb("epsb", [128, 1], F32)
    P.memset(epsb[:], EPS, writes=["epsb"])
    EPSB = [epsb]

    def linear(Wd, kcn, M, in_tile, in_name, subs, consume):
        Wv = Wd.rearrange("(kc p) m -> p kc m", p=128)
        for blk in range(0, M, 512):
            bw = min(512, M - blk)
            wt, wn = ring.next()
            P.dma(wt[:, 0:kcn, 0:bw], Wv[:, :, blk:blk + bw], writes=[wn], q="gpsimd")
            for m0 in range(0, bw, 128):
                mw = min(128, bw - m0)
                for (lo, hi) in subs:
                    n = hi - lo
                    ps, psn = pr.next()
                    for kc in range(kcn):
                        P.mm(ps[0:mw, 0:n], wt[:, kc, m0:m0 + mw], in_tile[:, kc, lo:hi], start=(kc == 0), stop=(kc == kcn - 1),
                             reads=[wn, (in_name, (kc, lo))], writes=[psn])
                    consume((blk + m0) // 128, mw, lo, hi, ps, psn)

    for mt in range(nmt):
        hoff = HAL if mt == 0 else 0
        c0 = 0 if mt == 0 else HAL + mt * TM
        nsub = TM // 512
        if hoff:
            subs = [(0, 2)] + [(2 + i * 512, 2 + (i + 1) * 512) for i in range(nsub)]
        else:
            subs = [(i * 512, (i + 1) * 512) for i in range(nsub)]
        real = [s for s in subs if s[1] - s[0] > 2]

        for (lo, hi) in subs:
            P.dma(x_sb[:, :, lo:hi], xv[:, :, c0 + lo:c0 + hi], writes=[("x", (m, lo)) for m in range(8)])
        if do_post:
            for (lo, hi) in subs:
                P.dma(y_sb[:, :, lo:hi], yv[:, :, c0 + lo:c0 + hi], writes=["a"] + [("y", (m, lo)) for m in range(8)], q="gpsimd")

            def c_mix(m, mw, lo, hi, ps, psn):
                P.cp(f8[:, m, lo:hi], ps[:, 0:hi - lo], reads=[psn], writes=[("f8", (m, lo))], eng="scalar")
            linear(w_o, 8, D, y_sb, "y", subs, c_mix)
            for (lo, hi) in subs:
                n = hi - lo
                r, rn = rms_rstd(f8, "f8", lo, hi)
                for m in range(8):
                    P.stt(f8[:, m, lo:hi], f8[:, m, lo:hi], gv_sb[:, m:m + 1], r[:, 0:n], ALU.mult, ALU.mult,
                          reads=[("f8", (m, lo)), rn, "gv_sb"], writes=[("f8", (m, lo))])
                    P.tt(x_sb[:, m, lo:hi], x_sb[:, m, lo:hi], f8[:, m, lo:hi], ALU.add,
                         reads=[("f8", (m, lo)), ("x", (m, lo))], writes=[("x", (m, lo))])
                r, rn = rms_rstd(x_sb, "x", lo, hi)
                for m in range(8):
                    P.stt(h_sb[:, m, lo:hi], x_sb[:, m, lo:hi], gv_sb[:, 8 + m:9 + m], r[:, 0:n], ALU.mult, ALU.mult,
                          reads=[("x", (m, lo)), rn, "gv_sb"], writes=[("h", (m, lo))])

            Wv1 = w_f1.rearrange("(kc p) m -> p kc m", p=128)
            first_act = True
            ngb = 0
            for blk in range(0, DFF, 512):
                bw = min(512, DFF - blk)
                wt, wn = ring.next()
                P.dma(wt[:, 0:8, 0:bw], Wv1[:, :, blk:blk + bw], writes=[wn], q="gpsimd")
                P.dma(wt[:, 8:16, 0:bw], Wv1[:, :, DFF + blk:DFF + blk + bw], writes=[wn], q="gpsimd")
                for m0 in range(0, bw, 128):
                    j = (blk + m0) // 128
                    for (lo, hi) in subs:
                        n = hi - lo
                        pg, pgn = pr.next()
                        for kc in range(8):
                            P.mm(pg[:, 0:n], wt[:, kc, m0:m0 + 128], h_sb[:, kc, lo:hi], start=(kc == 0), stop=(kc == 7),
                                 reads=[wn, ("h", (kc, lo))], writes=[pgn])
                        if n == 2:
                            P.ts(ghalo[:, j, :], pg[:, 0:2], hm_sb[:, 0:1], None, ALU.mult, reads=[pgn, "hm_sb"], writes=[("ghalo", j)])
                            continue
                        pu, pun = pr.next()
                        for kc in range(8):
                            P.mm(pu[:, 0:n], wt[:, 8 + kc, m0:m0 + 128], h_sb[:, kc, lo:hi], start=(kc == 0), stop=(kc == 7),
                                 reads=[wn, ("h", (kc, lo))], writes=[pun])
                        g_t, gname = gb[ngb % 2], f"gb{ngb % 2}"
                        c_t, cname = cv[ngb % 2], f"cv{ngb % 2}"
                        ngb += 1
                        P.cp(g_t[:, 0:2], ghalo[:, j, :], reads=[("ghalo", j)], writes=[(gname, 0)], eng="gpsimd")
                        P.cp(g_t[:, 2:2 + n], pg[:, 0:n], reads=[pgn], writes=[(gname, 1)], eng="scalar")
                        P.act(c_t[:, 0:n], pg[:, 0:n], AF.Identity, reads=[pgn, "cw_sb"], writes=[cname], scale=cw_sb[:, j, 2:3], bias=cw_sb[:, j, 3:4])
                        P.cp(ghalo[:, j, :], g_t[:, n:n + 2], reads=[(gname, 1)], writes=[("ghalo", j)], eng="gpsimd")
                        P.stt(c_t[:, 0:n], g_t[:, 1:1 + n], cw_sb[:, j, 1:2], c_t[:, 0:n], ALU.mult, ALU.add, reads=[gname, cname, "cw_sb"], writes=[cname])
                        P.stt(c_t[:, 0:n], g_t[:, 0:n], cw_sb[:, j, 0:1], c_t[:, 0:n], ALU.mult, ALU.add, reads=[gname, cname, "cw_sb"], writes=[cname])
                        P.act(c_t[:, 0:n], c_t[:, 0:n], AF.Gelu_apprx_tanh, reads=[cname], writes=[cname])
                        tlo = lo - hoff
                        wr = [("a", (j, tlo))]
                        if first_act:
                            wr.append("y")
                            first_act = False
                        P.tt(act3[:, j, tlo:tlo + n], c_t[:, 0:n], pu[:, 0:n], ALU.mult, reads=[cname, pun], writes=wr)

            rsubs = [(lo - hoff, hi - hoff) for (lo, hi) in real]

            def c_ffo(m, mw, lo, hi, ps, psn):
                P.cp(f8[:, m, lo:hi], ps[:, 0:hi - lo], reads=[psn], writes=[("f8", (m, lo))], eng="scalar")
            linear(w_f2, NFC, D, act3, "a", rsubs, c_ffo)
            for (lo, hi) in rsubs:
                n = hi - lo
                r, rn = rms_rstd(f8, "f8", lo, hi)
                for m in range(8):
                    P.stt(f8[:, m, lo:hi], f8[:, m, lo:hi], gv_sb[:, 16 + m:17 + m], r[:, 0:n], ALU.mult, ALU.mult,
                          reads=[("f8", (m, lo)), rn, "gv_sb"], writes=[("f8", (m, lo))])
                    P.tt(x_sb[:, m, hoff + lo:hoff + hi], x_sb[:, m, hoff + lo:hoff + hi], f8[:, m, lo:hi], ALU.add,
                         reads=[("f8", (m, lo)), ("x", (m, hoff + lo))], writes=[("x", (m, hoff + lo))])
                P.dma(xov[:, :, mt * TM + lo:mt * TM + hi], x_sb[:, :, hoff + lo:hoff + hi], reads=[("x", (m, hoff + lo)) for m in range(8)], is_output=True)
        if do_pre:
            ph = hoff if do_post else 0
            for (lo, hi) in real:
                n = hi - lo
                r, rn = rms_rstd(x_sb, "x", lo, hi)
                for m in range(8):
                    P.stt(h_sb[:, m, lo:hi], x_sb[:, m, lo:hi], gp_sb[:, m:m + 1], r[:, 0:n], ALU.mult, ALU.mult,
                          reads=[("x", (m, lo)), rn, "gp_sb"], writes=[("h", (m, lo))])

            def c_z(m, mw, lo, hi, ps, psn):
                n = hi - lo
                zt, zn = zst.next()
                P.cp(zt[0:mw, 0:n], ps[0:mw, 0:n], reads=[psn], writes=[zn], eng="scalar")
                oc = mt * TM + lo - ph
                P.dma(zT[m * 128:m * 128 + mw, oc:oc + n], zt[0:mw, 0:n], reads=[zn], is_output=True)
            linear(w_in, 8, C, h_sb, "h", real, c_z)
    return P.finish()


def make_consts(P, need_strict=False):
    c = {}
    ident = P.sb("ident", [128, 128], F32)
    P.memset(ident[:], 1.0, writes=["ident"], eng="gpsimd")
    P.op("gpsimd", "affine_select", reads=["ident"], writes=["ident"], out=ident[:], in_=ident[:], pattern=[[1, 128]],
         compare_op=ALU.is_equal, fill=0.0, base=0, channel_multiplier=-1)
    c["ident"] = ident
    mi = P.sb("mask_i", [128, 128], F32)
    P.memset(mi[:], 1.0, writes=["mask_i"], eng="gpsimd")
    P.op("gpsimd", "affine_select", reads=["mask_i"], writes=["mask_i"], out=mi[:], in_=mi[:], pattern=[[1, 128]],
         compare_op=ALU.is_ge, fill=0.0, base=0, channel_multiplier=-1)
    P.memset(mi[0:64, 64:128], 0.0, writes=["mask_i"], eng="gpsimd")
    c["mask_i"] = mi
    if need_strict:
        ms = P.sb("mask_s", [128, 128], F32)
        P.memset(ms[:], 1.0, writes=["mask_s"], eng="gpsimd")
        P.op("gpsimd", "affine_select", reads=["mask_s"], writes=["mask_s"], out=ms[:], in_=ms[:], pattern=[[1, 128]],
             compare_op=ALU.is_gt, fill=0.0, base=0, channel_multiplier=-1)
        P.memset(ms[0:64, 64:128], 0.0, writes=["mask_s"], eng="gpsimd")
        c["mask_s"] = ms
    ob = P.sb("ones_bd", [128, 128], F32)
    P.memset(ob[:], 0.0, writes=["ones_bd"], eng="gpsimd")
    P.memset(ob[0:64, 0:64], 1.0 / 64, writes=["ones_bd"], eng="gpsimd")
    P.memset(ob[64:128, 64:128], 1.0 / 64, writes=["ones_bd"], eng="gpsimd")
    c["ones_bd"] = ob
    cm = P.sb("cmask", [128, 8, 64], F32)
    P.memset(cm[:], 1.0, writes=["cmask"], eng="gpsimd")
    P.memset(cm[:, :, 0:1], 0.0, writes=["cmask"], eng="gpsimd")
    c["cmask"] = cm
    return c


def zipper(a, b, ra=1, rb=1):
    da = db = False
    while not (da and db):
        for _ in range(ra):
            if not da:
                try:
                    next(a)
                except StopIteration:
                    da = True
        for _ in range(rb):
            if not db:
                try:
                    next(b)
                except StopIteration:
                    db = True


def _adv(gen, n):
    for _ in range(n):
        try:
            next(gen)
        except StopIteration:
            return True
    return False


def pipeline3(P, seq, genA, genB, prep, post, NCH, ntile):
    n = len(seq)
    gens = {}
    bg = []

    def drain_bg():
        while bg:
            P.run_deferred(bg.pop(0))

    def start(g):
        if g < n:
            if seq[g][1] == 0:
                drain_bg()
            gens[g] = genA(seq[g][0], seq[g][1], g)

    prep(0)
    if ntile > 1:
        prep(1)
    start(0)
    while not _adv(gens[0], 1000):
        pass
    start(1)
    for g, (ti, c) in enumerate(seq):
        start(g + 2)
        b = genB(ti, c, g)
        a1 = gens.get(g + 1)
        a2 = gens.get(g + 2)
        d1 = a1 is None
        db = False
        while not (d1 and db):
            if not d1:
                d1 = _adv(a1, 2)
            if a2 is not None:
                if _adv(a2, 1):
                    a2 = None
            if not db:
                db = _adv(b, 1)
            for _ in range(2):
                if bg:
                    P.run_deferred(bg.pop(0))
        gens.pop(g, None)
        if c == NCH - 1:
            bg.extend(P.record(post, ti))
            if ti + 2 < ntile:
                bg.extend(P.record(prep, ti + 2))
    drain_bg()


def bd_write(P, dst, dname, src, sname, mul, mname, nch, extra_reads=()):
    for h in range(2):
        pp = slice(64 * h, 64 * h + 64)
        s3 = src[pp, 0:nch * 64].rearrange("p (c t) -> p c t", t=64)
        m3 = mul[pp, 0:nch * 64].rearrange("p (c t) -> p c t", t=64)
        P.tt(dst[pp, 0:nch, 64 * h:64 * h + 64], s3, m3, ALU.mult, reads=[sname, mname] + list(extra_reads), writes=[(dname, h)],
             eng=("vector" if h == 0 else "gpsimd"))


def build_hgrn(T, layer_o):
    P = Prog()
    TT = 512
    NCH = TT // 64
    ntile = T // TT
    zin = P.dram("zin", [512, T])
    vtok = P.dram("vtok", [2, 64, T // 64, 64])
    prm = P.dram("prm", [128, 4])
    yT = P.dram("yT", [128, T], F32, kind="ExternalOutput")
    C = make_consts(P)
    prm_sb = P.sb("prm_sb", [128, 4], F32)
    P.dma(prm_sb[:], prm, writes=["prm"])
    lbv = P.sb("lbv", [128, 2], F32)
    if layer_o == 0:
        P.memset(lbv[:, 0:1], 0.0, writes=["lbv"])
        P.memset(lbv[:, 1:2], 1.0, writes=["lbv"])
    else:
        P.tt(lbv[:, 0:1], prm_sb[:, 1:2], prm_sb[:, 0:1], ALU.subtract, reads=["prm"], writes=["lbv"])
        P.act(lbv[:, 0:1], lbv[:, 0:1], AF.Sigmoid, reads=["lbv"], writes=["lbv"])
        P.ts(lbv[:, 1:2], lbv[:, 0:1], -1.0, 1.0, ALU.mult, ALU.add, reads=["lbv"], writes=["lbv"])
    epsb = P.sb("epsb", [128, 1], F32)
    P.memset(epsb[:], EPS, writes=["epsb"])

    zv = zin.rearrange("(q p) t -> p q t", p=128)
    NB = 2
    z_sb = [P.sb(f"z_sb{i}", [128, 4, TT], F32) for i in range(NB)]
    f_sb = [P.sb(f"f_sb{i}", [128, TT], F32) for i in range(NB)]
    k_sb = [P.sb(f"k_sb{i}", [128, TT], F32) for i in range(NB)]
    b_sb = [P.sb(f"b_sb{i}", [128, TT], F32) for i in range(NB)]
    eb_sb = [P.sb(f"eb_sb{i}", [128, TT], F32) for i in range(NB)]
    enb_sb = [P.sb(f"enb_sb{i}", [128, TT], F32) for i in range(NB)]
    qF = [P.sb(f"qF{i}", [128, NCH, 128], F32) for i in range(NB)]
    kF = [P.sb(f"kF{i}", [128, NCH, 128], F32) for i in range(NB)]
    Vb = [P.sb(f"Vb{i}", [128, NCH, 128], F32) for i in range(NB)]
    y_sb = [P.sb(f"y_sb{i}", [128, TT], F32) for i in range(NB)]
    for i in range(NB):
        for t_, n_ in ((qF[i], f"qF{i}"), (kF[i], f"kF{i}"), (Vb[i], f"Vb{i}")):
            P.memset(t_[:], 0.0, writes=[n_], eng="gpsimd")
    Tst = [P.sb(f"Tst{i}", [128, 128], F32) for i in range(2)]
    P.memset(Tst[0][:], 0.0, writes=["Tst0"])
    tmpT = P.sb("tmpT", [128, 128], F32)
    MTm = [P.sb(f"MTm{i}", [128, 128], F32) for i in range(2)]
    kTok = [P.sb(f"kTok{i}", [128, 128], F32) for i in range(2)]
    p_mt = [P.ps(f"p_mt{i}", [128, 512], F32) for i in range(2)]
    p_tr = [P.ps(f"p_tr{i}", [128, 512], F32) for i in range(2)]
    p_su = P.ps("p_su", [128, 512], F32)
    p_o = [P.ps(f"p_o{i}", [128, 512], F32) for i in range(2)]
    p_n = P.ps("p_n", [128, 512], F32)
    MT3 = [P.sb(f"MT3_{i}", [128, 128], F32) for i in range(3)]
    SU3 = [P.sb(f"SU3_{i}", [128, 128], F32) for i in range(3)]
    kTk = [P.sb(f"kTk{i}", [128, 128], F32) for i in range(2)]

    def prep(ti):
        i = ti % NB
        t0 = ti * TT
        zs, zn = z_sb[i], f"z{i}"
        P.dma(zs[:], zv[:, :, t0:t0 + TT], writes=[zn])
        c0 = ti * NCH
        for h in range(2):
            P.dma(Vb[i][64 * h:64 * h + 64, :, 64 * h:64 * h + 64], vtok[h, :, c0:c0 + NCH, :], writes=[(f"Vb{i}", h)], q="gpsimd")
        f, fn = f_sb[i], f"f{i}"
        P.act(f[:], zs[:, 1, :], AF.Sigmoid, reads=[zn], writes=[fn])
        P.ts(f[:], f[:], lbv[:, 1:2], lbv[:, 0:1], ALU.mult, ALU.add, reads=[fn, "lbv"], writes=[fn])
        k, kn = k_sb[i], f"k{i}"
        P.ts(k[:], f[:], -1.0, 1.0, ALU.mult, ALU.add, reads=[fn], writes=[kn])
        P.act(f[:], f[:], AF.Ln, reads=[fn], writes=[fn])
        bb, bn = b_sb[i], f"b{i}"
        P.op("vector", "tensor_tensor_scan", reads=[fn, "cmask"], writes=[bn], out=bb[:], data0=C["cmask"][:].rearrange("p c t -> p (c t)"),
             data1=f[:], initial=0.0, op0=ALU.mult, op1=ALU.add)
        eb, ebn = eb_sb[i], f"eb{i}"
        enb, enbn = enb_sb[i], f"enb{i}"
        P.act(eb[:], bb[:], AF.Exp, reads=[bn], writes=[ebn])
        P.act(enb[:], bb[:], AF.Exp, reads=[bn], writes=[enbn], scale=-1.0)
        P.act(zs[:, 0, :], zs[:, 0, :], AF.Silu, reads=[zn], writes=[zn])
        P.act(zs[:, 3, :], zs[:, 3, :], AF.Silu, reads=[zn], writes=[zn])
        bd_write(P, qF[i], f"qF{i}", zs[:, 0, :], zn, eb, ebn, NCH)
        bd_write(P, kF[i], f"kF{i}", k, kn, enb, enbn, NCH)

    def genA(ti, c, g):
        i = ti % NB
        j = g % 3
        s_ = g % 2
        pm, pmn = p_mt[s_], f"p_mt{s_}"
        ptr, ptrn = p_tr[s_], f"p_tr{s_}"
        kt, ktn = kTk[s_], f"kTk{s_}"
        P.mm(pm[:, 0:128], kF[i][:, c, :], qF[i][:, c, :], reads=[f"kF{i}", f"qF{i}"], writes=[pmn]); yield
        P.tt(MT3[j][:], pm[:, 0:128], C["mask_i"][:], ALU.mult, reads=[pmn, "mask_i"], writes=[f"MT3_{j}"]); yield
        P.tr(ptr[:, 0:128], kF[i][:, c, :], C["ident"][:], reads=[f"kF{i}", "ident"], writes=[ptrn]); yield
        P.cp(kt[:], ptr[:, 0:128], reads=[ptrn], writes=[ktn], eng="scalar"); yield
        P.mm(ptr[:, 128:256], kt[:], Vb[i][:, c, :], reads=[ktn, f"Vb{i}"], writes=[ptrn]); yield
        P.cp(SU3[j][:], ptr[:, 128:256], reads=[ptrn], writes=[f"SU3_{j}"], eng="scalar"); yield

    def genB(ti, c, g):
        i = ti % NB
        j = g % 3
        Tc, Tcn = Tst[g % 2], f"Tst{g % 2}"
        Tn, Tnn = Tst[1 - g % 2], f"Tst{1 - g % 2}"
        y, yn = y_sb[i], f"y{i}"
        po, pon = p_o[g % 2], f"p_o{g % 2}"
        pl = eb_sb[i][:, c * 64 + 63:c * 64 + 64]
        P.tt(tmpT[:], Tc[:], SU3[j][:], ALU.add, reads=[Tcn, f"SU3_{j}"], writes=["tmpT"]); yield
        P.ts(Tn[:], tmpT[:], pl, None, ALU.mult, reads=["tmpT", f"eb{i}"], writes=[Tnn]); yield
        P.mm(po[:, 0:128], Tc[:], qF[i][:, c, :], start=True, stop=False, reads=[Tcn, f"qF{i}"], writes=[pon]); yield
        P.mm(po[:, 0:128], Vb[i][:, c, :], MT3[j][:], start=False, stop=True, reads=[f"Vb{i}", f"MT3_{j}"], writes=[pon]); yield
        for h in range(2):
            P.cp(y[64 * h:64 * h + 64, c * 64:c * 64 + 64], po[64 * h:64 * h + 64, 64 * h:64 * h + 64], reads=[pon], writes=[(yn, (c, h))], eng="scalar"); yield

    def post(ti):
        i = ti % NB
        t0 = ti * TT
        zs, zn = z_sb[i], f"z{i}"
        k, kn = k_sb[i], f"k{i}"
        y, yn = y_sb[i], f"y{i}"
        P.act(tmpY[:], y[:], AF.Square, reads=[yn], writes=["tmpY"])
        P.mm(p_n[:, 0:TT], C["ones_bd"][:], tmpY[:], reads=["ones_bd", "tmpY"], writes=["p_n"])
        P.act(tmpY[:], p_n[:, 0:TT], AF.Ln, reads=["p_n"], writes=["tmpY"], bias=epsb[:, 0:1], scale=1.0)
        P.act(tmpY[:], tmpY[:], AF.Exp, reads=["tmpY"], writes=["tmpY"], scale=-0.5)
        P.stt(y[:], y[:], prm_sb[:, 2:3], tmpY[:], ALU.mult, ALU.mult, reads=[yn, "tmpY", "prm"], writes=[yn])
        P.tt(y[:], y[:], zs[:, 3, :], ALU.mult, reads=[yn, zn], writes=[yn])
        P.dma(yT[:, t0:t0 + TT], y[:], reads=[yn], is_output=True)

    tmpY = P.sb("tmpY", [128, TT], F32)
    seq = [(ti, c) for ti in range(ntile) for c in range(NCH)]
    pipeline3(P, seq, genA, genB, prep, post, NCH, ntile)
    return P.finish()


DEC = 0.6065306597126334


def build_rwkv(T, has_vres, dbg=0):
    P = Prog()
    TT = 512
    NCH = TT // 64
    ntile = T // TT
    zin = P.dram("zin", [384, T])
    lrin = P.dram("lrin", [160, T])
    prm = P.dram("prm", [128, 16])
    prm2 = P.dram("prm2", [96, 4])
    wlr = P.dram("wlr", [96, 4, 128])
    if has_vres:
        vlr = P.dram("vlr", [32, T])
        vfirst = P.dram("vfirst", [128, T])
    yT = P.dram("yT", [128, T], F32, kind="ExternalOutput")
    vout = P.dram("vout", [128, T], F32, kind="ExternalOutput")
    C = make_consts(P, need_strict=True)
    mA = P.sb("mA", [128, 256], F32)
    mK = P.sb("mK", [128, 256], F32)
    P.ts(mA[:, 0:128], C["mask_s"][:], -1.0, None, ALU.mult, reads=["mask_s"], writes=["mA"])
    P.cp(mA[:, 128:256], C["mask_i"][:], reads=["mask_i"], writes=["mA"])
    P.cp(mK[:, 0:128], C["mask_s"][:], reads=["mask_s"], writes=["mK"])
    P.cp(mK[:, 128:256], C["mask_i"][:], reads=["mask_i"], writes=["mK"])
    prm_sb = P.sb("prm_sb", [128, 16], F32)
    prm2_sb = P.sb("prm2_sb", [96, 4], F32)
    wlr_sb = P.sb("wlr_sb", [96, 4, 128], F32)
    P.dma(prm_sb[:], prm, writes=["prm"])
    P.dma(prm2_sb[:], prm2, writes=["prm2"])
    P.dma(wlr_sb[:], wlr, writes=["wlr"])
    omk = P.sb("omk", [128, 1], F32)
    P.ts(omk[:], prm_sb[:, 6:7], -1.0, 1.0, ALU.mult, ALU.add, reads=["prm"], writes=["omk"])
    cb = P.sb("cb", [128, 2], F32)
    P.memset(cb[:, 0:1], 1e-6, writes=["cb"])
    P.memset(cb[:, 1:2], 64e-5, writes=["cb"])

    def col(j):
        return prm_sb[:, j:j + 1]

    zv = zin.rearrange("(q p) t -> p q t", p=128)
    z_sb = [P.sb(f"z_sb{i}", [128, 3, 1 + TT], F32) for i in range(2)]
    l_sb = [P.sb(f"l_sb{i}", [96, 3, 1 + TT], F32) for i in range(2)]
    zl = [P.sb(f"zl{i}", [128, 3, TT], F32) for i in range(2)]
    ll = P.sb("ll", [96, 3, TT], F32)
    if has_vres:
        vl_sb = P.sb("vl_sb", [32, TT], F32)
        vf_sb = P.sb("vf_sb", [128, TT], F32)
    names1 = ["sw", "bb", "bx", "a_s", "kkr", "t1", "kk", "kmod", "alpha", "enb", "ebx"]
    S1 = {n: P.sb("s_" + n, [128, TT], F32) for n in names1}
    eb = [P.sb(f"eb{i}", [128, TT], F32) for i in range(2)]
    g_sb = [P.sb(f"g_sb{i}", [128, TT], F32) for i in range(2)]
    bv = [P.sb(f"bv{i}", [128, TT], F32) for i in range(2)]
    y_sb = [P.sb(f"y_sb{i}", [128, TT], F32) for i in range(2)]
    aF = [P.sb(f"aF{i}", [128, NCH, 128], F32) for i in range(2)]
    kF = [P.sb(f"kF{i}", [128, NCH, 128], F32) for i in range(2)]
    cq = [P.sb(f"cq{i}", [128, NCH, 256], F32) for i in range(2)]
    vTb = P.sb("vTb", [128, NCH, 128], F32)
    Vb = [P.sb(f"Vb{i}", [128, NCH, 128], F32) for i in range(2)]
    aTok = [P.sb(f"aTok{i}", [128, NCH, 128], F32) for i in range(2)]
    kTok = [P.sb(f"kTok{i}", [128, NCH, 128], F32) for i in range(2)]
    for i in range(2):
        for t_, n_ in ((aF[i], f"aF{i}"), (kF[i], f"kF{i}"), (cq[i], f"cq{i}")):
            P.memset(t_[:], 0.0, writes=[n_], eng="gpsimd")
    P.memset(vTb[:], 0.0, writes=["vTb"], eng="gpsimd")
    Tst = [P.sb(f"Tst{i}", [128, 128], F32) for i in range(2)]
    P.memset(Tst[0][:], 0.0, writes=["Tst0"])
    tmpT = P.sb("tmpT", [128, 128], F32)
    MA = [P.sb(f"MA{i}", [128, 256], F32) for i in range(2)]
    MK = [P.sb(f"MK{i}", [128, 256], F32) for i in range(2)]
    YY = [P.sb(f"YY{i}", [128, 256], F32) for i in range(2)]
    PI = [P.sb(f"PI{i}", [128, 128], F32) for i in range(2)]
    W0s = P.sb("W0s", [128, 128], F32)
    Us = P.sb("Us", [128, 128], F32)
    pA = P.ps("pA", [128, 512], F32)
    pB = P.ps("pB", [128, 512], F32)
    pMA = P.ps("pMA", [128, 512], F32)
    pPi = P.ps("pPi", [128, 512], F32)
    pY = P.ps("pY", [128, 512], F32)
    pW = P.ps("pW", [128, 512], F32)
    pU = P.ps("pU", [128, 512], F32)
    pO = P.ps("pO", [128, 512], F32)

    def prep(ti):
        i = ti % 2
        t0 = ti * TT
        zs, zn = z_sb[i], f"z{i}"
        ls, ln_ = l_sb[i], f"l{i}"
        if ti == 0:
            P.memset(zs[:, :, 0:1], 0.0, writes=[zn])
            P.memset(ls[:, :, 0:1], 0.0, writes=[ln_])
            P.dma(zs[:, :, 1:1 + TT], zv[:, :, 0:TT], writes=[zn])
            P.dma(ls[0:32, 0, 1:1 + TT], lrin[0:32, 0:TT], writes=[ln_])
            P.dma(ls[0:32, 1, 1:1 + TT], lrin[32:64, 0:TT], writes=[ln_])
            P.dma(ls[0:96, 2, 1:1 + TT], lrin[64:160, 0:TT], writes=[ln_])
        else:
            P.dma(zs[:], zv[:, :, t0 - 1:t0 + TT], writes=[zn])
            P.dma(ls[0:32, 0, :], lrin[0:32, t0 - 1:t0 + TT], writes=[ln_])
            P.dma(ls[0:32, 1, :], lrin[32:64, t0 - 1:t0 + TT], writes=[ln_])
            P.dma(ls[0:96, 2, :], lrin[64:160, t0 - 1:t0 + TT], writes=[ln_])
        z, zln = zl[i], f"zl{i}"
        P.tt(z[:], zs[:, :, 0:TT], zs[:, :, 1:1 + TT], ALU.subtract, reads=[zn], writes=[zln])
        for q in range(3):
            P.stt(z[:, q, :], z[:, q, :], col(q), zs[:, q, 1:1 + TT], ALU.mult, ALU.add, reads=[zln, zn, "prm"], writes=[zln])
        for q, rows in ((0, 32), (1, 32), (2, 96)):
            P.tt(ll[0:rows, q, :], ls[0:rows, q, 0:TT], ls[0:rows, q, 1:1 + TT], ALU.subtract, reads=[ln_], writes=["ll"])
            P.stt(ll[0:rows, q, :], ll[0:rows, q, :], prm2_sb[0:rows, q:q + 1], ls[0:rows, q, 1:1 + TT], ALU.mult, ALU.add, reads=["ll", ln_, "prm2"], writes=["ll"])
        r_, k_, v_ = z[:, 0, :], z[:, 1, :], z[:, 2, :]
        if has_vres:
            P.dma(vl_sb[:], vlr[:, t0:t0 + TT], writes=["vl"])
            P.dma(vf_sb[:], vfirst[:, t0:t0 + TT], writes=["vf"])
            P.mm(pA[:, 0:TT], wlr_sb[0:32, 3, :], vl_sb[:], reads=["wlr", "vl"], writes=["pA"])
            P.act(S1["t1"][:], pA[:, 0:TT], AF.Sigmoid, reads=["pA", "prm"], writes=["t1"], bias=col(10), scale=1.0)
            P.tt(vf_sb[:], vf_sb[:], v_, ALU.subtract, reads=["vf", zln], writes=["vf"])
            P.tt(vf_sb[:], vf_sb[:], S1["t1"][:], ALU.mult, reads=["vf", "t1"], writes=["vf"])
            P.tt(v_, v_, vf_sb[:], ALU.add, reads=["vf", zln], writes=[zln])
        P.dma(vout[:, t0:t0 + TT], v_, reads=[zln], is_output=True)
        P.act(ll[0:32, 0, :], ll[0:32, 0, :], AF.Tanh, reads=["ll"], writes=["ll"])
        P.mm(pA[:, 0:TT], wlr_sb[0:32, 0, :], ll[0:32, 0, :], reads=["wlr", "ll"], writes=["pA"])
        P.act(S1["sw"][:], pA[:, 0:TT], AF.Sigmoid, reads=["pA", "prm"], writes=["sw"], bias=col(3), scale=1.0)
        P.op("vector", "tensor_tensor_scan", reads=["sw", "cmask"], writes=["bb"], out=S1["bb"][:], data0=C["cmask"][:].rearrange("p c t -> p (c t)"),
             data1=S1["sw"][:], initial=0.0, op0=ALU.mult, op1=ALU.add)
        P.tt(S1["bx"][:], S1["bb"][:], S1["sw"][:], ALU.subtract, reads=["bb", "sw"], writes=["bx"])
        e_, en = eb[i], f"eb{i}"
        P.act(e_[:], S1["bb"][:], AF.Exp, reads=["bb"], writes=[en], scale=-DEC)
        P.act(S1["enb"][:], S1["bb"][:], AF.Exp, reads=["bb"], writes=["enb"], scale=DEC)
        P.act(S1["ebx"][:], S1["bx"][:], AF.Exp, reads=["bx"], writes=["ebx"], scale=-DEC)
        P.mm(pB[:, 0:TT], wlr_sb[0:32, 1, :], ll[0:32, 1, :], reads=["wlr", "ll"], writes=["pB"])
        P.act(S1["a_s"][:], pB[:, 0:TT], AF.Sigmoid, reads=["pB", "prm"], writes=["a_s"], bias=col(4), scale=1.0)
        P.act(ll[0:96, 2, :], ll[0:96, 2, :], AF.Sigmoid, reads=["ll"], writes=["ll"])
        P.mm(pA[:, 0:TT], wlr_sb[0:96, 2, :], ll[0:96, 2, :], reads=["wlr", "ll"], writes=["pA"])
        P.cp(g_sb[i][:], pA[:, 0:TT], reads=["pA"], writes=[f"g{i}"], eng="scalar")
        P.ts(S1["kkr"][:], k_, col(5), None, ALU.mult, reads=[zln, "prm"], writes=["kkr"])
        P.act(S1["t1"][:], S1["kkr"][:], AF.Square, reads=["kkr"], writes=["t1"])
        P.mm(pB[:, 0:TT], C["ones_bd"][:], S1["t1"][:], reads=["ones_bd", "t1"], writes=["pB"])
        P.act(S1["t1"][:], pB[:, 0:TT], AF.Ln, reads=["pB", "cb"], writes=["t1"], bias=cb[:, 0:1], scale=64.0)
        P.act(S1["t1"][:], S1["t1"][:], AF.Exp, reads=["t1"], writes=["t1"], scale=-0.5)
        P.tt(S1["kk"][:], S1["kkr"][:], S1["t1"][:], ALU.mult, reads=["kkr", "t1"], writes=["kk"])
        P.ts(S1["kmod"][:], S1["a_s"][:], col(6), omk[:, 0:1], ALU.mult, ALU.add, reads=["a_s", "prm", "omk"], writes=["kmod"])
        P.tt(S1["kmod"][:], S1["kmod"][:], k_, ALU.mult, reads=["kmod", zln], writes=["kmod"])
        P.tt(S1["alpha"][:], S1["kk"][:], S1["a_s"][:], ALU.mult, reads=["kk", "a_s"], writes=["alpha"])
        P.stt(S1["t1"][:], r_, col(7), S1["kmod"][:], ALU.mult, ALU.mult, reads=[zln, "prm", "kmod"], writes=["t1"])
        P.mm(pA[:, 0:TT], C["ones_bd"][:], S1["t1"][:], reads=["ones_bd", "t1"], writes=["pA"])
        P.stt(bv[i][:], pA[:, 0:TT], 64.0, v_, ALU.mult, ALU.mult, reads=["pA", zln], writes=[f"bv{i}"])
        bd_write(P, aF[i], f"aF{i}", S1["alpha"], "alpha", S1["enb"], "enb", NCH)
        bd_write(P, kF[i], f"kF{i}", S1["kmod"], "kmod", S1["enb"], "enb", NCH)
        bd_write(P, cq[i][:, :, 0:128], f"cq{i}", S1["kk"], "kk", S1["ebx"], "ebx", NCH)
        bd_write(P, cq[i][:, :, 128:256], f"cq{i}", z[:, 0, :], zln, e_, en, NCH)
        for h in range(2):
            pp = slice(64 * h, 64 * h + 64)
            P.cp(vTb[pp, :, 64 * h:64 * h + 64], z[pp, 2, :].rearrange("p (c t) -> p c t", t=64), reads=[zln], writes=[("vTb", h)])
        for (src, sn, dst, dn) in ((vTb, "vTb", Vb[i], f"Vb{i}"), (aF[i], f"aF{i}", aTok[i], f"aTok{i}"), (kF[i], f"kF{i}", kTok[i], f"kTok{i}")):
            for half in range(2):
                pt, ptn = (pA, "pA") if half == 0 else (pB, "pB")
                for cc in range(4):
                    c = half * 4 + cc
                    P.tr(pt[:, cc * 128:(cc + 1) * 128], src[:, c, :], C["ident"][:], reads=[sn, "ident"], writes=[ptn])
                P.cp(dst[:, half * 4:half * 4 + 4, :], pt[:, :].rearrange("p (c t) -> p c t", t=128), reads=[ptn], writes=[(dn, half)], eng="scalar")

    PIF = [P.sb(f"PIF{i}", [128, 128], F32) for i in range(3)]
    MA3 = [P.sb(f"MA3_{i}", [128, 256], F32) for i in range(3)]
    MK3 = [P.sb(f"MK3_{i}", [128, 256], F32) for i in range(3)]
    YYs = [[P.sb(f"YYs{s_}_{i}", [128, 256], F32) for i in range(2)] for s_ in range(2)]
    PIs = [[P.sb(f"PIs{s_}_{i}", [128, 128], F32) for i in range(2)] for s_ in range(2)]
    pYs = [pMA, pY]
    pPs = [pPi, pW]

    def genA(ti, c, g):
        i = ti % 2
        j = g % 3
        s_ = g % 2
        pYb, pYn = pYs[s_], f"pYs{s_}"
        pPb, pPn = pPs[s_], f"pPs{s_}"
        ma, man = MA3[j], f"MA3_{j}"
        mk, mkn = MK3[j], f"MK3_{j}"
        P.mm(pPb[:, 0:256], aF[i][:, c, :], cq[i][:, c, :], reads=[f"aF{i}", f"cq{i}"], writes=[pPn]); yield
        P.mm(pPb[:, 256:512], kF[i][:, c, :], cq[i][:, c, :], reads=[f"kF{i}", f"cq{i}"], writes=[pPn]); yield
        P.tt(ma[:], pPb[:, 0:256], mA[:], ALU.mult, reads=[pPn, "mA"], writes=[man]); yield
        P.tt(mk[:], pPb[:, 256:512], mK[:], ALU.mult, reads=[pPn, "mK"], writes=[mkn]); yield
        yt, ytn = YYs[s_][0], f"YYs{s_}_0"
        P.tr(pYb[:, 0:128], ma[:, 0:128], C["ident"][:], reads=[man, "ident"], writes=[pYn]); yield
        P.cp(yt[:, 128:256], pYb[:, 0:128], reads=[pYn], writes=[ytn], eng="scalar"); yield
        Yap, Yn_ = ma[:, 0:128], man
        YTap, YTn = yt[:, 128:256], ytn
        pi, pin = PIs[s_][0], f"PIs{s_}_0"
        P.tt(pi[:], ma[:, 0:128], C["ident"][:], ALU.add, reads=[man, "ident"], writes=[pin]); yield
        for k in range(1, 6):
            y2, y2n = YYs[s_][k % 2], f"YYs{s_}_{k % 2}"
            if k < 5:
                P.mm(pYb[:, 0:128], YTap, Yap, reads=[Yn_, YTn], writes=[pYn]); yield
            P.mm(pYb[:, 128:256], Yap, YTap, reads=[Yn_, YTn], writes=[pYn]); yield
            if k < 5:
                P.cp(y2[:], pYb[:, 0:256], reads=[pYn], writes=[y2n], eng="scalar"); yield
            else:
                P.cp(y2[:, 128:256], pYb[:, 128:256], reads=[pYn], writes=[y2n], eng="scalar"); yield
            if k < 5:
                pi2, pi2n = PIs[s_][k % 2], f"PIs{s_}_{k % 2}"
            else:
                pi2, pi2n = PIF[j], f"PIF{j}"
            P.mm(pPb[:, 0:128], y2[:, 128:256], pi[:], reads=[y2n, pin], writes=[pPn]); yield
            P.tt(pi2[:], pPb[:, 0:128], pi[:], ALU.add, reads=[pPn, pin], writes=[pi2n]); yield
            Yap, Yn_, YTap, YTn, pi, pin = y2[:, 0:128], y2n, y2[:, 128:256], y2n, pi2, pi2n

    def genB(ti, c, g):
        i = ti % 2
        j = g % 3
        y, yn = y_sb[i], f"y{i}"
        Tc, Tcn = Tst[g % 2], f"Tst{g % 2}"
        Tn, Tnn = Tst[1 - g % 2], f"Tst{1 - g % 2}"
        ma, man = MA3[j], f"MA3_{j}"
        mk, mkn = MK3[j], f"MK3_{j}"
        pi, pin = PIF[j], f"PIF{j}"
        pl = eb[i][:, c * 64 + 63:c * 64 + 64]
        P.ts(tmpT[:], Tc[:], pl, None, ALU.mult, reads=[Tcn, f"eb{i}"], writes=["tmpT"]); yield
        P.mm(pU[:, 0:128], cq[i][:, c, 0:128], Tc[:], start=True, stop=False, reads=[f"cq{i}", Tcn], writes=["pU"]); yield
        P.mm(pU[:, 0:128], mk[:, 0:128], Vb[i][:, c, :], start=False, stop=True, reads=[mkn, f"Vb{i}"], writes=["pU"]); yield
        P.act(W0s[:], pU[:, 0:128], AF.Identity, reads=["pU"], writes=["W0s"], scale=-1.0); yield
        P.mm(pU[:, 128:256], pi[:], W0s[:], reads=[pin, "W0s"], writes=["pU"]); yield
        P.cp(Us[:], pU[:, 128:256], reads=["pU"], writes=["Us"], eng="scalar"); yield
        P.mm(pU[:, 256:384], aTok[i][:, c, :], Us[:], start=True, stop=False, reads=[f"aTok{i}", "Us"], writes=["pU"]); yield
        P.mm(pU[:, 256:384], kTok[i][:, c, :], Vb[i][:, c, :], start=False, stop=True, reads=[f"kTok{i}", f"Vb{i}"], writes=["pU"]); yield
        P.stt(Tn[:], pU[:, 256:384], pl, tmpT[:], ALU.mult, ALU.add, reads=["pU", f"eb{i}", "tmpT"], writes=[Tnn]); yield
        P.mm(pO[:, 0:128], Tc[:], cq[i][:, c, 128:256], start=True, stop=False, reads=[Tcn, f"cq{i}"], writes=["pO"]); yield
        P.mm(pO[:, 0:128], Us[:], ma[:, 128:256], start=False, stop=False, reads=["Us", man], writes=["pO"]); yield
        P.mm(pO[:, 0:128], Vb[i][:, c, :], mk[:, 128:256], start=False, stop=True, reads=[f"Vb{i}", mkn], writes=["pO"]); yield
        for h in range(2):
            P.cp(y[64 * h:64 * h + 64, c * 64:c * 64 + 64], pO[64 * h:64 * h + 64, 64 * h:64 * h + 64], reads=["pO"], writes=[(yn, (c, h))], eng="scalar"); yield

    def post(ti):
        i = ti % 2
        t0 = ti * TT
        y, yn = y_sb[i], f"y{i}"
        t1 = S1["t1"]
        P.mm(pA[:, 0:TT], C["ones_bd"][:], y[:], reads=["ones_bd", yn], writes=["pA"])
        P.tt(y[:], y[:], pA[:, 0:TT], ALU.subtract, reads=[yn, "pA"], writes=[yn])
        P.act(t1[:], y[:], AF.Square, reads=[yn], writes=["t1"])
        P.mm(pB[:, 0:TT], C["ones_bd"][:], t1[:], reads=["ones_bd", "t1"], writes=["pB"])
        P.act(t1[:], pB[:, 0:TT], AF.Ln, reads=["pB", "cb"], writes=["t1"], bias=cb[:, 1:2], scale=1.0)
        P.act(t1[:], t1[:], AF.Exp, reads=["t1"], writes=["t1"], scale=-0.5)
        P.stt(y[:], y[:], col(8), t1[:], ALU.mult, ALU.mult, reads=[yn, "prm", "t1"], writes=[yn])
        P.stt(y[:], y[:], col(9), bv[i][:], ALU.add, ALU.add, reads=[yn, "prm", f"bv{i}"], writes=[yn])
        P.tt(y[:], y[:], g_sb[i][:], ALU.mult, reads=[yn, f"g{i}"], writes=[yn])
        P.dma(yT[:, t0:t0 + TT], y[:], reads=[yn], is_output=True)

    seq = [(ti, c) for ti in range(ntile) for c in range(NCH)]
    pipeline3(P, seq, genA, genB, prep, post, NCH, ntile)
    return P.finish()


def build_gdn(T, CH=128):
    P = Prog()
    TT = 512
    NCH = TT // CH
    NR = {64: 5, 128: 6}[CH]
    ntile = T // TT
    NC_ALL = T // CH
    zin = P.dram("zin", [512, T])
    abrow = P.dram("abrow", [2, T])
    abcol = P.dram("abcol", [CH, 2, NC_ALL])
    prm = P.dram("prm", [128, 16])
    yT = P.dram("yT", [128, T], F32, kind="ExternalOutput")
    ident = P.sb("ident", [128, 128], F32)
    P.memset(ident[:], 1.0, writes=["ident"], eng="gpsimd")
    P.op("gpsimd", "affine_select", reads=["ident"], writes=["ident"], out=ident[:], in_=ident[:], pattern=[[1, 128]],
         compare_op=ALU.is_equal, fill=0.0, base=0, channel_multiplier=-1)
    m2 = P.sb("m2", [CH, 2, CH], F32)
    P.memset(m2[:], 1.0, writes=["m2"], eng="gpsimd")
    P.op("gpsimd", "affine_select", reads=["m2"], writes=["m2"], out=m2[:, 0, :], in_=m2[:, 0, :], pattern=[[1, CH]],
         compare_op=ALU.is_gt, fill=0.0, base=0, channel_multiplier=-1)
    P.op("gpsimd", "affine_select", reads=["m2"], writes=["m2"], out=m2[:, 1, :], in_=m2[:, 1, :], pattern=[[1, CH]],
         compare_op=ALU.is_ge, fill=0.0, base=0, channel_multiplier=-1)
    P.ts(m2[:, 0, :], m2[:, 0, :], -1.0, None, ALU.mult, reads=["m2"], writes=["m2"])
    tri = P.sb("tri", [CH, CH], F32)
    P.cp(tri[:], m2[:, 1, :], reads=["m2"], writes=["tri"])
    o64 = P.sb("o64", [CH, CH], F32)
    P.memset(o64[:], 1.0, writes=["o64"])
    o128 = P.sb("o128", [128, 128], F32)
    P.memset(o128[:], 1.0, writes=["o128"])
    cm = P.sb("cmask", [128, NCH, CH], F32)
    P.memset(cm[:], 1.0, writes=["cmask"], eng="gpsimd")
    P.memset(cm[:, :, 0:1], 0.0, writes=["cmask"], eng="gpsimd")
    prm_sb = P.sb("prm_sb", [128, 16], F32)
    P.dma(prm_sb[:], prm, writes=["prm"])
    cb = P.sb("cb", [128, 4], F32)
    P.memset(cb[:, 0:1], 1e-6, writes=["cb"])
    P.memset(cb[:, 1:2], 1.0, writes=["cb"])
    P.memset(cb[:, 2:3], -0.5 * float(np.log(128.0)), writes=["cb"])
    P.act(cb[:, 3:4], prm_sb[:, 12:13], AF.Exp, reads=["prm"], writes=["cb"])
    P.ts(cb[:, 3:4], cb[:, 3:4], -1.0, None, ALU.mult, reads=["cb"], writes=["cb"])

    def col(j):
        return prm_sb[:, j:j + 1]

    pA = P.ps("pA", [128, 512], F32)
    pB = P.ps("pB", [128, 512], F32)
    pM = P.ps("pM", [128, 512], F32)
    pY = P.ps("pY", [128, 512], F32)
    pP = P.ps("pP", [128, 512], F32)
    pUW = P.ps("pUW", [128, 512], F32)
    pV = P.ps("pV", [128, 512], F32)
    pO = P.ps("pO", [128, 512], F32)

    ac = P.sb("ac", [CH, 2, NC_ALL], F32)
    P.dma(ac[:], abcol, writes=["ac"])
    betac = P.sb("betac", [CH, NC_ALL], F32)
    gcol = P.sb("gcol", [CH, NC_ALL], F32)
    gccol = P.sb("gccol", [CH, NC_ALL], F32)
    c2 = P.sb("c2", [CH, NC_ALL], F32)
    c3 = P.sb("c3", [CH, NC_ALL], F32)
    P.act(betac[:], ac[:, 0, :], AF.Sigmoid, reads=["ac"], writes=["betac"])
    P.act(gcol[:], ac[:, 1, :], AF.Exp, reads=["ac", "prm"], writes=["gcol"], bias=prm_sb[0:CH, 13:14], scale=1.0)
    P.act(gcol[:], gcol[:], AF.Ln, reads=["gcol", "cb"], writes=["gcol"], bias=cb[0:CH, 1:2], scale=1.0)
    P.ts(gcol[:], gcol[:], cb[0:CH, 3:4], None, ALU.mult, reads=["gcol", "cb"], writes=["gcol"])
    for c0 in range(0, NC_ALL, 512):
        n = min(512, NC_ALL - c0)
        P.mm(pA[0:CH, 0:n], tri[:], gcol[:, c0:c0 + n], reads=["tri", "gcol"], writes=["pA"])
        P.cp(gccol[:, c0:c0 + n], pA[0:CH, 0:n], reads=["pA"], writes=["gccol"], eng="scalar")
        P.mm(pB[0:CH, 0:n], o64[:], gcol[:, c0:c0 + n], reads=["o64", "gcol"], writes=["pB"])
        P.tt(c3[:, c0:c0 + n], pB[0:CH, 0:n], gccol[:, c0:c0 + n], ALU.subtract, reads=["pB", "gccol"], writes=["c3"])
    P.act(c3[:], c3[:], AF.Exp, reads=["c3"], writes=["c3"])
    P.act(c2[:], gccol[:], AF.Exp, reads=["gccol"], writes=["c2"])
    P.tt(c2[:], c2[:], betac[:], ALU.mult, reads=["c2", "betac"], writes=["c2"])

    zv = zin.rearrange("(q p) t -> p q t", p=128)
    z_sb = [P.sb(f"z_sb{i}", [128, 4, 3 + TT], F32) for i in range(2)]
    ab_sb = P.sb("ab_sb", [128, 2, TT], F32)
    cv_sb = P.sb("cv_sb", [128, 3, TT], F32)
    sq_sb = P.sb("sq_sb", [128, TT], F32)
    rn_sb = P.sb("rn_sb", [128, TT], F32)
    gcb = [P.sb(f"gcb{i}", [128, TT], F32) for i in range(2)]
    egc = [P.sb(f"egc{i}", [128, TT], F32) for i in range(2)]
    kT = [P.sb(f"kT{i}", [128, TT], F32) for i in range(2)]
    kq = [P.sb(f"kq{i}", [128, NCH, 2 * CH], F32) for i in range(2)]
    qg = [P.sb(f"qg{i}", [128, TT], F32) for i in range(2)]
    sg = [P.sb(f"sg{i}", [128, TT], F32) for i in range(2)]
    DTm = [P.sb(f"DTm{i}", [CH, NCH, 2, CH], F32) for i in range(2)]
    dt_tmp = P.sb("dt_tmp", [CH, NCH, CH], F32)
    RV = [P.sb(f"RV{i}", [CH, NCH, 128], F32) for i in range(2)]
    RK = [P.sb(f"RK{i}", [CH, NCH, 128], F32) for i in range(2)]
    KS = [P.sb(f"KS{i}", [CH, NCH, 128], F32) for i in range(2)]
    y_sb = [P.sb(f"y_sb{i}", [128, TT], F32) for i in range(2)]
    Sst = [P.sb(f"Sst{i}", [128, 128], F32) for i in range(2)]
    P.memset(Sst[0][:], 0.0, writes=["Sst0"])
    tmpS = P.sb("tmpS", [128, 128], F32)
    u_sb = P.sb("u_sb", [CH, 128], F32)
    wT_sb = P.sb("wT_sb", [128, CH], F32)
    vn_sb = P.sb("vn_sb", [CH, 128], F32)

    def prep(ti):
        i = ti % 2
        t0 = ti * TT
        c0 = ti * NCH
        zs, zn = z_sb[i], f"z{i}"
        if ti == 0:
            P.memset(zs[:, :, 0:3], 0.0, writes=[zn])
            P.dma(zs[:, :, 3:3 + TT], zv[:, :, 0:TT], writes=[zn])
        else:
            P.dma(zs[:], zv[:, :, t0 - 3:t0 + TT], writes=[zn])
        P.dma(ab_sb[:, 0, :], abrow[0:1, t0:t0 + TT].to_broadcast([128, TT]), writes=["ab"])
        P.dma(ab_sb[:, 1, :], abrow[1:2, t0:t0 + TT].to_broadcast([128, TT]), writes=["ab"])
        for q in range(3):
            P.ts(cv_sb[:, q, :], zs[:, q, 3:3 + TT], col(4 * q + 3), None, ALU.mult, reads=[zn, "prm"], writes=[("cv", q)])
            for j in range(3):
                P.stt(cv_sb[:, q, :], zs[:, q, j:j + TT], col(4 * q + j), cv_sb[:, q, :], ALU.mult, ALU.add, reads=[zn, "prm", ("cv", q)], writes=[("cv", q)])
        P.act(cv_sb[:], cv_sb[:], AF.Silu, reads=["cv"], writes=["cv"])
        P.act(sg[i][:], zs[:, 3, 3:3 + TT], AF.Silu, reads=[zn], writes=[f"sg{i}"])
        P.act(ab_sb[:, 0, :], ab_sb[:, 0, :], AF.Sigmoid, reads=["ab"], writes=["ab"])
        P.act(ab_sb[:, 1, :], ab_sb[:, 1, :], AF.Exp, reads=["ab", "prm"], writes=["ab"], bias=col(13), scale=1.0)
        P.act(ab_sb[:, 1, :], ab_sb[:, 1, :], AF.Ln, reads=["ab", "cb"], writes=["ab"], bias=cb[:, 1:2], scale=1.0)
        P.ts(ab_sb[:, 1, :], ab_sb[:, 1, :], cb[:, 3:4], None, ALU.mult, reads=["ab", "cb"], writes=["ab"])
        P.op("vector", "tensor_tensor_scan", reads=["ab", "cmask"], writes=[f"gcb{i}"], out=gcb[i][:], data0=cm[:].rearrange("p c t -> p (c t)"),
             data1=ab_sb[:, 1, :], initial=0.0, op0=ALU.mult, op1=ALU.add)
        P.act(egc[i][:], gcb[i][:], AF.Exp, reads=[f"gcb{i}"], writes=[f"egc{i}"])
        for q in range(2):
            P.act(sq_sb[:], cv_sb[:, q, :], AF.Square, reads=["cv"], writes=["sq"])
            pp, ppn = (pA, "pA") if q == 0 else (pB, "pB")
            P.mm(pp[:, 0:TT], o128[:], sq_sb[:], reads=["o128", "sq"], writes=[ppn])
            P.act(rn_sb[:], pp[:, 0:TT], AF.Ln, reads=[ppn, "cb"], writes=["rn"], bias=cb[:, 0:1], scale=1.0)
            if q == 0:
                P.act(rn_sb[:], rn_sb[:], AF.Exp, reads=["rn", "cb"], writes=["rn"], bias=cb[:, 2:3], scale=-0.5)
                P.tt(kq[i][:, :, CH:2 * CH], cv_sb[:, 0, :].rearrange("p (c t) -> p c t", t=CH), rn_sb[:].rearrange("p (c t) -> p c t", t=CH), ALU.mult,
                     reads=["cv", "rn"], writes=[(f"kq{i}", 1)])
            else:
                P.act(rn_sb[:], rn_sb[:], AF.Exp, reads=["rn"], writes=["rn"], scale=-0.5)
                P.tt(kT[i][:], cv_sb[:, 1, :], rn_sb[:], ALU.mult, reads=["cv", "rn"], writes=[f"kT{i}"])
        P.tt(qg[i][:].rearrange("p (c t) -> p c t", t=CH), kq[i][:, :, CH:2 * CH], egc[i][:].rearrange("p (c t) -> p c t", t=CH), ALU.mult,
             reads=[f"kq{i}", f"egc{i}"], writes=[f"qg{i}"])
        P.tt(kq[i][:, :, 0:CH], kT[i][:].rearrange("p (c t) -> p c t", t=CH), ab_sb[:, 0, :].rearrange("p (c t) -> p c t", t=CH), ALU.mult,
             reads=[f"kT{i}", "ab"], writes=[(f"kq{i}", 0)])
        P.tt(dt_tmp[:], gcb[i][0:CH, :].rearrange("p (c t) -> p c t", t=CH), gccol[:, c0:c0 + NCH].unsqueeze(2).to_broadcast([CH, NCH, CH]), ALU.subtract,
             reads=[f"gcb{i}", "gccol"], writes=["dt_tmp"])
        P.ts(dt_tmp[:], dt_tmp[:], 0.0, None, ALU.min, reads=["dt_tmp"], writes=["dt_tmp"])
        P.act(dt_tmp[:], dt_tmp[:], AF.Exp, reads=["dt_tmp"], writes=["dt_tmp"])
        for w in range(2):
            P.tt(DTm[i][:, :, w, :], dt_tmp[:], m2[:, w, :].unsqueeze(1).to_broadcast([CH, NCH, CH]), ALU.mult, reads=["dt_tmp", "m2"], writes=[(f"DTm{i}", w)])
        NG = min(NCH, 512 // 128)
        for grp in range(NCH // NG):
            for cc in range(NG):
                c = grp * NG + cc
                P.tr(pA[0:CH, cc * 128:(cc + 1) * 128], kT[i][:, c * CH:(c + 1) * CH], ident[:], reads=[f"kT{i}", "ident"], writes=["pA"])
                P.tr(pB[0:CH, cc * 128:(cc + 1) * 128], cv_sb[:, 2, c * CH:(c + 1) * CH], ident[:], reads=["cv", "ident"], writes=["pB"])
            cs = slice(c0 + grp * NG, c0 + grp * NG + NG)
            hs = slice(grp * NG, grp * NG + NG)
            pa3 = pA[0:CH, 0:NG * 128].rearrange("p (c t) -> p c t", t=128)
            pb3 = pB[0:CH, 0:NG * 128].rearrange("p (c t) -> p c t", t=128)
            P.tt(RK[i][:, hs, :], pa3, c2[:, cs].unsqueeze(2).to_broadcast([CH, NG, 128]), ALU.mult, reads=["pA", "c2"], writes=[(f"RK{i}", grp)])
            P.tt(KS[i][:, hs, :], pa3, c3[:, cs].unsqueeze(2).to_broadcast([CH, NG, 128]), ALU.mult, reads=["pA", "c3"], writes=[(f"KS{i}", grp)])
            P.tt(RV[i][:, hs, :], pb3, betac[:, cs].unsqueeze(2).to_broadcast([CH, NG, 128]), ALU.mult, reads=["pB", "betac"], writes=[(f"RV{i}", grp)])

    M23 = [P.sb(f"M23_{i}", [CH, 2 * CH], F32) for i in range(3)]
    PIF = [P.sb(f"PIF{i}", [CH, CH], F32) for i in range(3)]
    YYs = [[P.sb(f"YYs{s_}_{i}", [CH, 2 * CH], F32) for i in range(2)] for s_ in range(2)]
    PIs = [[P.sb(f"PIs{s_}_{i}", [CH, CH], F32) for i in range(2)] for s_ in range(2)]
    pYs = [pM, pY]
    pPs = [pP, pUW]

    def genA(ti, c, g):
        i = ti % 2
        j = g % 3
        s_ = g % 2
        pYb, pYn = pYs[s_], f"pYs{s_}"
        pPb, pPn = pPs[s_], f"pPs{s_}"
        mm2, m2n = M23[j], f"M23_{j}"
        cs = slice(c * CH, c * CH + CH)
        P.mm(pPb[0:CH, 0:2 * CH], kT[i][:, cs], kq[i][:, c, :], reads=[f"kT{i}", f"kq{i}"], writes=[pPn]); yield
        P.tt(mm2[:], pPb[0:CH, 0:2 * CH], DTm[i][:, c, :, :].rearrange("p w t -> p (w t)"), ALU.mult, reads=[pPn, f"DTm{i}"], writes=[m2n]); yield
        yt, ytn = YYs[s_][0], f"YYs{s_}_0"
        P.tr(pYb[0:CH, 0:CH], mm2[:, 0:CH], ident[0:CH, 0:CH], reads=[m2n, "ident"], writes=[pYn]); yield
        P.cp(yt[:, CH:2 * CH], pYb[0:CH, 0:CH], reads=[pYn], writes=[ytn], eng="scalar"); yield
        Yap, Yn_ = mm2[:, 0:CH], m2n
        YTap, YTn = yt[:, CH:2 * CH], ytn
        pi, pin = PIs[s_][0], f"PIs{s_}_0"
        P.tt(pi[:], mm2[:, 0:CH], ident[0:CH, 0:CH], ALU.add, reads=[m2n, "ident"], writes=[pin]); yield
        for k in range(1, NR + 1):
            last = (k == NR)
            y2, y2n = YYs[s_][k % 2], f"YYs{s_}_{k % 2}"
            if not last:
                P.mm(pYb[0:CH, 0:CH], YTap, Yap, reads=[Yn_, YTn], writes=[pYn]); yield
            P.mm(pYb[0:CH, CH:2 * CH], Yap, YTap, reads=[Yn_, YTn], writes=[pYn]); yield
            if not last:
                P.cp(y2[:], pYb[0:CH, 0:2 * CH], reads=[pYn], writes=[y2n], eng="scalar"); yield
            else:
                P.cp(y2[:, CH:2 * CH], pYb[0:CH, CH:2 * CH], reads=[pYn], writes=[y2n], eng="scalar"); yield
            if not last:
                pi2, pi2n = PIs[s_][k % 2], f"PIs{s_}_{k % 2}"
            else:
                pi2, pi2n = PIF[j], f"PIF{j}"
            P.mm(pPb[0:CH, 0:CH], y2[:, CH:2 * CH], pi[:], reads=[y2n, pin], writes=[pPn]); yield
            P.tt(pi2[:], pPb[0:CH, 0:CH], pi[:], ALU.add, reads=[pPn, pin], writes=[pi2n]); yield
            Yap, Yn_, YTap, YTn, pi, pin = y2[:, 0:CH], y2n, y2[:, CH:2 * CH], y2n, pi2, pi2n

    def genB(ti, c, g):
        i = ti % 2
        j = g % 3
        y, yn = y_sb[i], f"y{i}"
        Sc, Scn = Sst[g % 2], f"Sst{g % 2}"
        Sn, Snn = Sst[1 - g % 2], f"Sst{1 - g % 2}"
        mm2, m2n = M23[j], f"M23_{j}"
        pi, pin = PIF[j], f"PIF{j}"
        cs = slice(c * CH, c * CH + CH)
        el = egc[i][:, c * CH + CH - 1:c * CH + CH]
        P.mm(pV[0:CH, 0:128], pi[:], RV[i][:, c, :], reads=[pin, f"RV{i}"], writes=["pV"]); yield
        P.mm(pV[:, 128:128 + CH], RK[i][:, c, :], pi[:], reads=[pin, f"RK{i}"], writes=["pV"]); yield
        P.cp(u_sb[:], pV[0:CH, 0:128], reads=["pV"], writes=["u"], eng="scalar"); yield
        P.cp(wT_sb[:], pV[:, 128:128 + CH], reads=["pV"], writes=["wT"], eng="scalar"); yield
        P.ts(tmpS[:], Sc[:], el, None, ALU.mult, reads=[Scn, f"egc{i}"], writes=["tmpS"]); yield
        P.mm(pV[0:CH, 256:384], wT_sb[:], Sc[:], reads=["wT", Scn], writes=["pV"]); yield
        P.tt(vn_sb[:], u_sb[:], pV[0:CH, 256:384], ALU.subtract, reads=["u", "pV"], writes=["vn"]); yield
        P.mm(pV[:, 384:512], KS[i][:, c, :], vn_sb[:], reads=[f"KS{i}", "vn"], writes=["pV"]); yield
        P.tt(Sn[:], pV[:, 384:512], tmpS[:], ALU.add, reads=["pV", "tmpS"], writes=[Snn]); yield
        P.mm(pO[:, 0:CH], Sc[:], qg[i][:, cs], start=True, stop=False, reads=[Scn, f"qg{i}"], writes=["pO"]); yield
        P.mm(pO[:, 0:CH], vn_sb[:], mm2[:, CH:2 * CH], start=False, stop=True, reads=["vn", m2n], writes=["pO"]); yield
        P.cp(y[:, cs], pO[:, 0:CH], reads=["pO"], writes=[(yn, c)], eng="scalar"); yield

    def post(ti):
        i = ti % 2
        t0 = ti * TT
        y, yn = y_sb[i], f"y{i}"
        P.act(sq_sb[:], y[:], AF.Square, reads=[yn], writes=["sq"])
        P.mm(pA[:, 0:TT], o128[:], sq_sb[:], reads=["o128", "sq"], writes=["pA"])
        P.act(rn_sb[:], pA[:, 0:TT], AF.Ln, reads=["pA", "cb"], writes=["rn"], bias=cb[:, 0:1], scale=1.0 / 128)
        P.act(rn_sb[:], rn_sb[:], AF.Exp, reads=["rn"], writes=["rn"], scale=-0.5)
        P.stt(y[:], y[:], col(14), rn_sb[:], ALU.mult, ALU.mult, reads=[yn, "prm", "rn"], writes=[yn])
        P.tt(y[:], y[:], sg[i][:], ALU.mult, reads=[yn, f"sg{i}"], writes=[yn])
        P.dma(yT[:, t0:t0 + TT], y[:], reads=[yn], is_output=True)

    seq = [(ti, c) for ti in range(ntile) for c in range(NCH)]
    pipeline3(P, seq, genA, genB, prep, post, NCH, ntile)
    return P.finish()


def build_sgu(T):
    P = Prog()
    NB = T // 128
    TB = 4
    ntile = NB // TB
    uT = P.dram("uT", [128, T])
    vtok = P.dram("vtok", [128, NB, 128])
    lnp = P.dram("lnp", [1, 256])
    wT = P.dram("wT", [2, 128, 128])
    bs = P.dram("bs", [2, 128])
    yT = P.dram("yT", [128, T], F32, kind="ExternalOutput")
    lnp_sb = P.sb("lnp_sb", [128, 256], F32)
    P.dma(lnp_sb[:], lnp.to_broadcast([128, 256]), writes=["lnp"])
    w_sb = P.sb("w_sb", [128, 2, 128], F32)
    for g in range(2):
        P.dma(w_sb[:, g, :], wT[g], writes=["w"])
    P.memset(w_sb[64:128, :, 0:64], 0.0, writes=["w"])
    b_sb = P.sb("b_sb", [128, 128], F32)
    for g in range(2):
        P.dma(b_sb[64 * g:64 * g + 64, :], bs[g:g + 1, :].to_broadcast([64, 128]), writes=["b"])
    cb = P.sb("cb", [128, 1], F32)
    P.memset(cb[:], 1e-5, writes=["cb"])
    v_sb = [P.sb(f"v_sb{i}", [128, TB, 128], F32) for i in range(2)]
    sq_sb = P.sb("sq_sb", [128, TB, 128], F32)
    st = P.sb("st", [128, 4, TB * 2], F32)
    u_sb = [P.sb(f"u_sb{i}", [128, TB * 128], F32) for i in range(2)]
    y_sb = [P.sb(f"y_sb{i}", [128, TB * 128], F32) for i in range(2)]
    po = [P.ps(f"po{g}", [128, 512], F32) for g in range(2)]
    for ti in range(ntile):
        i = ti % 2
        n0 = ti * TB
        v, vn = v_sb[i], f"v{i}"
        u, un = u_sb[i], f"u{i}"
        y, yn = y_sb[i], f"y{i}"
        P.dma(v[:], vtok[:, n0:n0 + TB, :], writes=[vn])
        P.dma(u[:], uT[:, n0 * 128:(n0 + TB) * 128], writes=[un])
        P.act(v[:], v[:], AF.Gelu, reads=[vn], writes=[vn])
        P.act(u[:], u[:], AF.Gelu, reads=[un], writes=[un])
        v3 = v[:].rearrange("p n (g c) -> p (n g) c", c=64)
        s3 = sq_sb[:].rearrange("p n (g c) -> p (n g) c", c=64)
        P.op("vector", "tensor_reduce", reads=[vn], writes=[("st", 0)], out=st[:, 0, :], in_=v3, axis=AX.X, op=ALU.add)
        P.ts(st[:, 0, :], st[:, 0, :], 1.0 / 64, None, ALU.mult, reads=[("st", 0)], writes=[("st", 0)])
        P.tt(v3, v3, st[:, 0, :].unsqueeze(2).to_broadcast([128, TB * 2, 64]), ALU.subtract, reads=[vn, ("st", 0)], writes=[vn])
        P.act(sq_sb[:], v[:], AF.Square, reads=[vn], writes=["sq"])
        P.op("vector", "tensor_reduce", reads=["sq"], writes=[("st", 1)], out=st[:, 1, :], in_=s3, axis=AX.X, op=ALU.add)
        P.act(st[:, 2, :], st[:, 1, :], AF.Ln, reads=[("st", 1), "cb"], writes=[("st", 2)], bias=cb[:, 0:1], scale=1.0 / 64)
        P.act(st[:, 2, :], st[:, 2, :], AF.Exp, reads=[("st", 2)], writes=[("st", 2)], scale=-0.5)
        P.tt(v3, v3, st[:, 2, :].unsqueeze(2).to_broadcast([128, TB * 2, 64]), ALU.mult, reads=[vn, ("st", 2)], writes=[vn])
        P.tt(v[:], v[:], lnp_sb[:, 0:128].unsqueeze(1).to_broadcast([128, TB, 128]), ALU.mult, reads=[vn, "lnp"], writes=[vn])
        P.tt(v[:], v[:], lnp_sb[:, 128:256].unsqueeze(1).to_broadcast([128, TB, 128]), ALU.add, reads=[vn, "lnp"], writes=[vn])
        for g in range(2):
            for n in range(TB):
                P.mm(po[g][:, n * 128:(n + 1) * 128], v[:, n, :], w_sb[:, g, :], reads=[vn, "w"], writes=[f"po{g}"])
            pp = slice(64 * g, 64 * g + 64)
            P.tt(y[pp, :].rearrange("p (n i) -> p n i", i=128), po[g][pp, :].rearrange("p (n i) -> p n i", i=128),
                 b_sb[pp, :].unsqueeze(1).to_broadcast([64, TB, 128]), ALU.add, reads=[f"po{g}", "b"], writes=[(yn, g)])
            P.tt(y[pp, :], y[pp, :], u[pp, :], ALU.mult, reads=[(yn, g), un], writes=[(yn, g)])
        P.dma(yT[:, n0 * 128:(n0 + TB) * 128], y[:], reads=[yn], is_output=True)
    return P.finish()


def prep_hgrn(zTb, hp, lb_logits, norm_g):
    T = zTb.shape[1]
    rows = [zTb[q * 512 + 128 * hp:q * 512 + 128 * hp + 128] for q in range(4)]
    zin = np.ascontiguousarray(np.concatenate(rows, 0))
    iT = rows[2]
    vtok = iT.reshape(2, 64, T // 64, 64).transpose(0, 3, 2, 1)
    prm = np.zeros((128, 4), np.float32)
    prm[:, 0] = lb_logits[0, 128 * hp:128 * hp + 128]
    prm[:, 1] = lb_logits[1, 128 * hp:128 * hp + 128]
    prm[:, 2] = norm_g[128 * hp:128 * hp + 128]
    return {"zin": zin, "vtok": np.ascontiguousarray(vtok), "prm": prm}


def prep_rwkv(zTb, hp, e, prm_in, vfirstT=None):
    T = zTb.shape[1]
    f = slice(128 * hp, 128 * hp + 128)
    zin = np.ascontiguousarray(np.concatenate([zTb[q * 512 + 128 * hp:q * 512 + 128 * hp + 128] for q in range(3)], 0))
    lrin = np.ascontiguousarray(zTb[1536:1696])
    mu = prm_in["rwkv_mu"][e]
    prm = np.zeros((128, 16), np.float32)
    prm[:, 0] = mu[0:512][f]; prm[:, 1] = mu[512:1024][f]; prm[:, 2] = mu[1024:1536][f]
    prm[:, 3] = prm_in["rwkv_w0"][e][f]; prm[:, 4] = prm_in["rwkv_a0"][e][f]
    prm[:, 5] = prm_in["rwkv_k_k"][e][f]; prm[:, 6] = prm_in["rwkv_k_a"][e][f]
    prm[:, 7] = prm_in["rwkv_r_k"][e].reshape(512)[f]
    prm[:, 8] = prm_in["rwkv_ln_g"][e][f]; prm[:, 9] = prm_in["rwkv_ln_b"][e][f]
    prm2 = np.zeros((96, 4), np.float32)
    prm2[0:32, 0] = mu[1536:1568]; prm2[0:32, 1] = mu[1568:1600]; prm2[0:96, 2] = mu[1600:1696]
    wlr = np.zeros((96, 4, 128), np.float32)
    wlr[0:32, 0] = prm_in["rwkv_w_up"][e][:, f]; wlr[0:32, 1] = prm_in["rwkv_a_up"][e][:, f]; wlr[0:96, 2] = prm_in["rwkv_g_up"][e][:, f]
    d = {"zin": zin, "lrin": lrin, "prm": prm, "prm2": prm2, "wlr": wlr}
    if e > 0:
        prm[:, 10] = prm_in["rwkv_v0"][e - 1][f]
        wlr[0:32, 3] = prm_in["rwkv_vres_up"][e - 1][:, f]
        d["vlr"] = np.ascontiguousarray(zTb[2720:2752])
        d["vfirst"] = np.ascontiguousarray(vfirstT)
    return d


def prep_gdn(zTb, hd, o, prm_in, CH=128):
    T = zTb.shape[1]
    base = 2048
    rows = [zTb[base + q * 512 + 128 * hd:base + q * 512 + 128 * hd + 128] for q in range(4)]
    zin = np.ascontiguousarray(np.concatenate(rows, 0))
    brow = zTb[base + 2048 + hd]
    arow = zTb[base + 2052 + hd]
    abrow = np.ascontiguousarray(np.stack([brow, arow], 0))
    abcol = np.ascontiguousarray(np.stack([brow.reshape(T // CH, CH).T, arow.reshape(T // CH, CH).T], 1))
    prm = np.zeros((128, 16), np.float32)
    cw = prm_in["gdn_conv_w"][o]
    for q in range(3):
        prm[:, 4 * q:4 * q + 4] = cw[:, q * 512 + 128 * hd:q * 512 + 128 * hd + 128].T
    prm[:, 12] = prm_in["gdn_a_log"][o][hd]
    prm[:, 13] = prm_in["gdn_dt_bias"][o][hd]
    prm[:, 14] = prm_in["gdn_norm_g"][o]
    return {"zin": zin, "abrow": abrow, "abcol": abcol, "prm": prm}


def prep_sgu(zTb, gp, e, prm_in):
    T = zTb.shape[1]
    base = 1696
    uT = np.ascontiguousarray(zTb[base + 128 * gp:base + 128 * gp + 128])
    vT = zTb[base + 512 + 128 * gp:base + 512 + 128 * gp + 128]
    vtok = np.ascontiguousarray(vT.reshape(128, T // 128, 128).transpose(2, 1, 0))
    f = slice(128 * gp, 128 * gp + 128)
    lnp = np.concatenate([prm_in["sgu_ln_g"][e][f], prm_in["sgu_ln_b"][e][f]])[None, :].astype(np.float32)
    w = prm_in["sgu_w"][e][2 * gp:2 * gp + 2]
    wT = np.ascontiguousarray(w.transpose(0, 2, 1))
    bs = np.ascontiguousarray(prm_in["sgu_b"][e][2 * gp:2 * gp + 2])
    return {"uT": uT, "vtok": vtok, "lnp": np.ascontiguousarray(lnp), "wT": wT, "bs": bs}


_PROGS = {}


def _prog(key, fn):
    if key not in _PROGS:
        _PROGS[key] = fn()
    return _PROGS[key]


def _run(nc, in_maps):
    res = run_bass_kernel_spmd(nc, in_maps, core_ids=list(range(8)))
    return res.results


NTOK = 2048


def _dense_inputs(xTb, yTb, l, p, do_pre, w_in_next, g_pre_next):
    w_o = p["ev_w_out"][l // 2] if l % 2 == 0 else p["od_w_out"][l // 2]
    gvec = np.ascontiguousarray(np.concatenate([colvec(p["norm_mix_post"][l]), colvec(p["norm_ffn_pre"][l]), colvec(p["norm_ffn_post"][l])], 1))
    cw = np.concatenate([p["ffn_conv_w"][l].T, p["ffn_conv_b"][l][:, None]], 1).reshape(NFC, 128, 4).transpose(1, 0, 2)
    cw = np.ascontiguousarray(cw)
    maps = []
    for b in range(2):
        for q in range(4):
            lo = q * NTOK
            if q == 0:
                xs = np.concatenate([np.zeros((D, 2), np.float32), xTb[b][:, 0:NTOK]], 1)
                ys = np.concatenate([np.zeros((D, 2), np.float32), yTb[b][:, 0:NTOK]], 1)
            else:
                xs = xTb[b][:, lo - 2:lo + NTOK]
                ys = yTb[b][:, lo - 2:lo + NTOK]
            m = {"xT": np.ascontiguousarray(xs), "yT": np.ascontiguousarray(ys), "w_o": w_o, "w_f1": p["ffn_w_in"][l], "w_f2": p["ffn_w_out"][l],
                 "gvec": gvec, "cw": cw, "hmask": np.full((128, 1), 0.0 if q == 0 else 1.0, np.float32)}
            if do_pre:
                m["gpre"] = g_pre_next
                m["w_in"] = w_in_next
            maps.append(m)
    return maps


def _w_in(l, p):
    if l % 2 == 1:
        return p["od_w_in"][l // 2]
    e = l // 2
    if e == 0:
        return p["ev_w_in"][0]
    return np.ascontiguousarray(np.concatenate([p["ev_w_in"][e], p["rwkv_vres_down"][e - 1]], 1))


def kernel(**inputs):
    p = {k: np.ascontiguousarray(np.asarray(v, dtype=np.float32)) for k, v in inputs.items()}
    x = p["x"]
    B, T, _ = x.shape
    xTb = [np.ascontiguousarray(x[b].T) for b in range(B)]
    w_in0 = _w_in(0, p)
    nc = _prog(("dense", False, True, w_in0.shape[1]), lambda: build_dense(NTOK, False, True, w_in0.shape[1]))
    maps = [{"xT": np.ascontiguousarray(xTb[b][:, q * NTOK:(q + 1) * NTOK]), "gpre": colvec(p["norm_mix_pre"][0]), "w_in": w_in0}
            for b in range(B) for q in range(4)]
    res = _run(nc, maps)
    zTb = [np.concatenate([res[4 * b + q]["zT"] for q in range(4)], 1) for b in range(B)]
    vfirst = None
    for l in range(4):
        yTb = [np.empty((D, T), np.float32) for _ in range(B)]
        if l % 2 == 0:
            e = l // 2
            nc = _prog(("rwkv", e > 0), lambda: build_rwkv(T, e > 0))
            maps = [prep_rwkv(zTb[b], hp, e, p, None if vfirst is None else vfirst[b][128 * hp:128 * hp + 128]) for b in range(B) for hp in range(4)]
            res = _run(nc, maps)
            for b in range(B):
                for hp in range(4):
                    yTb[b][128 * hp:128 * hp + 128] = res[4 * b + hp]["yT"]
            if e == 0:
                vfirst = [np.concatenate([res[4 * b + hp]["vout"] for hp in range(4)], 0) for b in range(B)]
            nc = _prog(("sgu",), lambda: build_sgu(T))
            maps = [prep_sgu(zTb[b], gp, e, p) for b in range(B) for gp in range(4)]
            res = _run(nc, maps)
            for b in range(B):
                for gp in range(4):
                    yTb[b][512 + 128 * gp:512 + 128 * gp + 128] = res[4 * b + gp]["yT"]
        else:
            o = l // 2
            nc = _prog(("hgrn", o), lambda: build_hgrn(T, o))
            maps = [prep_hgrn(zTb[b], hp, p["hgrn_lb_logits"], p["hgrn_norm_g"][o]) for b in range(B) for hp in range(4)]
            res = _run(nc, maps)
            for b in range(B):
                for hp in range(4):
                    yTb[b][128 * hp:128 * hp + 128] = res[4 * b + hp]["yT"]
            nc = _prog(("gdn",), lambda: build_gdn(T))
            maps = [prep_gdn(zTb[b], hd, o, p) for b in range(B) for hd in range(4)]
            res = _run(nc, maps)
            for b in range(B):
                for hd in range(4):
                    yTb[b][512 + 128 * hd:512 + 128 * hd + 128] = res[4 * b + hd]["yT"]
        do_pre = l < 3
        w_next = _w_in(l + 1, p) if do_pre else None
        C = w_next.shape[1] if do_pre else 0
        nc = _prog(("dense", True, do_pre, C), lambda: build_dense(NTOK, True, do_pre, C))
        maps = _dense_inputs(xTb, yTb, l, p, do_pre, w_next, colvec(p["norm_mix_pre"][l + 1]) if do_pre else None)
        res = _run(nc, maps)
        xTb = [np.concatenate([res[4 * b + q]["xoT"] for q in range(4)], 1) for b in range(B)]
        if do_pre:
            zTb = [np.concatenate([res[4 * b + q]["zT"] for q in range(4)], 1) for b in range(B)]
    return np.ascontiguousarray(np.stack([xTb[b].T for b in range(B)], 0)).astype(np.float32)
```

```python
import numpy as np
from contextlib import ExitStack
import concourse.bass as bass
import concourse.mybir as mybir
from concourse.bass_utils import run_bass_kernel_spmd

F32 = mybir.dt.float32
BF16 = mybir.dt.bfloat16
F32R = mybir.dt.float32r
AF = mybir.ActivationFunctionType
ALU = mybir.AluOpType
AX = mybir.AxisListType

COMPUTE = ("tensor", "vector", "scalar", "gpsimd")
NPOOL = 24
SKIP_SELF = ("tensor",)


class Prog:
    def __init__(self):
        self.nc = bass.Bass("TRN2", target_bir_lowering=False)
        self.es = ExitStack()
        self.ops = {e: [] for e in COMPUTE + ("sync",)}
        self.cnt = {e: 0 for e in COMPUTE}
        self.sem = {}
        for e in COMPUTE:
            self.sem[e] = self.nc.alloc_semaphore("s_" + e)
        self.dpool = {q: [self.nc.alloc_semaphore(f"d_{q}_{i}") for i in range(NPOOL)] for q in ("sync", "gpsimd")}
        self.dcnt = {"sync": 0, "gpsimd": 0}
        self.known = {e: {} for e in COMPUTE + ("sync",)}
        self.lastw = {}
        self.readers = {}
        self.out_tokens = []
        self.nuniq = 0
        self.fast_fp32 = False
        self.defer = None

    def sb(self, name, shape, dtype=F32):
        return self.es.enter_context(self.nc.sbuf_tensor(name, list(shape), dtype))

    def ps(self, name, shape, dtype=F32):
        return self.es.enter_context(self.nc.psum_tensor(name, list(shape), dtype))

    def dram(self, name, shape, dtype=F32, kind="ExternalInput"):
        return self.nc.dram_tensor(name, list(shape), dtype, kind=kind).ap()

    @staticmethod
    def _key(r):
        if isinstance(r, tuple):
            return r[0], r[1]
        return r, None

    def _conf(self, table, name, sub):
        d = table.get(name, {})
        if sub is None:
            return list(d.values())
        out = []
        if sub in d:
            out.append(d[sub])
        if None in d:
            out.append(d[None])
        return out

    def _deps(self, reads, writes):
        toks = []
        for r in reads:
            n, s = self._key(r)
            toks += self._conf(self.lastw, n, s)
        for w in writes:
            n, s = self._key(w)
            toks += self._conf(self.lastw, n, s)
            for lst in self._conf(self.readers, n, s):
                toks += lst
        return toks

    def _commit(self, reads, writes, tok):
        for r in reads:
            n, s = self._key(r)
            self.readers.setdefault(n, {}).setdefault(s, []).append(tok)
        for w in writes:
            n, s = self._key(w)
            if s is None:
                self.lastw[n] = {None: tok}
                self.readers[n] = {}
            else:
                self.lastw.setdefault(n, {})[s] = tok
                self.readers.setdefault(n, {})[s] = []

    def _waits(self, eng, toks, skip_self=False):
        need = {}
        for (sname, sem, val, src) in toks:
            if skip_self and src == eng:
                continue
            if val > need.get(sname, (None, 0))[1]:
                need[sname] = (sem, val)
        out = []
        kn = self.known[eng]
        for sname, (sem, val) in need.items():
            if kn.get(sname, 0) >= val:
                continue
            kn[sname] = val
            out.append((sem, val))
        return out

    def op(self, eng, meth, reads=(), writes=(), **kw):
        if self.defer is not None:
            self.defer.append(("op", eng, meth, tuple(reads), tuple(writes), kw))
            return None
        toks = self._deps(reads, writes)
        waits = self._waits(eng, toks, skip_self=(eng in SKIP_SELF))
        self.cnt[eng] += 1
        tok = ("s_" + eng, self.sem[eng], self.cnt[eng], eng)

        def fn(e, meth=meth, kw=kw):
            return getattr(e, meth)(**kw)

        self.ops[eng].append((waits, fn, (self.sem[eng], 1)))
        self._commit(reads, writes, tok)
        return tok

    def mm(self, out, lhsT, rhs, start=True, stop=True, reads=(), writes=()):
        if self.fast_fp32 and lhsT.dtype == F32 and rhs.dtype == F32:
            lhsT = lhsT.bitcast(F32R)
            rhs = rhs.bitcast(F32R)
        return self.op("tensor", "matmul", reads, writes, out=out, lhsT=lhsT, rhs=rhs, start=start, stop=stop)

    def tr(self, out, in_, identity, reads=(), writes=()):
        return self.op("tensor", "transpose", reads, writes, out=out, in_=in_, identity=identity)

    def act(self, out, in_, func, reads=(), writes=(), **kw):
        return self.op("scalar", "activation", reads, writes, out=out, in_=in_, func=func, **kw)

    def tt(self, out, in0, in1, op, reads=(), writes=(), eng="vector"):
        return self.op(eng, "tensor_tensor", reads, writes, out=out, in0=in0, in1=in1, op=op)

    def stt(self, out, in0, scalar, in1, op0, op1, reads=(), writes=()):
        return self.op("vector", "scalar_tensor_tensor", reads, writes, out=out, in0=in0, scalar=scalar, in1=in1, op0=op0, op1=op1)

    def ts(self, out, in0, scalar1, scalar2, op0, op1=None, reads=(), writes=(), eng="vector"):
        kw = dict(out=out, in0=in0, scalar1=scalar1, scalar2=scalar2, op0=op0)
        if op1 is not None:
            kw["op1"] = op1
        return self.op(eng, "tensor_scalar", reads, writes, **kw)

    def cp(self, out, in_, reads=(), writes=(), eng="vector"):
        if eng == "scalar":
            return self.op("scalar", "copy", reads, writes, out=out, in_=in_)
        return self.op(eng, "tensor_copy", reads, writes, out=out, in_=in_)

    def memset(self, ap, val, writes=(), eng="vector"):
        return self.op(eng, "memset", (), writes, ap=ap, constant=val)

    def record(self, fn, *a):
        assert self.defer is None
        self.defer = []
        fn(*a)
        lst, self.defer = self.defer, None
        return lst

    def run_deferred(self, item):
        if item[0] == "op":
            _, eng, meth, reads, writes, kw = item
            self.op(eng, meth, reads, writes, **kw)
        else:
            _, out, in_, reads, writes, q, is_output, kw = item
            self.dma(out, in_, reads, writes, q, is_output, **kw)

    def dma(self, out, in_, reads=(), writes=(), q="sync", is_output=False, **kw):
        if self.defer is not None:
            self.defer.append(("dma", out, in_, tuple(reads), tuple(writes), q, is_output, kw))
            return None
        toks = self._deps(reads, writes)
        j = self.dcnt[q]
        self.dcnt[q] += 1
        slot, rnd = j % NPOOL, j // NPOOL
        sem = self.dpool[q][slot]
        sname = f"d_{q}_{slot}"
        if rnd > 0:
            toks = toks + [(sname, sem, 16 * rnd, "dma")]
        waits = self._waits(q, toks)
        tok = (sname, sem, 16 * (rnd + 1), "dma")

        def fn(e, out=out, in_=in_, kw=kw):
            return e.dma_start(out=out, in_=in_, **kw)

        self.ops[q].append((waits, fn, (sem, 16)))
        self._commit(reads, writes, tok)
        if is_output:
            self.out_tokens.append(tok)
        return tok

    def finish(self):
        nc = self.nc
        final = list(self.out_tokens)
        for e in COMPUTE:
            if self.cnt[e]:
                final.append(("s_" + e, self.sem[e], self.cnt[e], e))
        fw = self._waits("sync", final)
        ops = self.ops
        with nc.Block() as block:
            def emit(e, lst, extra=()):
                for waits, fn, (sem, inc) in lst:
                    for (ws, wv) in waits:
                        e.wait_ge(ws, wv)
                    fn(e).then_inc(sem, inc)
                for (ws, wv) in extra:
                    e.wait_ge(ws, wv)

            @block.sync
            def _(e):
                emit(e, ops["sync"], fw)

            @block.tensor
            def _(e):
                emit(e, ops["tensor"])

            @block.vector
            def _(e):
                emit(e, ops["vector"])

            @block.scalar
            def _(e):
                emit(e, ops["scalar"])

            @block.gpsimd
            def _(e):
                emit(e, ops["gpsimd"])
        self.es.close()
        return nc


D = 1024
DFF = 2816
NFC = DFF // 128
EPS = 1e-6


class Ring:
    def __init__(self, P, name, n, shape, dtype):
        self.tiles = [P.sb(f"{name}{i}", shape, dtype) for i in range(n)]
        self.names = [f"{name}{i}" for i in range(n)]
        self.i = 0

    def next(self):
        k = self.i % len(self.tiles)
        self.i += 1
        return self.tiles[k], self.names[k]


class PsRing:
    def __init__(self, P, name, n, width=512):
        self.tiles = [P.ps(f"{name}{i}", [128, width], F32) for i in range(n)]
        self.names = [f"{name}{i}" for i in range(n)]
        self.i = 0

    def next(self):
        k = self.i % len(self.tiles)
        self.i += 1
        return self.tiles[k], self.names[k]


def colvec(a):
    a = np.asarray(a, np.float32)
    return np.ascontiguousarray(a.reshape(-1, 128).T)


def build_dense(NT, do_post, do_pre, C):
    P = Prog()
    TM = min(1024, NT)
    nmt = NT // TM
    HAL = 2 if do_post else 0
    W = HAL + NT
    TW = TM + 2
    xT = P.dram("xT", [D, W])
    if do_post:
        yT = P.dram("yT", [D, W])
        w_o = P.dram("w_o", [D, D])
        w_f1 = P.dram("w_f1", [D, 2 * DFF])
        w_f2 = P.dram("w_f2", [DFF, D])
        gvec = P.dram("gvec", [128, 24])
        cw = P.dram("cw", [128, NFC, 4])
        hmask = P.dram("hmask", [128, 1])
        xoT = P.dram("xoT", [D, NT], F32, kind="ExternalOutput")
    if do_pre:
        gpre = P.dram("gpre", [128, 8])
        w_in = P.dram("w_in", [D, C])
        zT = P.dram("zT", [C, NT], F32, kind="ExternalOutput")

    x_sb = P.sb("x_sb", [128, 8, TW], F32)
    h_sb = P.sb("h_sb", [128, 8, TW], BF16)
    ring = Ring(P, "wr", 2, [128, NFC, 512], BF16)
    sq = P.sb("sq", [128, 8, 512], BF16)
    rstd = [P.sb(f"rstd{i}", [128, 512], F32) for i in range(2)]
    ones = P.sb("ones", [128, 128], BF16)
    P.memset(ones[:], 1.0 / D, writes=["ones"])
    pr = PsRing(P, "pp", 6)
    pn = P.ps("pn", [128, 512], F32)
    nrs = [0]
    if do_post:
        f8 = P.sb("f8", [128, 8, TW], F32)
        act = P.sb("act", [128, NFC * TM], BF16)
        act3 = act[:, :].rearrange("p (j t) -> p j t", t=TM)
        y_sb = act[:, 0:8 * TW].rearrange("p (k t) -> p k t", t=TW)
        gb = [P.sb(f"gb{i}", [128, 2 + 512], F32) for i in range(2)]
        cv = [P.sb(f"cv{i}", [128, 512], F32) for i in range(2)]
        ghalo = P.sb("ghalo", [128, NFC, 2], F32)
        gv_sb = P.sb("gv_sb", [128, 24], F32)
        cw_sb = P.sb("cw_sb", [128, NFC, 4], F32)
        hm_sb = P.sb("hm_sb", [128, 1], F32)
        P.dma(gv_sb[:], gvec, writes=["gv_sb"])
        P.dma(cw_sb[:], cw, writes=["cw_sb"])
        P.dma(hm_sb[:], hmask, writes=["hm_sb"])
    if do_pre:
        gp_sb = P.sb("gp_sb", [128, 8], F32)
        P.dma(gp_sb[:], gpre, writes=["gp_sb"])
        zst = Ring(P, "zst", 3, [128, 512], F32)

    xv = xT.rearrange("(k p) t -> p k t", p=128)
    if do_post:
        yv = yT.rearrange("(k p) t -> p k t", p=128)
        xov = xoT.rearrange("(k p) t -> p k t", p=128)

    def rms_rstd(src, sname, lo, hi):
        n = hi - lo
        for m in range(8):
            P.act(sq[:, m, 0:n], src[:, m, lo:hi], AF.Square, reads=[(sname, (m, lo))], writes=[("sq", m)])
        for m in range(8):
            P.mm(pn[:, 0:n], ones[:], sq[:, m, 0:n], start=(m == 0), stop=(m == 7), reads=["ones", ("sq", m)], writes=["pn"])
        r = rstd[nrs[0] % 2]
        rn = f"rstd{nrs[0] % 2}"
        nrs[0] += 1
        P.act(r[:, 0:n], pn[:, 0:n], AF.Ln, reads=["pn"], writes=[rn], bias=EPSB[0][:, 0:1], scale=1.0)
        P.act(r[:, 0:n], r[:, 0:n], AF.Exp, reads=[rn], writes=[rn], scale=-0.5)
        return r, rn

    epsb = P.sb("epsb", [128, 1], F32)
    P.memset(epsb[:], EPS, writes=["epsb"])
    EPSB = [epsb]

    def linear(Wd, kcn, M, in_tile, in_name, subs, consume):
        Wv = Wd.rearrange("(kc p) m -> p kc m", p=128)
        for blk in range(0, M, 512):
            bw = min(512, M - blk)
            wt, wn = ring.next()
            P.dma(wt[:, 0:kcn, 0:bw], Wv[:, :, blk:blk + bw], writes=[wn], q="gpsimd")
            for m0 in range(0, bw, 128):
                mw = min(128, bw - m0)
                for (lo, hi) in subs:
                    n = hi - lo
                    ps, psn = pr.next()
                    for kc in range(kcn):
                        P.mm(ps[0:mw, 0:n], wt[:, kc, m0:m0 + mw], in_tile[:, kc, lo:hi], start=(kc == 0), stop=(kc == kcn - 1),
                             reads=[wn, (in_name, (kc, lo))], writes=[psn])
                    consume((blk + m0) // 128, mw, lo, hi, ps, psn)

    for mt in range(nmt):
        hoff = HAL if mt == 0 else 0
        c0 = 0 if mt == 0 else HAL + mt * TM
        nsub = TM // 512
        if hoff:
            subs = [(0, 2)] + [(2 + i * 512, 2 + (i + 1) * 512) for i in range(nsub)]
        else:
            subs = [(i * 512, (i + 1) * 512) for i in range(nsub)]
        real = [s for s in subs if s[1] - s[0] > 2]

        for (lo, hi) in subs:
            P.dma(x_sb[:, :, lo:hi], xv[:, :, c0 + lo:c0 + hi], writes=[("x", (m, lo)) for m in range(8)])
        if do_post:
            for (lo, hi) in subs:
                P.dma(y_sb[:, :, lo:hi], yv[:, :, c0 + lo:c0 + hi], writes=["a"] + [("y", (m, lo)) for m in range(8)], q="gpsimd")

            def c_mix(m, mw, lo, hi, ps, psn):
                P.cp(f8[:, m, lo:hi], ps[:, 0:hi - lo], reads=[psn], writes=[("f8", (m, lo))], eng="scalar")
            linear(w_o, 8, D, y_sb, "y", subs, c_mix)
            for (lo, hi) in subs:
                n = hi - lo
                r, rn = rms_rstd(f8, "f8", lo, hi)
                for m in range(8):
                    P.stt(f8[:, m, lo:hi], f8[:, m, lo:hi], gv_sb[:, m:m + 1], r[:, 0:n], ALU.mult, ALU.mult,
                          reads=[("f8", (m, lo)), rn, "gv_sb"], writes=[("f8", (m, lo))])
                    P.tt(x_sb[:, m, lo:hi], x_sb[:, m, lo:hi], f8[:, m, lo:hi], ALU.add,
                         reads=[("f8", (m, lo)), ("x", (m, lo))], writes=[("x", (m, lo))])
                r, rn = rms_rstd(x_sb, "x", lo, hi)
                for m in range(8):
                    P.stt(h_sb[:, m, lo:hi], x_sb[:, m, lo:hi], gv_sb[:, 8 + m:9 + m], r[:, 0:n], ALU.mult, ALU.mult,
                          reads=[("x", (m, lo)), rn, "gv_sb"], writes=[("h", (m, lo))])

            Wv1 = w_f1.rearrange("(kc p) m -> p kc m", p=128)
            first_act = True
            ngb = 0
            for blk in range(0, DFF, 512):
                bw = min(512, DFF - blk)
                wt, wn = ring.next()
                P.dma(wt[:, 0:8, 0:bw], Wv1[:, :, blk:blk + bw], writes=[wn], q="gpsimd")
                P.dma(wt[:, 8:16, 0:bw], Wv1[:, :, DFF + blk:DFF + blk + bw], writes=[wn], q="gpsimd")
                for m0 in range(0, bw, 128):
                    j = (blk + m0) // 128
                    for (lo, hi) in subs:
                        n = hi - lo
                        pg, pgn = pr.next()
                        for kc in range(8):
                            P.mm(pg[:, 0:n], wt[:, kc, m0:m0 + 128], h_sb[:, kc, lo:hi], start=(kc == 0), stop=(kc == 7),
                                 reads=[wn, ("h", (kc, lo))], writes=[pgn])
                        if n == 2:
                            P.ts(ghalo[:, j, :], pg[:, 0:2], hm_sb[:, 0:1], None, ALU.mult, reads=[pgn, "hm_sb"], writes=[("ghalo", j)])
                            continue
                        pu, pun = pr.next()
                        for kc in range(8):
                            P.mm(pu[:, 0:n], wt[:, 8 + kc, m0:m0 + 128], h_sb[:, kc, lo:hi], start=(kc == 0), stop=(kc == 7),
                                 reads=[wn, ("h", (kc, lo))], writes=[pun])
                        g_t, gname = gb[ngb % 2], f"gb{ngb % 2}"
                        c_t, cname = cv[ngb % 2], f"cv{ngb % 2}"
                        ngb += 1
                        P.cp(g_t[:, 0:2], ghalo[:, j, :], reads=[("ghalo", j)], writes=[(gname, 0)], eng="vector")
                        P.cp(g_t[:, 2:2 + n], pg[:, 0:n], reads=[pgn], writes=[(gname, 1)], eng="scalar")
                        P.act(c_t[:, 0:n], pg[:, 0:n], AF.Identity, reads=[pgn, "cw_sb"], writes=[cname], scale=cw_sb[:, j, 2:3], bias=cw_sb[:, j, 3:4])
                        P.cp(ghalo[:, j, :], g_t[:, n:n + 2], reads=[(gname, 1)], writes=[("ghalo", j)], eng="vector")
                        P.stt(c_t[:, 0:n], g_t[:, 1:1 + n], cw_sb[:, j, 1:2], c_t[:, 0:n], ALU.mult, ALU.add, reads=[gname, cname, "cw_sb"], writes=[cname])
                        P.stt(c_t[:, 0:n], g_t[:, 0:n], cw_sb[:, j, 0:1], c_t[:, 0:n], ALU.mult, ALU.add, reads=[gname, cname, "cw_sb"], writes=[cname])
                        P.act(c_t[:, 0:n], c_t[:, 0:n], AF.Gelu_apprx_tanh, reads=[cname], writes=[cname])
                        tlo = lo - hoff
                        wr = [("a", (j, tlo))]
                        if first_act:
                            wr.append("y")
                            first_act = False
                        P.tt(act3[:, j, tlo:tlo + n], c_t[:, 0:n], pu[:, 0:n], ALU.mult, reads=[cname, pun], writes=wr)

            rsubs = [(lo - hoff, hi - hoff) for (lo, hi) in real]

            def c_ffo(m, mw, lo, hi, ps, psn):
                P.cp(f8[:, m, lo:hi], ps[:, 0:hi - lo], reads=[psn], writes=[("f8", (m, lo))], eng="scalar")
            linear(w_f2, NFC, D, act3, "a", rsubs, c_ffo)
            for (lo, hi) in rsubs:
                n = hi - lo
                r, rn = rms_rstd(f8, "f8", lo, hi)
                for m in range(8):
                    P.stt(f8[:, m, lo:hi], f8[:, m, lo:hi], gv_sb[:, 16 + m:17 + m], r[:, 0:n], ALU.mult, ALU.mult,
                          reads=[("f8", (m, lo)), rn, "gv_sb"], writes=[("f8", (m, lo))])
                    P.tt(x_sb[:, m, hoff + lo:hoff + hi], x_sb[:, m, hoff + lo:hoff + hi], f8[:, m, lo:hi], ALU.add,
                         reads=[("f8", (m, lo)), ("x", (m, hoff + lo))], writes=[("x", (m, hoff + lo))])
                P.dma(xov[:, :, mt * TM + lo:mt * TM + hi], x_sb[:, :, hoff + lo:hoff + hi], reads=[("x", (m, hoff + lo)) for m in range(8)], is_output=True)
        if do_pre:
            ph = hoff if do_post else 0
            for (lo, hi) in real:
                n = hi - lo
                r, rn = rms_rstd(x_sb, "x", lo, hi)
                for m in range(8):
                    P.stt(h_sb[:, m, lo:hi], x_sb[:, m, lo:hi], gp_sb[:, m:m + 1], r[:, 0:n], ALU.mult, ALU.mult,
                          reads=[("x", (m, lo)), rn, "gp_sb"], writes=[("h", (m, lo))])

            def c_z(m, mw, lo, hi, ps, psn):
                n = hi - lo
                zt, zn = zst.next()
                P.cp(zt[0:mw, 0:n], ps[0:mw, 0:n], reads=[psn], writes=[zn], eng="scalar")
                oc = mt * TM + lo - ph
                P.dma(zT[m * 128:m * 128 + mw, oc:oc + n], zt[0:mw, 0:n], reads=[zn], is_output=True)
            linear(w_in, 8, C, h_sb, "h", real, c_z)
    return P.finish()


def make_consts(P, need_strict=False):
    c = {}
    ident = P.sb("ident", [128, 128], F32)
    P.memset(ident[:], 1.0, writes=["ident"], eng="gpsimd")
    P.op("gpsimd", "affine_select", reads=["ident"], writes=["ident"], out=ident[:], in_=ident[:], pattern=[[1, 128]],
         compare_op=ALU.is_equal, fill=0.0, base=0, channel_multiplier=-1)
    c["ident"] = ident
    mi = P.sb("mask_i", [128, 128], F32)
    P.memset(mi[:], 1.0, writes=["mask_i"], eng="gpsimd")
    P.op("gpsimd", "affine_select", reads=["mask_i"], writes=["mask_i"], out=mi[:], in_=mi[:], pattern=[[1, 128]],
         compare_op=ALU.is_ge, fill=0.0, base=0, channel_multiplier=-1)
    P.memset(mi[0:64, 64:128], 0.0, writes=["mask_i"], eng="gpsimd")
    c["mask_i"] = mi
    if need_strict:
        ms = P.sb("mask_s", [128, 128], F32)
        P.memset(ms[:], 1.0, writes=["mask_s"], eng="gpsimd")
        P.op("gpsimd", "affine_select", reads=["mask_s"], writes=["mask_s"], out=ms[:], in_=ms[:], pattern=[[1, 128]],
             compare_op=ALU.is_gt, fill=0.0, base=0, channel_multiplier=-1)
        P.memset(ms[0:64, 64:128], 0.0, writes=["mask_s"], eng="gpsimd")
        c["mask_s"] = ms
    ob = P.sb("ones_bd", [128, 128], F32)
    P.memset(ob[:], 0.0, writes=["ones_bd"], eng="gpsimd")
    P.memset(ob[0:64, 0:64], 1.0 / 64, writes=["ones_bd"], eng="gpsimd")
    P.memset(ob[64:128, 64:128], 1.0 / 64, writes=["ones_bd"], eng="gpsimd")
    c["ones_bd"] = ob
    cm = P.sb("cmask", [128, 8, 64], F32)
    P.memset(cm[:], 1.0, writes=["cmask"], eng="gpsimd")
    P.memset(cm[:, :, 0:1], 0.0, writes=["cmask"], eng="gpsimd")
    c["cmask"] = cm
    return c


def zipper(a, b, ra=1, rb=1):
    da = db = False
    while not (da and db):
        for _ in range(ra):
            if not da:
                try:
                    next(a)
                except StopIteration:
                    da = True
        for _ in range(rb):
            if not db:
                try:
                    next(b)
                except StopIteration:
                    db = True


def _adv(gen, n):
    for _ in range(n):
        try:
            next(gen)
        except StopIteration:
            return True
    return False


def pipeline3(P, seq, genA, genB, prep, post, NCH, ntile):
    n = len(seq)
    gens = {}
    bg = []

    def drain_bg():
        while bg:
            P.run_deferred(bg.pop(0))

    def start(g):
        if g < n:
            if seq[g][1] == 0:
                drain_bg()
            gens[g] = genA(seq[g][0], seq[g][1], g)

    prep(0)
    if ntile > 1:
        prep(1)
    start(0)
    while not _adv(gens[0], 1000):
        pass
    start(1)
    for g, (ti, c) in enumerate(seq):
        start(g + 2)
        b = genB(ti, c, g)
        a1 = gens.get(g + 1)
        a2 = gens.get(g + 2)
        d1 = a1 is None
        db = False
        while not (d1 and db):
            if not d1:
                d1 = _adv(a1, 2)
            if a2 is not None:
                if _adv(a2, 1):
                    a2 = None
            if not db:
                db = _adv(b, 1)
            for _ in range(2):
                if bg:
                    P.run_deferred(bg.pop(0))
        gens.pop(g, None)
        if c == NCH - 1:
            bg.extend(P.record(post, ti))
            if ti + 2 < ntile:
                bg.extend(P.record(prep, ti + 2))
    drain_bg()


def bd_write(P, dst, dname, src, sname, mul, mname, nch, extra_reads=()):
    for h in range(2):
        pp = slice(64 * h, 64 * h + 64)
        s3 = src[pp, 0:nch * 64].rearrange("p (c t) -> p c t", t=64)
        m3 = mul[pp, 0:nch * 64].rearrange("p (c t) -> p c t", t=64)
        P.tt(dst[pp, 0:nch, 64 * h:64 * h + 64], s3, m3, ALU.mult, reads=[sname, mname] + list(extra_reads), writes=[(dname, h)],
             eng=("vector" if h == 0 else "gpsimd"))


def build_hgrn(T, layer_o):
    P = Prog()
    TT = 512
    NCH = TT // 64
    ntile = T // TT
    zin = P.dram("zin", [512, T])
    vtok = P.dram("vtok", [2, 64, T // 64, 64])
    prm = P.dram("prm", [128, 4])
    yT = P.dram("yT", [128, T], F32, kind="ExternalOutput")
    C = make_consts(P)
    prm_sb = P.sb("prm_sb", [128, 4], F32)
    P.dma(prm_sb[:], prm, writes=["prm"])
    lbv = P.sb("lbv", [128, 2], F32)
    if layer_o == 0:
        P.memset(lbv[:, 0:1], 0.0, writes=["lbv"])
        P.memset(lbv[:, 1:2], 1.0, writes=["lbv"])
    else:
        P.tt(lbv[:, 0:1], prm_sb[:, 1:2], prm_sb[:, 0:1], ALU.subtract, reads=["prm"], writes=["lbv"])
        P.act(lbv[:, 0:1], lbv[:, 0:1], AF.Sigmoid, reads=["lbv"], writes=["lbv"])
        P.ts(lbv[:, 1:2], lbv[:, 0:1], -1.0, 1.0, ALU.mult, ALU.add, reads=["lbv"], writes=["lbv"])
    epsb = P.sb("epsb", [128, 1], F32)
    P.memset(epsb[:], EPS, writes=["epsb"])

    zv = zin.rearrange("(q p) t -> p q t", p=128)
    NB = 2
    z_sb = [P.sb(f"z_sb{i}", [128, 4, TT], F32) for i in range(NB)]
    f_sb = [P.sb(f"f_sb{i}", [128, TT], F32) for i in range(NB)]
    k_sb = [P.sb(f"k_sb{i}", [128, TT], F32) for i in range(NB)]
    b_sb = [P.sb(f"b_sb{i}", [128, TT], F32) for i in range(NB)]
    eb_sb = [P.sb(f"eb_sb{i}", [128, TT], F32) for i in range(NB)]
    enb_sb = [P.sb(f"enb_sb{i}", [128, TT], F32) for i in range(NB)]
    qF = [P.sb(f"qF{i}", [128, NCH, 128], F32) for i in range(NB)]
    kF = [P.sb(f"kF{i}", [128, NCH, 128], F32) for i in range(NB)]
    Vb = [P.sb(f"Vb{i}", [128, NCH, 128], F32) for i in range(NB)]
    y_sb = [P.sb(f"y_sb{i}", [128, TT], F32) for i in range(NB)]
    for i in range(NB):
        for t_, n_ in ((qF[i], f"qF{i}"), (kF[i], f"kF{i}"), (Vb[i], f"Vb{i}")):
            P.memset(t_[:], 0.0, writes=[n_], eng="gpsimd")
    Tst = [P.sb(f"Tst{i}", [128, 128], F32) for i in range(2)]
    P.memset(Tst[0][:], 0.0, writes=["Tst0"])
    tmpT = P.sb("tmpT", [128, 128], F32)
    MTm = [P.sb(f"MTm{i}", [128, 128], F32) for i in range(2)]
    kTok = [P.sb(f"kTok{i}", [128, 128], F32) for i in range(2)]
    p_mt = [P.ps(f"p_mt{i}", [128, 512], F32) for i in range(2)]
    p_tr = [P.ps(f"p_tr{i}", [128, 512], F32) for i in range(2)]
    p_su = P.ps("p_su", [128, 512], F32)
    p_o = [P.ps(f"p_o{i}", [128, 512], F32) for i in range(2)]
    p_n = P.ps("p_n", [128, 512], F32)
    MT3 = [P.sb(f"MT3_{i}", [128, 128], F32) for i in range(3)]
    SU3 = [P.sb(f"SU3_{i}", [128, 128], F32) for i in range(3)]
    kTk = [P.sb(f"kTk{i}", [128, 128], F32) for i in range(2)]

    def prep(ti):
        i = ti % NB
        t0 = ti * TT
        zs, zn = z_sb[i], f"z{i}"
        P.dma(zs[:], zv[:, :, t0:t0 + TT], writes=[zn])
        c0 = ti * NCH
        for h in range(2):
            P.dma(Vb[i][64 * h:64 * h + 64, :, 64 * h:64 * h + 64], vtok[h, :, c0:c0 + NCH, :], writes=[(f"Vb{i}", h)], q="gpsimd")
        f, fn = f_sb[i], f"f{i}"
        P.act(f[:], zs[:, 1, :], AF.Sigmoid, reads=[zn], writes=[fn])
        P.ts(f[:], f[:], lbv[:, 1:2], lbv[:, 0:1], ALU.mult, ALU.add, reads=[fn, "lbv"], writes=[fn])
        k, kn = k_sb[i], f"k{i}"
        P.ts(k[:], f[:], -1.0, 1.0, ALU.mult, ALU.add, reads=[fn], writes=[kn])
        P.act(f[:], f[:], AF.Ln, reads=[fn], writes=[fn])
        bb, bn = b_sb[i], f"b{i}"
        P.op("vector", "tensor_tensor_scan", reads=[fn, "cmask"], writes=[bn], out=bb[:], data0=C["cmask"][:].rearrange("p c t -> p (c t)"),
             data1=f[:], initial=0.0, op0=ALU.mult, op1=ALU.add)
        eb, ebn = eb_sb[i], f"eb{i}"
        enb, enbn = enb_sb[i], f"enb{i}"
        P.act(eb[:], bb[:], AF.Exp, reads=[bn], writes=[ebn])
        P.act(enb[:], bb[:], AF.Exp, reads=[bn], writes=[enbn], scale=-1.0)
        P.act(zs[:, 0, :], zs[:, 0, :], AF.Silu, reads=[zn], writes=[zn])
        P.act(zs[:, 3, :], zs[:, 3, :], AF.Silu, reads=[zn], writes=[zn])
        bd_write(P, qF[i], f"qF{i}", zs[:, 0, :], zn, eb, ebn, NCH)
        bd_write(P, kF[i], f"kF{i}", k, kn, enb, enbn, NCH)

    def genA(ti, c, g):
        i = ti % NB
        j = g % 3
        s_ = g % 2
        pm, pmn = p_mt[s_], f"p_mt{s_}"
        ptr, ptrn = p_tr[s_], f"p_tr{s_}"
        kt, ktn = kTk[s_], f"kTk{s_}"
        P.mm(pm[:, 0:128], kF[i][:, c, :], qF[i][:, c, :], reads=[f"kF{i}", f"qF{i}"], writes=[pmn]); yield
        P.tt(MT3[j][:], pm[:, 0:128], C["mask_i"][:], ALU.mult, reads=[pmn, "mask_i"], writes=[f"MT3_{j}"]); yield
        P.tr(ptr[:, 0:128], kF[i][:, c, :], C["ident"][:], reads=[f"kF{i}", "ident"], writes=[ptrn]); yield
        P.cp(kt[:], ptr[:, 0:128], reads=[ptrn], writes=[ktn], eng="scalar"); yield
        P.mm(ptr[:, 128:256], kt[:], Vb[i][:, c, :], reads=[ktn, f"Vb{i}"], writes=[ptrn]); yield
        P.cp(SU3[j][:], ptr[:, 128:256], reads=[ptrn], writes=[f"SU3_{j}"], eng="scalar"); yield

    def genB(ti, c, g):
        i = ti % NB
        j = g % 3
        Tc, Tcn = Tst[g % 2], f"Tst{g % 2}"
        Tn, Tnn = Tst[1 - g % 2], f"Tst{1 - g % 2}"
        y, yn = y_sb[i], f"y{i}"
        po, pon = p_o[g % 2], f"p_o{g % 2}"
        pl = eb_sb[i][:, c * 64 + 63:c * 64 + 64]
        P.tt(tmpT[:], Tc[:], SU3[j][:], ALU.add, reads=[Tcn, f"SU3_{j}"], writes=["tmpT"]); yield
        P.ts(Tn[:], tmpT[:], pl, None, ALU.mult, reads=["tmpT", f"eb{i}"], writes=[Tnn]); yield
        P.mm(po[:, 0:128], Tc[:], qF[i][:, c, :], start=True, stop=False, reads=[Tcn, f"qF{i}"], writes=[pon]); yield
        P.mm(po[:, 0:128], Vb[i][:, c, :], MT3[j][:], start=False, stop=True, reads=[f"Vb{i}", f"MT3_{j}"], writes=[pon]); yield
        for h in range(2):
            P.cp(y[64 * h:64 * h + 64, c * 64:c * 64 + 64], po[64 * h:64 * h + 64, 64 * h:64 * h + 64], reads=[pon], writes=[(yn, (c, h))], eng="scalar"); yield

    def post(ti):
        i = ti % NB
        t0 = ti * TT
        zs, zn = z_sb[i], f"z{i}"
        k, kn = k_sb[i], f"k{i}"
        y, yn = y_sb[i], f"y{i}"
        P.act(tmpY[:], y[:], AF.Square, reads=[yn], writes=["tmpY"])
        P.mm(p_n[:, 0:TT], C["ones_bd"][:], tmpY[:], reads=["ones_bd", "tmpY"], writes=["p_n"])
        P.act(tmpY[:], p_n[:, 0:TT], AF.Ln, reads=["p_n"], writes=["tmpY"], bias=epsb[:, 0:1], scale=1.0)
        P.act(tmpY[:], tmpY[:], AF.Exp, reads=["tmpY"], writes=["tmpY"], scale=-0.5)
        P.stt(y[:], y[:], prm_sb[:, 2:3], tmpY[:], ALU.mult, ALU.mult, reads=[yn, "tmpY", "prm"], writes=[yn])
        P.tt(y[:], y[:], zs[:, 3, :], ALU.mult, reads=[yn, zn], writes=[yn])
        P.dma(yT[:, t0:t0 + TT], y[:], reads=[yn], is_output=True)

    tmpY = P.sb("tmpY", [128, TT], F32)
    seq = [(ti, c) for ti in range(ntile) for c in range(NCH)]
    pipeline3(P, seq, genA, genB, prep, post, NCH, ntile)
    return P.finish()


DEC = 0.6065306597126334


def build_rwkv(T, has_vres, dbg=0):
    P = Prog()
    TT = 512
    NCH = TT // 64
    ntile = T // TT
    zin = P.dram("zin", [384, T])
    lrin = P.dram("lrin", [160, T])
    prm = P.dram("prm", [128, 16])
    prm2 = P.dram("prm2", [96, 4])
    wlr = P.dram("wlr", [96, 4, 128])
    if has_vres:
        vlr = P.dram("vlr", [32, T])
        vfirst = P.dram("vfirst", [128, T])
    yT = P.dram("yT", [128, T], F32, kind="ExternalOutput")
    vout = P.dram("vout", [128, T], F32, kind="ExternalOutput")
    C = make_consts(P, need_strict=True)
    mA = P.sb("mA", [128, 256], F32)
    mK = P.sb("mK", [128, 256], F32)
    P.ts(mA[:, 0:128], C["mask_s"][:], -1.0, None, ALU.mult, reads=["mask_s"], writes=["mA"])
    P.cp(mA[:, 128:256], C["mask_i"][:], reads=["mask_i"], writes=["mA"])
    P.cp(mK[:, 0:128], C["mask_s"][:], reads=["mask_s"], writes=["mK"])
    P.cp(mK[:, 128:256], C["mask_i"][:], reads=["mask_i"], writes=["mK"])
    prm_sb = P.sb("prm_sb", [128, 16], F32)
    prm2_sb = P.sb("prm2_sb", [96, 4], F32)
    wlr_sb = P.sb("wlr_sb", [96, 4, 128], F32)
    P.dma(prm_sb[:], prm, writes=["prm"])
    P.dma(prm2_sb[:], prm2, writes=["prm2"])
    P.dma(wlr_sb[:], wlr, writes=["wlr"])
    omk = P.sb("omk", [128, 1], F32)
    P.ts(omk[:], prm_sb[:, 6:7], -1.0, 1.0, ALU.mult, ALU.add, reads=["prm"], writes=["omk"])
    cb = P.sb("cb", [128, 2], F32)
    P.memset(cb[:, 0:1], 1e-6, writes=["cb"])
    P.memset(cb[:, 1:2], 64e-5, writes=["cb"])

    def col(j):
        return prm_sb[:, j:j + 1]

    zv = zin.rearrange("(q p) t -> p q t", p=128)
    z_sb = [P.sb(f"z_sb{i}", [128, 3, 1 + TT], F32) for i in range(2)]
    l_sb = [P.sb(f"l_sb{i}", [96, 3, 1 + TT], F32) for i in range(2)]
    zl = [P.sb(f"zl{i}", [128, 3, TT], F32) for i in range(2)]
    ll = P.sb("ll", [96, 3, TT], F32)
    if has_vres:
        vl_sb = P.sb("vl_sb", [32, TT], F32)
        vf_sb = P.sb("vf_sb", [128, TT], F32)
    names1 = ["sw", "bb", "bx", "a_s", "kkr", "t1", "kk", "kmod", "alpha", "enb", "ebx"]
    S1 = {n: P.sb("s_" + n, [128, TT], F32) for n in names1}
    eb = [P.sb(f"eb{i}", [128, TT], F32) for i in range(2)]
    g_sb = [P.sb(f"g_sb{i}", [128, TT], F32) for i in range(2)]
    bv = [P.sb(f"bv{i}", [128, TT], F32) for i in range(2)]
    y_sb = [P.sb(f"y_sb{i}", [128, TT], F32) for i in range(2)]
    aF = [P.sb(f"aF{i}", [128, NCH, 128], BF16) for i in range(2)]
    kF = [P.sb(f"kF{i}", [128, NCH, 128], BF16) for i in range(2)]
    cq = [P.sb(f"cq{i}", [128, NCH, 256], BF16) for i in range(2)]
    vTb = P.sb("vTb", [128, NCH, 128], BF16)
    Vb = [P.sb(f"Vb{i}", [128, NCH, 128], BF16) for i in range(2)]
    aTok = [P.sb(f"aTok{i}", [128, NCH, 128], BF16) for i in range(2)]
    kTok = [P.sb(f"kTok{i}", [128, NCH, 128], BF16) for i in range(2)]
    for i in range(2):
        for t_, n_ in ((aF[i], f"aF{i}"), (kF[i], f"kF{i}"), (cq[i], f"cq{i}")):
            P.memset(t_[:], 0.0, writes=[n_], eng="gpsimd")
    P.memset(vTb[:], 0.0, writes=["vTb"], eng="gpsimd")
    Tst = [P.sb(f"Tst{i}", [128, 128], F32) for i in range(2)]
    P.memset(Tst[0][:], 0.0, writes=["Tst0"])
    tmpT = P.sb("tmpT", [128, 128], F32)
    Tbf = [P.sb(f"Tbf{i}", [128, 128], BF16) for i in range(2)]
    P.memset(Tbf[0][:], 0.0, writes=["Tbf0"])
    identb = P.sb("identb", [128, 128], BF16)
    P.cp(identb[:], C["ident"][:], reads=["ident"], writes=["identb"])
    MA = [P.sb(f"MA{i}", [128, 256], F32) for i in range(2)]
    MK = [P.sb(f"MK{i}", [128, 256], F32) for i in range(2)]
    YY = [P.sb(f"YY{i}", [128, 256], F32) for i in range(2)]
    PI = [P.sb(f"PI{i}", [128, 128], F32) for i in range(2)]
    W0s = P.sb("W0s", [128, 128], BF16)
    Us = P.sb("Us", [128, 128], BF16)
    pA = P.ps("pA", [128, 512], F32)
    pB = P.ps("pB", [128, 512], F32)
    pMA = P.ps("pMA", [128, 512], F32)
    pPi = P.ps("pPi", [128, 512], F32)
    pY = P.ps("pY", [128, 512], F32)
    pW = P.ps("pW", [128, 512], F32)
    pU = P.ps("pU", [128, 512], F32)
    pO = P.ps("pO", [128, 512], F32)

    def prep(ti):
        i = ti % 2
        t0 = ti * TT
        zs, zn = z_sb[i], f"z{i}"
        ls, ln_ = l_sb[i], f"l{i}"
        if ti == 0:
            P.memset(zs[:, :, 0:1], 0.0, writes=[zn])
            P.memset(ls[:, :, 0:1], 0.0, writes=[ln_])
            P.dma(zs[:, :, 1:1 + TT], zv[:, :, 0:TT], writes=[zn])
            P.dma(ls[0:32, 0, 1:1 + TT], lrin[0:32, 0:TT], writes=[ln_])
            P.dma(ls[0:32, 1, 1:1 + TT], lrin[32:64, 0:TT], writes=[ln_])
            P.dma(ls[0:96, 2, 1:1 + TT], lrin[64:160, 0:TT], writes=[ln_])
        else:
            P.dma(zs[:], zv[:, :, t0 - 1:t0 + TT], writes=[zn])
            P.dma(ls[0:32, 0, :], lrin[0:32, t0 - 1:t0 + TT], writes=[ln_])
            P.dma(ls[0:32, 1, :], lrin[32:64, t0 - 1:t0 + TT], writes=[ln_])
            P.dma(ls[0:96, 2, :], lrin[64:160, t0 - 1:t0 + TT], writes=[ln_])
        z, zln = zl[i], f"zl{i}"
        P.tt(z[:], zs[:, :, 0:TT], zs[:, :, 1:1 + TT], ALU.subtract, reads=[zn], writes=[zln])
        for q in range(3):
            P.stt(z[:, q, :], z[:, q, :], col(q), zs[:, q, 1:1 + TT], ALU.mult, ALU.add, reads=[zln, zn, "prm"], writes=[zln])
        for q, rows in ((0, 32), (1, 32), (2, 96)):
            P.tt(ll[0:rows, q, :], ls[0:rows, q, 0:TT], ls[0:rows, q, 1:1 + TT], ALU.subtract, reads=[ln_], writes=["ll"])
            P.stt(ll[0:rows, q, :], ll[0:rows, q, :], prm2_sb[0:rows, q:q + 1], ls[0:rows, q, 1:1 + TT], ALU.mult, ALU.add, reads=["ll", ln_, "prm2"], writes=["ll"])
        r_, k_, v_ = z[:, 0, :], z[:, 1, :], z[:, 2, :]
        if has_vres:
            P.dma(vl_sb[:], vlr[:, t0:t0 + TT], writes=["vl"])
            P.dma(vf_sb[:], vfirst[:, t0:t0 + TT], writes=["vf"])
            P.mm(pA[:, 0:TT], wlr_sb[0:32, 3, :], vl_sb[:], reads=["wlr", "vl"], writes=["pA"])
            P.act(S1["t1"][:], pA[:, 0:TT], AF.Sigmoid, reads=["pA", "prm"], writes=["t1"], bias=col(10), scale=1.0)
            P.tt(vf_sb[:], vf_sb[:], v_, ALU.subtract, reads=["vf", zln], writes=["vf"])
            P.tt(vf_sb[:], vf_sb[:], S1["t1"][:], ALU.mult, reads=["vf", "t1"], writes=["vf"])
            P.tt(v_, v_, vf_sb[:], ALU.add, reads=["vf", zln], writes=[zln])
        P.dma(vout[:, t0:t0 + TT], v_, reads=[zln], is_output=True)
        P.act(ll[0:32, 0, :], ll[0:32, 0, :], AF.Tanh, reads=["ll"], writes=["ll"])
        P.mm(pA[:, 0:TT], wlr_sb[0:32, 0, :], ll[0:32, 0, :], reads=["wlr", "ll"], writes=["pA"])
        P.act(S1["sw"][:], pA[:, 0:TT], AF.Sigmoid, reads=["pA", "prm"], writes=["sw"], bias=col(3), scale=1.0)
        P.op("vector", "tensor_tensor_scan", reads=["sw", "cmask"], writes=["bb"], out=S1["bb"][:], data0=C["cmask"][:].rearrange("p c t -> p (c t)"),
             data1=S1["sw"][:], initial=0.0, op0=ALU.mult, op1=ALU.add)
        P.tt(S1["bx"][:], S1["bb"][:], S1["sw"][:], ALU.subtract, reads=["bb", "sw"], writes=["bx"])
        e_, en = eb[i], f"eb{i}"
        P.act(e_[:], S1["bb"][:], AF.Exp, reads=["bb"], writes=[en], scale=-DEC)
        P.act(S1["enb"][:], S1["bb"][:], AF.Exp, reads=["bb"], writes=["enb"], scale=DEC)
        P.act(S1["ebx"][:], S1["bx"][:], AF.Exp, reads=["bx"], writes=["ebx"], scale=-DEC)
        P.mm(pB[:, 0:TT], wlr_sb[0:32, 1, :], ll[0:32, 1, :], reads=["wlr", "ll"], writes=["pB"])
        P.act(S1["a_s"][:], pB[:, 0:TT], AF.Sigmoid, reads=["pB", "prm"], writes=["a_s"], bias=col(4), scale=1.0)
        P.act(ll[0:96, 2, :], ll[0:96, 2, :], AF.Sigmoid, reads=["ll"], writes=["ll"])
        P.mm(pA[:, 0:TT], wlr_sb[0:96, 2, :], ll[0:96, 2, :], reads=["wlr", "ll"], writes=["pA"])
        P.cp(g_sb[i][:], pA[:, 0:TT], reads=["pA"], writes=[f"g{i}"], eng="scalar")
        P.ts(S1["kkr"][:], k_, col(5), None, ALU.mult, reads=[zln, "prm"], writes=["kkr"])
        P.act(S1["t1"][:], S1["kkr"][:], AF.Square, reads=["kkr"], writes=["t1"])
        P.mm(pB[:, 0:TT], C["ones_bd"][:], S1["t1"][:], reads=["ones_bd", "t1"], writes=["pB"])
        P.act(S1["t1"][:], pB[:, 0:TT], AF.Ln, reads=["pB", "cb"], writes=["t1"], bias=cb[:, 0:1], scale=64.0)
        P.act(S1["t1"][:], S1["t1"][:], AF.Exp, reads=["t1"], writes=["t1"], scale=-0.5)
        P.tt(S1["kk"][:], S1["kkr"][:], S1["t1"][:], ALU.mult, reads=["kkr", "t1"], writes=["kk"])
        P.ts(S1["kmod"][:], S1["a_s"][:], col(6), omk[:, 0:1], ALU.mult, ALU.add, reads=["a_s", "prm", "omk"], writes=["kmod"])
        P.tt(S1["kmod"][:], S1["kmod"][:], k_, ALU.mult, reads=["kmod", zln], writes=["kmod"])
        P.tt(S1["alpha"][:], S1["kk"][:], S1["a_s"][:], ALU.mult, reads=["kk", "a_s"], writes=["alpha"])
        P.stt(S1["t1"][:], r_, col(7), S1["kmod"][:], ALU.mult, ALU.mult, reads=[zln, "prm", "kmod"], writes=["t1"])
        P.mm(pA[:, 0:TT], C["ones_bd"][:], S1["t1"][:], reads=["ones_bd", "t1"], writes=["pA"])
        P.stt(bv[i][:], pA[:, 0:TT], 64.0, v_, ALU.mult, ALU.mult, reads=["pA", zln], writes=[f"bv{i}"])
        bd_write(P, aF[i], f"aF{i}", S1["alpha"], "alpha", S1["enb"], "enb", NCH)
        bd_write(P, kF[i], f"kF{i}", S1["kmod"], "kmod", S1["enb"], "enb", NCH)
        bd_write(P, cq[i][:, :, 0:128], f"cq{i}", S1["kk"], "kk", S1["ebx"], "ebx", NCH)
        bd_write(P, cq[i][:, :, 128:256], f"cq{i}", z[:, 0, :], zln, e_, en, NCH)
        for h in range(2):
            pp = slice(64 * h, 64 * h + 64)
            P.cp(vTb[pp, :, 64 * h:64 * h + 64], z[pp, 2, :].rearrange("p (c t) -> p c t", t=64), reads=[zln], writes=[("vTb", h)])
        for (src, sn, dst, dn) in ((vTb, "vTb", Vb[i], f"Vb{i}"), (aF[i], f"aF{i}", aTok[i], f"aTok{i}"), (kF[i], f"kF{i}", kTok[i], f"kTok{i}")):
            for half in range(2):
                pt, ptn = (pA, "pA") if half == 0 else (pB, "pB")
                ptb = pt[:, 0:256].bitcast(BF16)
                for cc in range(4):
                    c = half * 4 + cc
                    P.tr(ptb[:, cc * 128:(cc + 1) * 128], src[:, c, :], identb[:], reads=[sn, "identb"], writes=[ptn])
                P.cp(dst[:, half * 4:half * 4 + 4, :], ptb.rearrange("p (c t) -> p c t", t=128), reads=[ptn], writes=[(dn, half)], eng="scalar")

    PIF = [P.sb(f"PIF{i}", [128, 128], BF16) for i in range(3)]
    MA3 = [P.sb(f"MA3_{i}", [128, 256], BF16) for i in range(3)]
    MK3 = [P.sb(f"MK3_{i}", [128, 256], BF16) for i in range(3)]
    YYs = [[P.sb(f"YYs{s_}_{i}", [128, 256], BF16) for i in range(2)] for s_ in range(2)]
    PIs = [[P.sb(f"PIs{s_}_{i}", [128, 128], BF16) for i in range(2)] for s_ in range(2)]
    pYs = [pMA, pY]
    pPs = [pPi, pW]

    def genA(ti, c, g):
        i = ti % 2
        j = g % 3
        s_ = g % 2
        pYb, pYn = pYs[s_], f"pYs{s_}"
        pPb, pPn = pPs[s_], f"pPs{s_}"
        ma, man = MA3[j], f"MA3_{j}"
        mk, mkn = MK3[j], f"MK3_{j}"
        P.mm(pPb[:, 0:256], aF[i][:, c, :], cq[i][:, c, :], reads=[f"aF{i}", f"cq{i}"], writes=[pPn]); yield
        P.mm(pPb[:, 256:512], kF[i][:, c, :], cq[i][:, c, :], reads=[f"kF{i}", f"cq{i}"], writes=[pPn]); yield
        P.tt(ma[:], pPb[:, 0:256], mA[:], ALU.mult, reads=[pPn, "mA"], writes=[man]); yield
        P.tt(mk[:], pPb[:, 256:512], mK[:], ALU.mult, reads=[pPn, "mK"], writes=[mkn]); yield
        yt, ytn = YYs[s_][0], f"YYs{s_}_0"
        pYt = pYb[:, 256:320].bitcast(BF16)
        P.tr(pYt, ma[:, 0:128], identb[:], reads=[man, "identb"], writes=[pYn]); yield
        P.cp(yt[:, 128:256], pYt, reads=[pYn], writes=[ytn], eng="scalar"); yield
        Yap, Yn_ = ma[:, 0:128], man
        YTap, YTn = yt[:, 128:256], ytn
        pi, pin = PIs[s_][0], f"PIs{s_}_0"
        P.tt(pi[:], ma[:, 0:128], C["ident"][:], ALU.add, reads=[man, "ident"], writes=[pin]); yield
        for k in range(1, 6):
            y2, y2n = YYs[s_][k % 2], f"YYs{s_}_{k % 2}"
            if k < 5:
                P.mm(pYb[:, 0:128], YTap, Yap, reads=[Yn_, YTn], writes=[pYn]); yield
            P.mm(pYb[:, 128:256], Yap, YTap, reads=[Yn_, YTn], writes=[pYn]); yield
            if k < 5:
                P.cp(y2[:], pYb[:, 0:256], reads=[pYn], writes=[y2n], eng="scalar"); yield
            else:
                P.cp(y2[:, 128:256], pYb[:, 128:256], reads=[pYn], writes=[y2n], eng="scalar"); yield
            if k < 5:
                pi2, pi2n = PIs[s_][k % 2], f"PIs{s_}_{k % 2}"
            else:
                pi2, pi2n = PIF[j], f"PIF{j}"
            P.mm(pPb[:, 0:128], y2[:, 128:256], pi[:], reads=[y2n, pin], writes=[pPn]); yield
            P.tt(pi2[:], pPb[:, 0:128], pi[:], ALU.add, reads=[pPn, pin], writes=[pi2n]); yield
            Yap, Yn_, YTap, YTn, pi, pin = y2[:, 0:128], y2n, y2[:, 128:256], y2n, pi2, pi2n

    def genB(ti, c, g):
        i = ti % 2
        j = g % 3
        y, yn = y_sb[i], f"y{i}"
        Tc, Tcn = Tst[g % 2], f"Tst{g % 2}"
        Tn, Tnn = Tst[1 - g % 2], f"Tst{1 - g % 2}"
        Tcb, Tcbn = Tbf[g % 2], f"Tbf{g % 2}"
        Tnb, Tnbn = Tbf[1 - g % 2], f"Tbf{1 - g % 2}"
        ma, man = MA3[j], f"MA3_{j}"
        mk, mkn = MK3[j], f"MK3_{j}"
        pi, pin = PIF[j], f"PIF{j}"
        pl = eb[i][:, c * 64 + 63:c * 64 + 64]
        P.ts(tmpT[:], Tc[:], pl, None, ALU.mult, reads=[Tcn, f"eb{i}"], writes=["tmpT"]); yield
        P.mm(pU[:, 0:128], cq[i][:, c, 0:128], Tcb[:], start=True, stop=False, reads=[f"cq{i}", Tcbn], writes=["pU"]); yield
        P.mm(pU[:, 0:128], mk[:, 0:128], Vb[i][:, c, :], start=False, stop=True, reads=[mkn, f"Vb{i}"], writes=["pU"]); yield
        P.act(W0s[:], pU[:, 0:128], AF.Identity, reads=["pU"], writes=["W0s"], scale=-1.0); yield
        P.mm(pU[:, 128:256], pi[:], W0s[:], reads=[pin, "W0s"], writes=["pU"]); yield
        P.cp(Us[:], pU[:, 128:256], reads=["pU"], writes=["Us"], eng="scalar"); yield
        P.mm(pU[:, 256:384], aTok[i][:, c, :], Us[:], start=True, stop=False, reads=[f"aTok{i}", "Us"], writes=["pU"]); yield
        P.mm(pU[:, 256:384], kTok[i][:, c, :], Vb[i][:, c, :], start=False, stop=True, reads=[f"kTok{i}", f"Vb{i}"], writes=["pU"]); yield
        P.stt(Tnb[:], pU[:, 256:384], pl, tmpT[:], ALU.mult, ALU.add, reads=["pU", f"eb{i}", "tmpT"], writes=[Tnbn]); yield
        P.stt(Tn[:], pU[:, 256:384], pl, tmpT[:], ALU.mult, ALU.add, reads=["pU", f"eb{i}", "tmpT"], writes=[Tnn]); yield
        P.mm(pO[:, 0:128], Tcb[:], cq[i][:, c, 128:256], start=True, stop=False, reads=[Tcbn, f"cq{i}"], writes=["pO"]); yield
        P.mm(pO[:, 0:128], Us[:], ma[:, 128:256], start=False, stop=False, reads=["Us", man], writes=["pO"]); yield
        P.mm(pO[:, 0:128], Vb[i][:, c, :], mk[:, 128:256], start=False, stop=True, reads=[f"Vb{i}", mkn], writes=["pO"]); yield
        for h in range(2):
            P.cp(y[64 * h:64 * h + 64, c * 64:c * 64 + 64], pO[64 * h:64 * h + 64, 64 * h:64 * h + 64], reads=["pO"], writes=[(yn, (c, h))], eng="scalar"); yield

    def post(ti):
        i = ti % 2
        t0 = ti * TT
        y, yn = y_sb[i], f"y{i}"
        t1 = S1["t1"]
        P.mm(pA[:, 0:TT], C["ones_bd"][:], y[:], reads=["ones_bd", yn], writes=["pA"])
        P.tt(y[:], y[:], pA[:, 0:TT], ALU.subtract, reads=[yn, "pA"], writes=[yn])
        P.act(t1[:], y[:], AF.Square, reads=[yn], writes=["t1"])
        P.mm(pB[:, 0:TT], C["ones_bd"][:], t1[:], reads=["ones_bd", "t1"], writes=["pB"])
        P.act(t1[:], pB[:, 0:TT], AF.Ln, reads=["pB", "cb"], writes=["t1"], bias=cb[:, 1:2], scale=1.0)
        P.act(t1[:], t1[:], AF.Exp, reads=["t1"], writes=["t1"], scale=-0.5)
        P.stt(y[:], y[:], col(8), t1[:], ALU.mult, ALU.mult, reads=[yn, "prm", "t1"], writes=[yn])
        P.stt(y[:], y[:], col(9), bv[i][:], ALU.add, ALU.add, reads=[yn, "prm", f"bv{i}"], writes=[yn])
        P.tt(y[:], y[:], g_sb[i][:], ALU.mult, reads=[yn, f"g{i}"], writes=[yn])
        P.dma(yT[:, t0:t0 + TT], y[:], reads=[yn], is_output=True)

    seq = [(ti, c) for ti in range(ntile) for c in range(NCH)]
    pipeline3(P, seq, genA, genB, prep, post, NCH, ntile)
    return P.finish()


def build_gdn(T, CH=128):
    P = Prog()
    TT = 512
    NCH = TT // CH
    NR = {64: 5, 128: 6}[CH]
    ntile = T // TT
    NC_ALL = T // CH
    zin = P.dram("zin", [512, T])
    abrow = P.dram("abrow", [2, T])
    abcol = P.dram("abcol", [CH, 2, NC_ALL])
    prm = P.dram("prm", [128, 16])
    yT = P.dram("yT", [128, T], F32, kind="ExternalOutput")
    ident = P.sb("ident", [128, 128], F32)
    P.memset(ident[:], 1.0, writes=["ident"], eng="gpsimd")
    P.op("gpsimd", "affine_select", reads=["ident"], writes=["ident"], out=ident[:], in_=ident[:], pattern=[[1, 128]],
         compare_op=ALU.is_equal, fill=0.0, base=0, channel_multiplier=-1)
    m2 = P.sb("m2", [CH, 2, CH], F32)
    P.memset(m2[:], 1.0, writes=["m2"], eng="gpsimd")
    P.op("gpsimd", "affine_select", reads=["m2"], writes=["m2"], out=m2[:, 0, :], in_=m2[:, 0, :], pattern=[[1, CH]],
         compare_op=ALU.is_gt, fill=0.0, base=0, channel_multiplier=-1)
    P.op("gpsimd", "affine_select", reads=["m2"], writes=["m2"], out=m2[:, 1, :], in_=m2[:, 1, :], pattern=[[1, CH]],
         compare_op=ALU.is_ge, fill=0.0, base=0, channel_multiplier=-1)
    P.ts(m2[:, 0, :], m2[:, 0, :], -1.0, None, ALU.mult, reads=["m2"], writes=["m2"])
    tri = P.sb("tri", [CH, CH], F32)
    P.cp(tri[:], m2[:, 1, :], reads=["m2"], writes=["tri"])
    o64 = P.sb("o64", [CH, CH], F32)
    P.memset(o64[:], 1.0, writes=["o64"])
    o128 = P.sb("o128", [128, 128], F32)
    P.memset(o128[:], 1.0, writes=["o128"])
    cm = P.sb("cmask", [128, NCH, CH], F32)
    P.memset(cm[:], 1.0, writes=["cmask"], eng="gpsimd")
    P.memset(cm[:, :, 0:1], 0.0, writes=["cmask"], eng="gpsimd")
    prm_sb = P.sb("prm_sb", [128, 16], F32)
    P.dma(prm_sb[:], prm, writes=["prm"])
    cb = P.sb("cb", [128, 4], F32)
    P.memset(cb[:, 0:1], 1e-6, writes=["cb"])
    P.memset(cb[:, 1:2], 1.0, writes=["cb"])
    P.memset(cb[:, 2:3], -0.5 * float(np.log(128.0)), writes=["cb"])
    P.act(cb[:, 3:4], prm_sb[:, 12:13], AF.Exp, reads=["prm"], writes=["cb"])
    P.ts(cb[:, 3:4], cb[:, 3:4], -1.0, None, ALU.mult, reads=["cb"], writes=["cb"])

    def col(j):
        return prm_sb[:, j:j + 1]

    pA = P.ps("pA", [128, 512], F32)
    pB = P.ps("pB", [128, 512], F32)
    pM = P.ps("pM", [128, 512], F32)
    pY = P.ps("pY", [128, 512], F32)
    pP = P.ps("pP", [128, 512], F32)
    pUW = P.ps("pUW", [128, 512], F32)
    pV = P.ps("pV", [128, 512], F32)
    pO = P.ps("pO", [128, 512], F32)

    ac = P.sb("ac", [CH, 2, NC_ALL], F32)
    P.dma(ac[:], abcol, writes=["ac"])
    betac = P.sb("betac", [CH, NC_ALL], F32)
    gcol = P.sb("gcol", [CH, NC_ALL], F32)
    gccol = P.sb("gccol", [CH, NC_ALL], F32)
    c2 = P.sb("c2", [CH, NC_ALL], F32)
    c3 = P.sb("c3", [CH, NC_ALL], F32)
    P.act(betac[:], ac[:, 0, :], AF.Sigmoid, reads=["ac"], writes=["betac"])
    P.act(gcol[:], ac[:, 1, :], AF.Exp, reads=["ac", "prm"], writes=["gcol"], bias=prm_sb[0:CH, 13:14], scale=1.0)
    P.act(gcol[:], gcol[:], AF.Ln, reads=["gcol", "cb"], writes=["gcol"], bias=cb[0:CH, 1:2], scale=1.0)
    P.ts(gcol[:], gcol[:], cb[0:CH, 3:4], None, ALU.mult, reads=["gcol", "cb"], writes=["gcol"])
    for c0 in range(0, NC_ALL, 512):
        n = min(512, NC_ALL - c0)
        P.mm(pA[0:CH, 0:n], tri[:], gcol[:, c0:c0 + n], reads=["tri", "gcol"], writes=["pA"])
        P.cp(gccol[:, c0:c0 + n], pA[0:CH, 0:n], reads=["pA"], writes=["gccol"], eng="scalar")
        P.mm(pB[0:CH, 0:n], o64[:], gcol[:, c0:c0 + n], reads=["o64", "gcol"], writes=["pB"])
        P.tt(c3[:, c0:c0 + n], pB[0:CH, 0:n], gccol[:, c0:c0 + n], ALU.subtract, reads=["pB", "gccol"], writes=["c3"])
    P.act(c3[:], c3[:], AF.Exp, reads=["c3"], writes=["c3"])
    P.act(c2[:], gccol[:], AF.Exp, reads=["gccol"], writes=["c2"])
    P.tt(c2[:], c2[:], betac[:], ALU.mult, reads=["c2", "betac"], writes=["c2"])

    zv = zin.rearrange("(q p) t -> p q t", p=128)
    z_sb = [P.sb(f"z_sb{i}", [128, 4, 3 + TT], F32) for i in range(2)]
    ab_sb = P.sb("ab_sb", [128, 2, TT], F32)
    cv_sb = P.sb("cv_sb", [128, 3, TT], F32)
    sq_sb = P.sb("sq_sb", [128, TT], F32)
    rn_sb = P.sb("rn_sb", [128, TT], F32)
    gcb = [P.sb(f"gcb{i}", [128, TT], F32) for i in range(2)]
    egc = [P.sb(f"egc{i}", [128, TT], F32) for i in range(2)]
    kT = [P.sb(f"kT{i}", [128, TT], F32) for i in range(2)]
    kq = [P.sb(f"kq{i}", [128, NCH, 2 * CH], F32) for i in range(2)]
    qg = [P.sb(f"qg{i}", [128, TT], F32) for i in range(2)]
    sg = [P.sb(f"sg{i}", [128, TT], F32) for i in range(2)]
    DTm = [P.sb(f"DTm{i}", [CH, NCH, 2, CH], F32) for i in range(2)]
    dt_tmp = P.sb("dt_tmp", [CH, NCH, CH], F32)
    RV = [P.sb(f"RV{i}", [CH, NCH, 128], F32) for i in range(2)]
    RK = [P.sb(f"RK{i}", [CH, NCH, 128], F32) for i in range(2)]
    KS = [P.sb(f"KS{i}", [CH, NCH, 128], F32) for i in range(2)]
    y_sb = [P.sb(f"y_sb{i}", [128, TT], F32) for i in range(2)]
    Sst = [P.sb(f"Sst{i}", [128, 128], F32) for i in range(2)]
    P.memset(Sst[0][:], 0.0, writes=["Sst0"])
    tmpS = P.sb("tmpS", [128, 128], F32)
    u_sb = P.sb("u_sb", [CH, 128], F32)
    wT_sb = P.sb("wT_sb", [128, CH], F32)
    vn_sb = P.sb("vn_sb", [CH, 128], F32)

    def prep(ti):
        i = ti % 2
        t0 = ti * TT
        c0 = ti * NCH
        zs, zn = z_sb[i], f"z{i}"
        if ti == 0:
            P.memset(zs[:, :, 0:3], 0.0, writes=[zn])
            P.dma(zs[:, :, 3:3 + TT], zv[:, :, 0:TT], writes=[zn])
        else:
            P.dma(zs[:], zv[:, :, t0 - 3:t0 + TT], writes=[zn])
        P.dma(ab_sb[:, 0, :], abrow[0:1, t0:t0 + TT].to_broadcast([128, TT]), writes=["ab"])
        P.dma(ab_sb[:, 1, :], abrow[1:2, t0:t0 + TT].to_broadcast([128, TT]), writes=["ab"])
        for q in range(3):
            P.ts(cv_sb[:, q, :], zs[:, q, 3:3 + TT], col(4 * q + 3), None, ALU.mult, reads=[zn, "prm"], writes=[("cv", q)])
            for j in range(3):
                P.stt(cv_sb[:, q, :], zs[:, q, j:j + TT], col(4 * q + j), cv_sb[:, q, :], ALU.mult, ALU.add, reads=[zn, "prm", ("cv", q)], writes=[("cv", q)])
        P.act(cv_sb[:], cv_sb[:], AF.Silu, reads=["cv"], writes=["cv"])
        P.act(sg[i][:], zs[:, 3, 3:3 + TT], AF.Silu, reads=[zn], writes=[f"sg{i}"])
        P.act(ab_sb[:, 0, :], ab_sb[:, 0, :], AF.Sigmoid, reads=["ab"], writes=["ab"])
        P.act(ab_sb[:, 1, :], ab_sb[:, 1, :], AF.Exp, reads=["ab", "prm"], writes=["ab"], bias=col(13), scale=1.0)
        P.act(ab_sb[:, 1, :], ab_sb[:, 1, :], AF.Ln, reads=["ab", "cb"], writes=["ab"], bias=cb[:, 1:2], scale=1.0)
        P.ts(ab_sb[:, 1, :], ab_sb[:, 1, :], cb[:, 3:4], None, ALU.mult, reads=["ab", "cb"], writes=["ab"])
        P.op("vector", "tensor_tensor_scan", reads=["ab", "cmask"], writes=[f"gcb{i}"], out=gcb[i][:], data0=cm[:].rearrange("p c t -> p (c t)"),
             data1=ab_sb[:, 1, :], initial=0.0, op0=ALU.mult, op1=ALU.add)
        P.act(egc[i][:], gcb[i][:], AF.Exp, reads=[f"gcb{i}"], writes=[f"egc{i}"])
        for q in range(2):
            P.act(sq_sb[:], cv_sb[:, q, :], AF.Square, reads=["cv"], writes=["sq"])
            pp, ppn = (pA, "pA") if q == 0 else (pB, "pB")
            P.mm(pp[:, 0:TT], o128[:], sq_sb[:], reads=["o128", "sq"], writes=[ppn])
            P.act(rn_sb[:], pp[:, 0:TT], AF.Ln, reads=[ppn, "cb"], writes=["rn"], bias=cb[:, 0:1], scale=1.0)
            if q == 0:
                P.act(rn_sb[:], rn_sb[:], AF.Exp, reads=["rn", "cb"], writes=["rn"], bias=cb[:, 2:3], scale=-0.5)
                P.tt(kq[i][:, :, CH:2 * CH], cv_sb[:, 0, :].rearrange("p (c t) -> p c t", t=CH), rn_sb[:].rearrange("p (c t) -> p c t", t=CH), ALU.mult,
                     reads=["cv", "rn"], writes=[(f"kq{i}", 1)])
            else:
                P.act(rn_sb[:], rn_sb[:], AF.Exp, reads=["rn"], writes=["rn"], scale=-0.5)
                P.tt(kT[i][:], cv_sb[:, 1, :], rn_sb[:], ALU.mult, reads=["cv", "rn"], writes=[f"kT{i}"])
        P.tt(qg[i][:].rearrange("p (c t) -> p c t", t=CH), kq[i][:, :, CH:2 * CH], egc[i][:].rearrange("p (c t) -> p c t", t=CH), ALU.mult,
             reads=[f"kq{i}", f"egc{i}"], writes=[f"qg{i}"])
        P.tt(kq[i][:, :, 0:CH], kT[i][:].rearrange("p (c t) -> p c t", t=CH), ab_sb[:, 0, :].rearrange("p (c t) -> p c t", t=CH), ALU.mult,
             reads=[f"kT{i}", "ab"], writes=[(f"kq{i}", 0)])
        P.tt(dt_tmp[:], gcb[i][0:CH, :].rearrange("p (c t) -> p c t", t=CH), gccol[:, c0:c0 + NCH].unsqueeze(2).to_broadcast([CH, NCH, CH]), ALU.subtract,
             reads=[f"gcb{i}", "gccol"], writes=["dt_tmp"])
        P.ts(dt_tmp[:], dt_tmp[:], 0.0, None, ALU.min, reads=["dt_tmp"], writes=["dt_tmp"])
        P.act(dt_tmp[:], dt_tmp[:], AF.Exp, reads=["dt_tmp"], writes=["dt_tmp"])
        for w in range(2):
            P.tt(DTm[i][:, :, w, :], dt_tmp[:], m2[:, w, :].unsqueeze(1).to_broadcast([CH, NCH, CH]), ALU.mult, reads=["dt_tmp", "m2"], writes=[(f"DTm{i}", w)])
        NG = min(NCH, 512 // 128)
        for grp in range(NCH // NG):
            for cc in range(NG):
                c = grp * NG + cc
                P.tr(pA[0:CH, cc * 128:(cc + 1) * 128], kT[i][:, c * CH:(c + 1) * CH], ident[:], reads=[f"kT{i}", "ident"], writes=["pA"])
                P.tr(pB[0:CH, cc * 128:(cc + 1) * 128], cv_sb[:, 2, c * CH:(c + 1) * CH], ident[:], reads=["cv", "ident"], writes=["pB"])
            cs = slice(c0 + grp * NG, c0 + grp * NG + NG)
            hs = slice(grp * NG, grp * NG + NG)
            pa3 = pA[0:CH, 0:NG * 128].rearrange("p (c t) -> p c t", t=128)
            pb3 = pB[0:CH, 0:NG * 128].rearrange("p (c t) -> p c t", t=128)
            P.tt(RK[i][:, hs, :], pa3, c2[:, cs].unsqueeze(2).to_broadcast([CH, NG, 128]), ALU.mult, reads=["pA", "c2"], writes=[(f"RK{i}", grp)])
            P.tt(KS[i][:, hs, :], pa3, c3[:, cs].unsqueeze(2).to_broadcast([CH, NG, 128]), ALU.mult, reads=["pA", "c3"], writes=[(f"KS{i}", grp)])
            P.tt(RV[i][:, hs, :], pb3, betac[:, cs].unsqueeze(2).to_broadcast([CH, NG, 128]), ALU.mult, reads=["pB", "betac"], writes=[(f"RV{i}", grp)])

    M23 = [P.sb(f"M23_{i}", [CH, 2 * CH], F32) for i in range(3)]
    PIF = [P.sb(f"PIF{i}", [CH, CH], F32) for i in range(3)]
    YYs = [[P.sb(f"YYs{s_}_{i}", [CH, 2 * CH], F32) for i in range(2)] for s_ in range(2)]
    PIs = [[P.sb(f"PIs{s_}_{i}", [CH, CH], F32) for i in range(2)] for s_ in range(2)]
    pYs = [pM, pY]
    pPs = [pP, pUW]

    def genA(ti, c, g):
        i = ti % 2
        j = g % 3
        s_ = g % 2
        pYb, pYn = pYs[s_], f"pYs{s_}"
        pPb, pPn = pPs[s_], f"pPs{s_}"
        mm2, m2n = M23[j], f"M23_{j}"
        cs = slice(c * CH, c * CH + CH)
        P.mm(pPb[0:CH, 0:2 * CH], kT[i][:, cs], kq[i][:, c, :], reads=[f"kT{i}", f"kq{i}"], writes=[pPn]); yield
        P.tt(mm2[:], pPb[0:CH, 0:2 * CH], DTm[i][:, c, :, :].rearrange("p w t -> p (w t)"), ALU.mult, reads=[pPn, f"DTm{i}"], writes=[m2n]); yield
        yt, ytn = YYs[s_][0], f"YYs{s_}_0"
        P.tr(pYb[0:CH, 0:CH], mm2[:, 0:CH], ident[0:CH, 0:CH], reads=[m2n, "ident"], writes=[pYn]); yield
        P.cp(yt[:, CH:2 * CH], pYb[0:CH, 0:CH], reads=[pYn], writes=[ytn], eng="scalar"); yield
        Yap, Yn_ = mm2[:, 0:CH], m2n
        YTap, YTn = yt[:, CH:2 * CH], ytn
        pi, pin = PIs[s_][0], f"PIs{s_}_0"
        P.tt(pi[:], mm2[:, 0:CH], ident[0:CH, 0:CH], ALU.add, reads=[m2n, "ident"], writes=[pin]); yield
        for k in range(1, NR + 1):
            last = (k == NR)
            y2, y2n = YYs[s_][k % 2], f"YYs{s_}_{k % 2}"
            if not last:
                P.mm(pYb[0:CH, 0:CH], YTap, Yap, reads=[Yn_, YTn], writes=[pYn]); yield
            P.mm(pYb[0:CH, CH:2 * CH], Yap, YTap, reads=[Yn_, YTn], writes=[pYn]); yield
            if not last:
                P.cp(y2[:], pYb[0:CH, 0:2 * CH], reads=[pYn], writes=[y2n], eng="scalar"); yield
            else:
                P.cp(y2[:, CH:2 * CH], pYb[0:CH, CH:2 * CH], reads=[pYn], writes=[y2n], eng="scalar"); yield
            if not last:
                pi2, pi2n = PIs[s_][k % 2], f"PIs{s_}_{k % 2}"
            else:
                pi2, pi2n = PIF[j], f"PIF{j}"
            P.mm(pPb[0:CH, 0:CH], y2[:, CH:2 * CH], pi[:], reads=[y2n, pin], writes=[pPn]); yield
            P.tt(pi2[:], pPb[0:CH, 0:CH], pi[:], ALU.add, reads=[pPn, pin], writes=[pi2n]); yield
            Yap, Yn_, YTap, YTn, pi, pin = y2[:, 0:CH], y2n, y2[:, CH:2 * CH], y2n, pi2, pi2n

    def genB(ti, c, g):
        i = ti % 2
        j = g % 3
        y, yn = y_sb[i], f"y{i}"
        Sc, Scn = Sst[g % 2], f"Sst{g % 2}"
        Sn, Snn = Sst[1 - g % 2], f"Sst{1 - g % 2}"
        mm2, m2n = M23[j], f"M23_{j}"
        pi, pin = PIF[j], f"PIF{j}"
        cs = slice(c * CH, c * CH + CH)
        el = egc[i][:, c * CH + CH - 1:c * CH + CH]
        P.mm(pV[0:CH, 0:128], pi[:], RV[i][:, c, :], reads=[pin, f"RV{i}"], writes=["pV"]); yield
        P.mm(pV[:, 128:128 + CH], RK[i][:, c, :], pi[:], reads=[pin, f"RK{i}"], writes=["pV"]); yield
        P.cp(u_sb[:], pV[0:CH, 0:128], reads=["pV"], writes=["u"], eng="scalar"); yield
        P.cp(wT_sb[:], pV[:, 128:128 + CH], reads=["pV"], writes=["wT"], eng="scalar"); yield
        P.ts(tmpS[:], Sc[:], el, None, ALU.mult, reads=[Scn, f"egc{i}"], writes=["tmpS"]); yield
        P.mm(pV[0:CH, 256:384], wT_sb[:], Sc[:], reads=["wT", Scn], writes=["pV"]); yield
        P.tt(vn_sb[:], u_sb[:], pV[0:CH, 256:384], ALU.subtract, reads=["u", "pV"], writes=["vn"]); yield
        P.mm(pV[:, 384:512], KS[i][:, c, :], vn_sb[:], reads=[f"KS{i}", "vn"], writes=["pV"]); yield
        P.tt(Sn[:], pV[:, 384:512], tmpS[:], ALU.add, reads=["pV", "tmpS"], writes=[Snn]); yield
        P.mm(pO[:, 0:CH], Sc[:], qg[i][:, cs], start=True, stop=False, reads=[Scn, f"qg{i}"], writes=["pO"]); yield
        P.mm(pO[:, 0:CH], vn_sb[:], mm2[:, CH:2 * CH], start=False, stop=True, reads=["vn", m2n], writes=["pO"]); yield
        P.cp(y[:, cs], pO[:, 0:CH], reads=["pO"], writes=[(yn, c)], eng="scalar"); yield

    def post(ti):
        i = ti % 2
        t0 = ti * TT
        y, yn = y_sb[i], f"y{i}"
        P.act(sq_sb[:], y[:], AF.Square, reads=[yn], writes=["sq"])
        P.mm(pA[:, 0:TT], o128[:], sq_sb[:], reads=["o128", "sq"], writes=["pA"])
        P.act(rn_sb[:], pA[:, 0:TT], AF.Ln, reads=["pA", "cb"], writes=["rn"], bias=cb[:, 0:1], scale=1.0 / 128)
        P.act(rn_sb[:], rn_sb[:], AF.Exp, reads=["rn"], writes=["rn"], scale=-0.5)
        P.stt(y[:], y[:], col(14), rn_sb[:], ALU.mult, ALU.mult, reads=[yn, "prm", "rn"], writes=[yn])
        P.tt(y[:], y[:], sg[i][:], ALU.mult, reads=[yn, f"sg{i}"], writes=[yn])
        P.dma(yT[:, t0:t0 + TT], y[:], reads=[yn], is_output=True)

    seq = [(ti, c) for ti in range(ntile) for c in range(NCH)]
    pipeline3(P, seq, genA, genB, prep, post, NCH, ntile)
    return P.finish()


def build_sgu(T):
    P = Prog()
    NB = T // 128
    TB = 4
    ntile = NB // TB
    uT = P.dram("uT", [128, T])
    vtok = P.dram("vtok", [128, NB, 128])
    lnp = P.dram("lnp", [1, 256])
    wT = P.dram("wT", [2, 128, 128])
    bs = P.dram("bs", [2, 128])
    yT = P.dram("yT", [128, T], F32, kind="ExternalOutput")
    lnp_sb = P.sb("lnp_sb", [128, 256], F32)
    P.dma(lnp_sb[:], lnp.to_broadcast([128, 256]), writes=["lnp"])
    w_sb = P.sb("w_sb", [128, 2, 128], F32)
    for g in range(2):
        P.dma(w_sb[:, g, :], wT[g], writes=["w"])
    P.memset(w_sb[64:128, :, 0:64], 0.0, writes=["w"])
    b_sb = P.sb("b_sb", [128, 128], F32)
    for g in range(2):
        P.dma(b_sb[64 * g:64 * g + 64, :], bs[g:g + 1, :].to_broadcast([64, 128]), writes=["b"])
    cb = P.sb("cb", [128, 1], F32)
    P.memset(cb[:], 1e-5, writes=["cb"])
    v_sb = [P.sb(f"v_sb{i}", [128, TB, 128], F32) for i in range(2)]
    sq_sb = P.sb("sq_sb", [128, TB, 128], F32)
    st = P.sb("st", [128, 4, TB * 2], F32)
    u_sb = [P.sb(f"u_sb{i}", [128, TB * 128], F32) for i in range(2)]
    y_sb = [P.sb(f"y_sb{i}", [128, TB * 128], F32) for i in range(2)]
    po = [P.ps(f"po{g}", [128, 512], F32) for g in range(2)]
    for ti in range(ntile):
        i = ti % 2
        n0 = ti * TB
        v, vn = v_sb[i], f"v{i}"
        u, un = u_sb[i], f"u{i}"
        y, yn = y_sb[i], f"y{i}"
        P.dma(v[:], vtok[:, n0:n0 + TB, :], writes=[vn])
        P.dma(u[:], uT[:, n0 * 128:(n0 + TB) * 128], writes=[un])
        P.act(v[:], v[:], AF.Gelu, reads=[vn], writes=[vn])
        P.act(u[:], u[:], AF.Gelu, reads=[un], writes=[un])
        v3 = v[:].rearrange("p n (g c) -> p (n g) c", c=64)
        s3 = sq_sb[:].rearrange("p n (g c) -> p (n g) c", c=64)
        P.op("vector", "tensor_reduce", reads=[vn], writes=[("st", 0)], out=st[:, 0, :], in_=v3, axis=AX.X, op=ALU.add)
        P.ts(st[:, 0, :], st[:, 0, :], 1.0 / 64, None, ALU.mult, reads=[("st", 0)], writes=[("st", 0)])
        P.tt(v3, v3, st[:, 0, :].unsqueeze(2).to_broadcast([128, TB * 2, 64]), ALU.subtract, reads=[vn, ("st", 0)], writes=[vn])
        P.act(sq_sb[:], v[:], AF.Square, reads=[vn], writes=["sq"])
        P.op("vector", "tensor_reduce", reads=["sq"], writes=[("st", 1)], out=st[:, 1, :], in_=s3, axis=AX.X, op=ALU.add)
        P.act(st[:, 2, :], st[:, 1, :], AF.Ln, reads=[("st", 1), "cb"], writes=[("st", 2)], bias=cb[:, 0:1], scale=1.0 / 64)
        P.act(st[:, 2, :], st[:, 2, :], AF.Exp, reads=[("st", 2)], writes=[("st", 2)], scale=-0.5)
        P.tt(v3, v3, st[:, 2, :].unsqueeze(2).to_broadcast([128, TB * 2, 64]), ALU.mult, reads=[vn, ("st", 2)], writes=[vn])
        P.tt(v[:], v[:], lnp_sb[:, 0:128].unsqueeze(1).to_broadcast([128, TB, 128]), ALU.mult, reads=[vn, "lnp"], writes=[vn])
        P.tt(v[:], v[:], lnp_sb[:, 128:256].unsqueeze(1).to_broadcast([128, TB, 128]), ALU.add, reads=[vn, "lnp"], writes=[vn])
        for g in range(2):
            for n in range(TB):
                P.mm(po[g][:, n * 128:(n + 1) * 128], v[:, n, :], w_sb[:, g, :], reads=[vn, "w"], writes=[f"po{g}"])
            pp = slice(64 * g, 64 * g + 64)
            P.tt(y[pp, :].rearrange("p (n i) -> p n i", i=128), po[g][pp, :].rearrange("p (n i) -> p n i", i=128),
                 b_sb[pp, :].unsqueeze(1).to_broadcast([64, TB, 128]), ALU.add, reads=[f"po{g}", "b"], writes=[(yn, g)])
            P.tt(y[pp, :], y[pp, :], u[pp, :], ALU.mult, reads=[(yn, g), un], writes=[(yn, g)])
        P.dma(yT[:, n0 * 128:(n0 + TB) * 128], y[:], reads=[yn], is_output=True)
    return P.finish()


def prep_hgrn(zTb, hp, lb_logits, norm_g):
    T = zTb.shape[1]
    rows = [zTb[q * 512 + 128 * hp:q * 512 + 128 * hp + 128] for q in range(4)]
    zin = np.ascontiguousarray(np.concatenate(rows, 0))
    iT = rows[2]
    vtok = iT.reshape(2, 64, T // 64, 64).transpose(0, 3, 2, 1)
    prm = np.zeros((128, 4), np.float32)
    prm[:, 0] = lb_logits[0, 128 * hp:128 * hp + 128]
    prm[:, 1] = lb_logits[1, 128 * hp:128 * hp + 128]
    prm[:, 2] = norm_g[128 * hp:128 * hp + 128]
    return {"zin": zin, "vtok": np.ascontiguousarray(vtok), "prm": prm}


def prep_rwkv(zTb, hp, e, prm_in, vfirstT=None):
    T = zTb.shape[1]
    f = slice(128 * hp, 128 * hp + 128)
    zin = np.ascontiguousarray(np.concatenate([zTb[q * 512 + 128 * hp:q * 512 + 128 * hp + 128] for q in range(3)], 0))
    lrin = np.ascontiguousarray(zTb[1536:1696])
    mu = prm_in["rwkv_mu"][e]
    prm = np.zeros((128, 16), np.float32)
    prm[:, 0] = mu[0:512][f]; prm[:, 1] = mu[512:1024][f]; prm[:, 2] = mu[1024:1536][f]
    prm[:, 3] = prm_in["rwkv_w0"][e][f]; prm[:, 4] = prm_in["rwkv_a0"][e][f]
    prm[:, 5] = prm_in["rwkv_k_k"][e][f]; prm[:, 6] = prm_in["rwkv_k_a"][e][f]
    prm[:, 7] = prm_in["rwkv_r_k"][e].reshape(512)[f]
    prm[:, 8] = prm_in["rwkv_ln_g"][e][f]; prm[:, 9] = prm_in["rwkv_ln_b"][e][f]
    prm2 = np.zeros((96, 4), np.float32)
    prm2[0:32, 0] = mu[1536:1568]; prm2[0:32, 1] = mu[1568:1600]; prm2[0:96, 2] = mu[1600:1696]
    wlr = np.zeros((96, 4, 128), np.float32)
    wlr[0:32, 0] = prm_in["rwkv_w_up"][e][:, f]; wlr[0:32, 1] = prm_in["rwkv_a_up"][e][:, f]; wlr[0:96, 2] = prm_in["rwkv_g_up"][e][:, f]
    d = {"zin": zin, "lrin": lrin, "prm": prm, "prm2": prm2, "wlr": wlr}
    if e > 0:
        prm[:, 10] = prm_in["rwkv_v0"][e - 1][f]
        wlr[0:32, 3] = prm_in["rwkv_vres_up"][e - 1][:, f]
        d["vlr"] = np.ascontiguousarray(zTb[2720:2752])
        d["vfirst"] = np.ascontiguousarray(vfirstT)
    return d


def prep_gdn(zTb, hd, o, prm_in, CH=128):
    T = zTb.shape[1]
    base = 2048
    rows = [zTb[base + q * 512 + 128 * hd:base + q * 512 + 128 * hd + 128] for q in range(4)]
    zin = np.ascontiguousarray(np.concatenate(rows, 0))
    brow = zTb[base + 2048 + hd]
    arow = zTb[base + 2052 + hd]
    abrow = np.ascontiguousarray(np.stack([brow, arow], 0))
    abcol = np.ascontiguousarray(np.stack([brow.reshape(T // CH, CH).T, arow.reshape(T // CH, CH).T], 1))
    prm = np.zeros((128, 16), np.float32)
    cw = prm_in["gdn_conv_w"][o]
    for q in range(3):
        prm[:, 4 * q:4 * q + 4] = cw[:, q * 512 + 128 * hd:q * 512 + 128 * hd + 128].T
    prm[:, 12] = prm_in["gdn_a_log"][o][hd]
    prm[:, 13] = prm_in["gdn_dt_bias"][o][hd]
    prm[:, 14] = prm_in["gdn_norm_g"][o]
    return {"zin": zin, "abrow": abrow, "abcol": abcol, "prm": prm}


def prep_sgu(zTb, gp, e, prm_in):
    T = zTb.shape[1]
    base = 1696
    uT = np.ascontiguousarray(zTb[base + 128 * gp:base + 128 * gp + 128])
    vT = zTb[base + 512 + 128 * gp:base + 512 + 128 * gp + 128]
    vtok = np.ascontiguousarray(vT.reshape(128, T // 128, 128).transpose(2, 1, 0))
    f = slice(128 * gp, 128 * gp + 128)
    lnp = np.concatenate([prm_in["sgu_ln_g"][e][f], prm_in["sgu_ln_b"][e][f]])[None, :].astype(np.float32)
    w = prm_in["sgu_w"][e][2 * gp:2 * gp + 2]
    wT = np.ascontiguousarray(w.transpose(0, 2, 1))
    bs = np.ascontiguousarray(prm_in["sgu_b"][e][2 * gp:2 * gp + 2])
    return {"uT": uT, "vtok": vtok, "lnp": np.ascontiguousarray(lnp), "wT": wT, "bs": bs}


_PROGS = {}


def _prog(key, fn):
    if key not in _PROGS:
        _PROGS[key] = fn()
    return _PROGS[key]


def _run(nc, in_maps):
    res = run_bass_kernel_spmd(nc, in_maps, core_ids=list(range(8)))
    return res.results


NTOK = 2048


def _dense_inputs(xTb, yTb, l, p, do_pre, w_in_next, g_pre_next):
    w_o = p["ev_w_out"][l // 2] if l % 2 == 0 else p["od_w_out"][l // 2]
    gvec = np.ascontiguousarray(np.concatenate([colvec(p["norm_mix_post"][l]), colvec(p["norm_ffn_pre"][l]), colvec(p["norm_ffn_post"][l])], 1))
    cw = np.concatenate([p["ffn_conv_w"][l].T, p["ffn_conv_b"][l][:, None]], 1).reshape(NFC, 128, 4).transpose(1, 0, 2)
    cw = np.ascontiguousarray(cw)
    maps = []
    for b in range(2):
        for q in range(4):
            lo = q * NTOK
            if q == 0:
                xs = np.concatenate([np.zeros((D, 2), np.float32), xTb[b][:, 0:NTOK]], 1)
                ys = np.concatenate([np.zeros((D, 2), np.float32), yTb[b][:, 0:NTOK]], 1)
            else:
                xs = xTb[b][:, lo - 2:lo + NTOK]
                ys = yTb[b][:, lo - 2:lo + NTOK]
            m = {"xT": np.ascontiguousarray(xs), "yT": np.ascontiguousarray(ys), "w_o": w_o, "w_f1": p["ffn_w_in"][l], "w_f2": p["ffn_w_out"][l],
                 "gvec": gvec, "cw": cw, "hmask": np.full((128, 1), 0.0 if q == 0 else 1.0, np.float32)}
            if do_pre:
                m["gpre"] = g_pre_next
                m["w_in"] = w_in_next
            maps.append(m)
    return maps


def _w_in(l, p):
    if l % 2 == 1:
        return p["od_w_in"][l // 2]
    e = l // 2
    if e == 0:
        return p["ev_w_in"][0]
    return np.ascontiguousarray(np.concatenate([p["ev_w_in"][e], p["rwkv_vres_down"][e - 1]], 1))


def kernel(**inputs):
    p = {k: np.ascontiguousarray(np.asarray(v, dtype=np.float32)) for k, v in inputs.items()}
    x = p["x"]
    B, T, _ = x.shape
    xTb = [np.ascontiguousarray(x[b].T) for b in range(B)]
    w_in0 = _w_in(0, p)
    nc = _prog(("dense", False, True, w_in0.shape[1]), lambda: build_dense(NTOK, False, True, w_in0.shape[1]))
    maps = [{"xT": np.ascontiguousarray(xTb[b][:, q * NTOK:(q + 1) * NTOK]), "gpre": colvec(p["norm_mix_pre"][0]), "w_in": w_in0}
            for b in range(B) for q in range(4)]
    res = _run(nc, maps)
    zTb = [np.concatenate([res[4 * b + q]["zT"] for q in range(4)], 1) for b in range(B)]
    vfirst = None
    for l in range(4):
        yTb = [np.empty((D, T), np.float32) for _ in range(B)]
        if l % 2 == 0:
            e = l // 2
            nc = _prog(("rwkv", e > 0), lambda: build_rwkv(T, e > 0))
            maps = [prep_rwkv(zTb[b], hp, e, p, None if vfirst is None else vfirst[b][128 * hp:128 * hp + 128]) for b in range(B) for hp in range(4)]
            res = _run(nc, maps)
            for b in range(B):
                for hp in range(4):
                    yTb[b][128 * hp:128 * hp + 128] = res[4 * b + hp]["yT"]
            if e == 0:
                vfirst = [np.concatenate([res[4 * b + hp]["vout"] for hp in range(4)], 0) for b in range(B)]
            nc = _prog(("sgu",), lambda: build_sgu(T))
            maps = [prep_sgu(zTb[b], gp, e, p) for b in range(B) for gp in range(4)]
            res = _run(nc, maps)
            for b in range(B):
                for gp in range(4):
                    yTb[b][512 + 128 * gp:512 + 128 * gp + 128] = res[4 * b + gp]["yT"]
        else:
            o = l // 2
            nc = _prog(("hgrn", o), lambda: build_hgrn(T, o))
            maps = [prep_hgrn(zTb[b], hp, p["hgrn_lb_logits"], p["hgrn_norm_g"][o]) for b in range(B) for hp in range(4)]
            res = _run(nc, maps)
            for b in range(B):
                for hp in range(4):
                    yTb[b][128 * hp:128 * hp + 128] = res[4 * b + hp]["yT"]
            nc = _prog(("gdn",), lambda: build_gdn(T))
            maps = [prep_gdn(zTb[b], hd, o, p) for b in range(B) for hd in range(4)]
            res = _run(nc, maps)
            for b in range(B):
                for hd in range(4):
                    yTb[b][512 + 128 * hd:512 + 128 * hd + 128] = res[4 * b + hd]["yT"]
        do_pre = l < 3
        w_next = _w_in(l + 1, p) if do_pre else None
        C = w_next.shape[1] if do_pre else 0
        nc = _prog(("dense", True, do_pre, C), lambda: build_dense(NTOK, True, do_pre, C))
        maps = _dense_inputs(xTb, yTb, l, p, do_pre, w_next, colvec(p["norm_mix_pre"][l + 1]) if do_pre else None)
        res = _run(nc, maps)
        xTb = [np.concatenate([res[4 * b + q]["xoT"] for q in range(4)], 1) for b in range(B)]
        if do_pre:
            zTb = [np.concatenate([res[4 * b + q]["zT"] for q in range(4)], 1) for b in range(B)]
    return np.ascontiguousarray(np.stack([xTb[b].T for b in range(B)], 0)).astype(np.float32)
```
